# Optimizing a Trainium2 kernel written in Bass

```python
import jax
import jax.numpy as jnp
from jax import lax
import numpy as np

D_MODEL = 1024
BATCH = 8
SEQ = 4096
DEPTH = 4

N_MIXERS = 3
HEAD_DIM = 64
BLOCK_Q = 128
GRID_W = 64
RMS_EPS = 1e-6
D_FF = 4 * D_MODEL
A_HEADS = D_MODEL // HEAD_DIM
A_KV_HEADS = A_HEADS // 4
ROPE_THETA = 10000.0
B_GROUPS = ((128, 1), (512, 4), (2048, 16))
B_HEADS_PER_GROUP = 6
B_KV_PER_GROUP = 2
C_HEADS = D_MODEL // HEAD_DIM
C_KV_HEADS = C_HEADS // 4
C_WINDOW = 128

kernel_name = "hybrid_interleaved_bidir_encoder"


def rmsnorm(x, gain):
    xf = x.astype(jnp.float32)
    y = xf * lax.rsqrt(jnp.mean(xf * xf, axis=-1, keepdims=True) + RMS_EPS)
    return (y * gain.astype(jnp.float32)).astype(x.dtype)


def alibi_slopes(n_heads):
    return 2.0 ** (-8.0 * jnp.arange(1, n_heads + 1, dtype=jnp.float32) / n_heads)


def stack_blocks(y, batch, seq):
    y = jnp.moveaxis(y, 0, 1)
    return y.reshape((batch, seq) + y.shape[3:])


def axial_rope_angles(seq):
    rows = seq // GRID_W
    row = jnp.repeat(jnp.arange(rows, dtype=jnp.float32), GRID_W)
    col = jnp.tile(jnp.arange(GRID_W, dtype=jnp.float32), rows)
    axis_dim = HEAD_DIM // 2
    inv_freq = ROPE_THETA ** (-jnp.arange(0, axis_dim, 2, dtype=jnp.float32) / axis_dim)
    return row[:, None] * inv_freq, col[:, None] * inv_freq


def rotate(x, ang):
    shape = (ang.shape[0],) + (1,) * (x.ndim - 3) + (ang.shape[1],)
    cos = jnp.cos(ang).reshape(shape).astype(x.dtype)
    sin = jnp.sin(ang).reshape(shape).astype(x.dtype)
    x1, x2 = jnp.split(x, 2, axis=-1)
    return jnp.concatenate([x1 * cos - x2 * sin, x2 * cos + x1 * sin], axis=-1)


def axial_rope(x, ang_row, ang_col):
    half = HEAD_DIM // 2
    return jnp.concatenate([rotate(x[..., :half], ang_row), rotate(x[..., half:], ang_col)], axis=-1)


def mixer_a(h, w_qkv, q_gain, k_gain, w_o):
    b, s, _ = h.shape
    rep = A_HEADS // A_KV_HEADS
    qkv = h @ w_qkv
    q, k, v = jnp.split(qkv, [A_HEADS * HEAD_DIM, (A_HEADS + A_KV_HEADS) * HEAD_DIM], axis=-1)
    q = rmsnorm(q.reshape(b, s, A_KV_HEADS, rep, HEAD_DIM), q_gain)
    k = rmsnorm(k.reshape(b, s, A_KV_HEADS, HEAD_DIM), k_gain)
    v = v.reshape(b, s, A_KV_HEADS, HEAD_DIM)
    ang_row, ang_col = axial_rope_angles(s)
    q = axial_rope(q, ang_row, ang_col) * (HEAD_DIM ** -0.5)
    k = axial_rope(k, ang_row, ang_col)

    def block(i):
        qb = lax.dynamic_slice_in_dim(q, i * BLOCK_Q, BLOCK_Q, axis=1)
        sc = jnp.einsum('bqhgd,bkhd->bhgqk', qb, k).astype(jnp.float32)
        p = jax.nn.softmax(sc, axis=-1).astype(v.dtype)
        o = jnp.einsum('bhgqk,bkhd->bqhgd', p, v)
        return o.reshape(b, BLOCK_Q, A_HEADS * HEAD_DIM)

    o = stack_blocks(lax.map(block, jnp.arange(s // BLOCK_Q)), b, s)
    return o @ w_o


def mixer_b(h, w_qkv, w_o):
    b, s, _ = h.shape
    n_g = len(B_GROUPS)
    rep = B_HEADS_PER_GROUP // B_KV_PER_GROUP
    nq = n_g * B_HEADS_PER_GROUP * HEAD_DIM
    nk = n_g * B_KV_PER_GROUP * HEAD_DIM
    qkv = h @ w_qkv
    q, k, v = jnp.split(qkv, [nq, nq + nk], axis=-1)
    q = q.reshape(b, s, n_g, B_KV_PER_GROUP, rep, HEAD_DIM) * (HEAD_DIM ** -0.5)
    k = k.reshape(b, s, n_g, B_KV_PER_GROUP, HEAD_DIM)
    v = v.reshape(b, s, n_g, B_KV_PER_GROUP, HEAD_DIM)
    slopes = alibi_slopes(n_g * B_HEADS_PER_GROUP).reshape(n_g, B_KV_PER_GROUP, rep)
    outs, lses = [], []
    for g, (window, dil) in enumerate(B_GROUPS):
        n_side = (window // 2) // dil
        offs = jnp.arange(-n_side, n_side + 1) * dil
        bias = -slopes[g][:, :, None, None] * jnp.abs(offs).astype(jnp.float32)
        qg, kg, vg = q[:, :, g], k[:, :, g], v[:, :, g]

        def block(i, qg=qg, kg=kg, vg=vg, offs=offs, bias=bias):
            start = i * BLOCK_Q
            idx = start + jnp.arange(BLOCK_Q)[:, None] + offs[None, :]
            valid = (idx >= 0) & (idx < s)
            idx = jnp.clip(idx, 0, s - 1)
            kb = jnp.take(kg, idx, axis=1)
            vb = jnp.take(vg, idx, axis=1)
            qb = lax.dynamic_slice_in_dim(qg, start, BLOCK_Q, axis=1)
            sc = jnp.einsum('bqhgd,bqnhd->bhgqn', qb, kb).astype(jnp.float32) + bias
            sc = jnp.where(valid, sc, -jnp.inf)
            m = jnp.max(sc, axis=-1, keepdims=True)
            p = jnp.exp(sc - m)
            den = jnp.sum(p, axis=-1, keepdims=True)
            o = jnp.einsum('bhgqn,bqnhd->bqhgd', (p / den).astype(vb.dtype), vb)
            lse = (m + jnp.log(den))[..., 0].transpose(0, 3, 1, 2)
            return o, lse

        o_g, lse_g = lax.map(block, jnp.arange(s // BLOCK_Q))
        outs.append(stack_blocks(o_g, b, s))
        lses.append(stack_blocks(lse_g, b, s))
    alpha = jax.nn.softmax(jnp.stack(lses, axis=2), axis=2)
    o = jnp.stack(outs, axis=2) * alpha[..., None].astype(h.dtype)
    return o.reshape(b, s, nq) @ w_o


def mixer_c(h, w_qkv, sinks, w_o):
    b, s, _ = h.shape
    rep = C_HEADS // C_KV_HEADS
    span = BLOCK_Q + 2 * C_WINDOW
    qkv = h @ w_qkv
    q, k, v = jnp.split(qkv, [C_HEADS * HEAD_DIM, (C_HEADS + C_KV_HEADS) * HEAD_DIM], axis=-1)
    q = q.reshape(b, s, C_KV_HEADS, rep, HEAD_DIM) * (HEAD_DIM ** -0.5)
    pad = ((0, 0), (C_WINDOW, C_WINDOW), (0, 0), (0, 0))
    kp = jnp.pad(k.reshape(b, s, C_KV_HEADS, HEAD_DIM), pad)
    vp = jnp.pad(v.reshape(b, s, C_KV_HEADS, HEAD_DIM), pad)
    slopes = alibi_slopes(C_HEADS).reshape(C_KV_HEADS, rep)[:, :, None, None]
    sink = sinks.astype(jnp.float32).reshape(1, C_KV_HEADS, rep, 1, 1)

    def block(i):
        start = i * BLOCK_Q
        qb = lax.dynamic_slice_in_dim(q, start, BLOCK_Q, axis=1)
        kb = lax.dynamic_slice_in_dim(kp, start, span, axis=1)
        vb = lax.dynamic_slice_in_dim(vp, start, span, axis=1)
        tq = start + jnp.arange(BLOCK_Q)
        tk = start - C_WINDOW + jnp.arange(span)
        dist = jnp.abs(tk[None, :] - tq[:, None])
        valid = (dist <= C_WINDOW) & (tk[None, :] >= 0) & (tk[None, :] < s)
        sc = jnp.einsum('bqhgd,bkhd->bhgqk', qb, kb).astype(jnp.float32)
        sc = jnp.where(valid, sc - slopes * dist.astype(jnp.float32), -jnp.inf)
        logits = jnp.concatenate([sc, jnp.broadcast_to(sink, sc.shape[:-1] + (1,))], axis=-1)
        p = jax.nn.softmax(logits, axis=-1)[..., :-1].astype(vb.dtype)
        o = jnp.einsum('bhgqk,bkhd->bqhgd', p, vb)
        return o.reshape(b, BLOCK_Q, C_HEADS * HEAD_DIM)

    o = stack_blocks(lax.map(block, jnp.arange(s // BLOCK_Q)), b, s)
    return o @ w_o


def squared_relu_mlp(h, w1, w2):
    u = jax.nn.relu(h @ w1)
    return (u * u) @ w2


def setup_inputs(seed: int = 0) -> dict:
    key = jax.random.key(seed)
    ks = iter(jax.random.split(key, 32))
    kinds = [i % N_MIXERS for i in range(DEPTH)]
    n_a, n_b, n_c = kinds.count(0), kinds.count(1), kinds.count(2)

    def dense(k, shape):
        return jax.random.normal(k, shape, jnp.float32) * (shape[-2] ** -0.5)

    def gain(k, shape):
        return 1.0 + 0.05 * jax.random.normal(k, shape, jnp.float32)

    a_cols = (A_HEADS + 2 * A_KV_HEADS) * HEAD_DIM
    b_q = len(B_GROUPS) * B_HEADS_PER_GROUP * HEAD_DIM
    b_cols = b_q + 2 * len(B_GROUPS) * B_KV_PER_GROUP * HEAD_DIM
    c_cols = (C_HEADS + 2 * C_KV_HEADS) * HEAD_DIM
    return {
        "x": jax.random.normal(next(ks), (BATCH, SEQ, D_MODEL), jnp.float32),
        "attn_norm": gain(next(ks), (DEPTH, D_MODEL)),
        "mlp_norm": gain(next(ks), (DEPTH, D_MODEL)),
        "a_w_qkv": dense(next(ks), (n_a, D_MODEL, a_cols)),
        "a_q_gain": gain(next(ks), (n_a, HEAD_DIM)),
        "a_k_gain": gain(next(ks), (n_a, HEAD_DIM)),
        "a_w_o": dense(next(ks), (n_a, A_HEADS * HEAD_DIM, D_MODEL)),
        "b_w_qkv": dense(next(ks), (n_b, D_MODEL, b_cols)),
        "b_w_o": dense(next(ks), (n_b, b_q, D_MODEL)),
        "c_w_qkv": dense(next(ks), (n_c, D_MODEL, c_cols)),
        "c_sinks": 0.5 * jax.random.normal(next(ks), (n_c, C_HEADS), jnp.float32),
        "c_w_o": dense(next(ks), (n_c, C_HEADS * HEAD_DIM, D_MODEL)),
        "mlp_w1": dense(next(ks), (DEPTH, D_MODEL, D_FF)),
        "mlp_w2": dense(next(ks), (DEPTH, D_FF, D_MODEL)),
        "final_norm": gain(next(ks), (D_MODEL,)),
    }


def reference(x, attn_norm, mlp_norm, a_w_qkv, a_q_gain, a_k_gain, a_w_o, b_w_qkv, b_w_o,
              c_w_qkv, c_sinks, c_w_o, mlp_w1, mlp_w2, final_norm):
    h = x
    used = [0, 0, 0]
    for layer in range(DEPTH):
        kind = layer % N_MIXERS
        j = used[kind]
        used[kind] += 1
        hn = rmsnorm(h, attn_norm[layer])
        if kind == 0:
            mix = mixer_a(hn, a_w_qkv[j], a_q_gain[j], a_k_gain[j], a_w_o[j])
        elif kind == 1:
            mix = mixer_b(hn, b_w_qkv[j], b_w_o[j])
        else:
            mix = mixer_c(hn, c_w_qkv[j], c_sinks[j], c_w_o[j])
        h = h + mix
        h = h + squared_relu_mlp(rmsnorm(h, mlp_norm[layer]), mlp_w1[layer], mlp_w2[layer])
    return rmsnorm(h, final_norm)
```

```python
import numpy as np
from contextlib import ExitStack
import concourse.bass as bass
import concourse.mybir as mybir
from concourse.bass_utils import run_bass_kernel_spmd

F32 = mybir.dt.float32
BF16 = mybir.dt.bfloat16
AF = mybir.ActivationFunctionType
ALU = mybir.AluOpType

S = 4096
D = 1024
DFF = 4096
HD = 64
NCORES = 8
DEPTH = 4
KINDS = [0, 1, 2, 0]
EPS = 1e-6
GRID_W = 64
ROPE_THETA = 10000.0
B_GROUPS = ((128, 1), (512, 4), (2048, 16))
NEG = -30000.0

CFG = {0: (16, 4), 1: (18, 6), 2: (16, 4)}

ENGS = ["sync", "scalar", "gpsimd", "vector", "tensor"]
STRICT = True


class Sem:
    def __init__(self, h):
        self.h = h
        self.n = 0


class Ctx:
    def __init__(self, nc, es):
        self.nc = nc
        self.es = es
        self.lists = {e: [] for e in ENGS}
        self.waited = {e: {} for e in ENGS}
        self.esem = {}
        self.allsems = []
        for e in ["scalar", "gpsimd", "vector", "tensor"]:
            self.esem[e] = self.new_sem("e_" + e)
        self.nsem = 0

    def new_sem(self, name):
        s = Sem(self.es.enter_context(self.nc.semaphore(name)))
        self.allsems.append(s)
        return s

    def op(self, eng, fn, waits=(), sig=False, dsem=None):
        ws = []
        wd = self.waited[eng]
        flat = []
        for ev in waits:
            if ev is None:
                continue
            if isinstance(ev[0], Sem):
                flat.append(ev)
            else:
                flat.extend(x for x in ev if x is not None)
        for ev in flat:
            sem, val = ev
            if eng in self.esem and sem is self.esem[eng] and not STRICT:
                continue
            if wd.get(id(sem), 0) >= val:
                continue
            wd[id(sem)] = val
            ws.append((sem, val))
        ev = None
        inc = 0
        if dsem is not None:
            dsem.n += 16
            inc = 16
            ev = (dsem, dsem.n)
        elif sig:
            s = self.esem[eng]
            s.n += 1
            inc = 1
            ev = (s, s.n)
        self.lists[eng].append((ws, fn, ev, inc))
        return ev

    def flush(self):
        nc = self.nc
        with nc.Block() as block:
            for eng in ENGS:
                lst = self.lists[eng]
                if not lst:
                    continue

                def body(e, lst=lst):
                    for ws, fn, ev, inc in lst:
                        for sem, val in ws:
                            e.wait_ge(sem.h, val)
                        ins = fn(e)
                        if ev is not None:
                            ins.then_inc(ev[0].h, inc)

                getattr(block, eng)(body)
        self.lists = {e: [] for e in ENGS}


def mm(out, lhsT, rhs, start, stop):
    return lambda e: e.matmul(out, lhsT=lhsT, rhs=rhs, start=start, stop=stop)


def act(out, in_, func, scale=1.0, bias=None):
    if bias is None:
        return lambda e: e.activation(out=out, in_=in_, func=func, scale=scale)
    return lambda e: e.activation(out=out, in_=in_, func=func, scale=scale, bias=bias)


def tt(out, in0, in1, op):
    return lambda e: e.tensor_tensor(out=out, in0=in0, in1=in1, op=op)


def stt(out, in0, scalar, in1, op0, op1):
    return lambda e: e.scalar_tensor_tensor(out=out, in0=in0, scalar=scalar, in1=in1, op0=op0, op1=op1)


def ts(out, in0, s1, op0, s2=None, op1=None):
    if op1 is None:
        return lambda e: e.tensor_scalar(out=out, in0=in0, scalar1=s1, scalar2=None, op0=op0)
    return lambda e: e.tensor_scalar(out=out, in0=in0, scalar1=s1, scalar2=s2, op0=op0, op1=op1)


def recip(out, in_):
    return lambda e: e.reciprocal(out=out, in_=in_)


def cpy(out, in_):
    return lambda e: e.tensor_copy(out=out, in_=in_)


def dma(out, in_):
    return lambda e: e.dma_start(out=out, in_=in_)


def mset(ap, v):
    return lambda e: e.memset(ap, v)


def gcol_attn(l):
    return l * 8


def gcol_mlp(l):
    return 32 + l * 8


GCOL_FINAL = 64
GCOL_QG = 72
GCOL_KG = 74
GCOL_EPS = 76
GCOL_EPS64 = 77
GCOL_SCL = 78
GCOL_BIA = 79
NG = 80


class RmsNorm:
    def __init__(self, cx, sb, ps, name, T, ones, G):
        self.cx, self.T, self.ones, self.G = cx, T, ones, G
        self.sq = [sb(name + "_sq%d" % i, [128, T], F32) for i in range(2)]
        self.sd = sb(name + "_sd", [128, T], F32)
        self.rstd = sb(name + "_rstd", [128, T], F32)
        self.ss = ps(name + "_ss")
        self.sq_free = [None, None]
        self.ss_free = None
        self.sd_free = None
        self.users = None

    def emit(self, ht, h_ready):
        cx, T = self.cx, self.T
        ev_mm = None
        for c in range(8):
            b = c % 2
            e_sq = cx.op("scalar", act(self.sq[b][:], ht[:, c, :], AF.Square),
                         waits=[h_ready, self.sq_free[b]], sig=True)
            ev_mm = cx.op("tensor", mm(self.ss[:, 0:T], self.ones, self.sq[b][:], c == 0, c == 7),
                          waits=[e_sq, self.ss_free if c == 0 else None], sig=True)
            self.sq_free[b] = ev_mm
        e_sd = cx.op("scalar", act(self.sd[:], self.ss[:, 0:T], AF.Sqrt, scale=1.0 / D,
                                   bias=self.G[:, GCOL_EPS:GCOL_EPS + 1]),
                     waits=[ev_mm, self.sd_free], sig=True)
        self.ss_free = e_sd
        e_r = cx.op("vector", recip(self.rstd[:], self.sd[:]), waits=[e_sd, self.users], sig=True)
        self.sd_free = e_r
        return e_r


def load_weights_cast(cx, Wsb, Wdram, wsem, pieces):
    ev = None
    for (c0, c1, n0, n1) in pieces:
        ev = cx.op("gpsimd", dma(Wsb[:, c0:c1, n0:n1], Wdram[:, c0:c1, n0:n1]), dsem=wsem)
    return ev


def stage_mlp(cx, l, h_in, h_out, w1d, w2d, G, ones, last):
    nc = cx.nc
    T = 256
    NT = S // T
    hin_v = h_in.rearrange("(c p) t -> p c t", p=128)
    hout_v = h_out.rearrange("(c p) t -> p c t", p=128)
    with ExitStack() as es:
        sb = lambda name, shape, dt: es.enter_context(nc.sbuf_tensor("L%d_" % l + name, shape, dt))
        ps = lambda name: es.enter_context(nc.psum_tensor("L%d_" % l + name, [128, 512], F32))
        W1 = sb("w1", [128, 8, DFF], BF16)
        W2 = sb("w2", [128, 32, D], BF16)
        ht = [sb("m_h%d" % i, [128, 8, T], F32) for i in range(2)]
        hn = sb("m_hn", [128, 8, T], BF16)
        u = sb("m_u", [128, 32, T], BF16)
        rl = [sb("m_rl%d" % i, [128, T], F32) for i in range(3)]
        nrm = RmsNorm(cx, sb, ps, "m_n", T, ones, G)
        if last:
            yt = sb("m_y", [128, 8, T], F32)
            nrm2 = RmsNorm(cx, sb, ps, "m_n2", T, ones, G)
        ups = [ps("m_ups%d" % i) for i in range(3 if last else 4)]
        ops_ = [ps("m_ops%d" % i) for i in range(2)]
        lds = [cx.new_sem("m%d_ld%d" % (l, i)) for i in range(2)]
        sts = [cx.new_sem("m%d_st%d" % (l, i)) for i in range(2)]
        w1s = cx.new_sem("m%d_w1" % l)
        w2s = cx.new_sem("m%d_w2" % l)
        NU = len(ups)

        e_w1 = load_weights_cast(cx, W1, w1d, w1s, [(0, 8, 0, 2048), (0, 8, 2048, 4096)])
        e_w2 = load_weights_cast(cx, W2, w2d, w2s, [(8 * i, 8 * i + 8, 0, D) for i in range(4)])

        h_free = [None, None]
        hn_free = None
        u_free = None
        yt_free = None
        rl_free = [None, None, None]
        ups_free = [None] * NU
        ops_free = [None] * 2
        gm = gcol_mlp(l)

        def load(i):
            b = i % 2
            return cx.op("sync", dma(ht[b][:], hin_v[:, :, i * T:(i + 1) * T]), waits=[h_free[b]], dsem=lds[b])

        def norm(i, h_ready):
            b = i % 2
            e_r = nrm.emit(ht[b], h_ready)
            e_hn = None
            for c in range(8):
                e_hn = cx.op("vector", stt(hn[:, c, :], ht[b][:, c, :], G[:, gm + c:gm + c + 1], nrm.rstd[:],
                                           ALU.mult, ALU.mult), waits=[e_r, hn_free], sig=True)
            nrm.users = e_hn
            return e_hn

        ld_ev = {0: load(0)}
        e_hn = norm(0, ld_ev[0])
        for i in range(NT):
            b = i % 2
            t0 = i * T
            if i + 1 < NT:
                ld_ev[i + 1] = load(i + 1)
            e_u = None
            e_mm = None
            for m in range(32):
                pb = m % NU
                for c in range(8):
                    e_mm = cx.op("tensor", mm(ups[pb][:, 0:T], W1[:, c, m * 128:(m + 1) * 128], hn[:, c, :],
                                              c == 0, c == 7),
                                 waits=[e_hn, e_w1, ups_free[pb] if c == 0 else None], sig=(c == 7))
                rb = m % 3
                e_rl = cx.op("scalar", act(rl[rb][:], ups[pb][:, 0:T], AF.Relu), waits=[e_mm, rl_free[rb]], sig=True)
                ups_free[pb] = e_rl
                e_u = cx.op("gpsimd", tt(u[:, m, :], rl[rb][:], rl[rb][:], ALU.mult), waits=[e_rl, u_free], sig=True)
                rl_free[rb] = e_u
            hn_free = e_mm
            if i + 1 < NT:
                e_hn_next = norm(i + 1, ld_ev[i + 1])
            e_add = None
            for n in range(8):
                pb = n % 2
                for m in range(32):
                    e_mm = cx.op("tensor", mm(ops_[pb][:, 0:T], W2[:, m, n * 128:(n + 1) * 128], u[:, m, :],
                                              m == 0, m == 31),
                                 waits=[e_u, e_w2, ops_free[pb] if m == 0 else None], sig=(m == 31))
                e_add = cx.op("vector", tt(ht[b][:, n, :], ops_[pb][:, 0:T], ht[b][:, n, :], ALU.add),
                              waits=[e_mm], sig=True)
                ops_free[pb] = e_add
            u_free = e_mm
            if not last:
                h_free[b] = cx.op("sync", dma(hout_v[:, :, t0:t0 + T], ht[b][:]), waits=[e_add], dsem=sts[b])
            else:
                e_r2 = nrm2.emit(ht[b], e_add)
                e_y = None
                for c in range(8):
                    e_y = cx.op("vector", stt(yt[:, c, :], ht[b][:, c, :], G[:, GCOL_FINAL + c:GCOL_FINAL + c + 1],
                                              nrm2.rstd[:], ALU.mult, ALU.mult), waits=[e_r2, yt_free], sig=True)
                nrm2.users = e_y
                h_free[b] = e_y
                yt_free = cx.op("sync", dma(hout_v[:, :, t0:t0 + T], yt[:]), waits=[e_y], dsem=sts[0])
            if i + 1 < NT:
                e_hn = e_hn_next
        for s_ in sts:
            if s_.n:
                cx.op("sync", lambda e, s_=s_: e.wait_ge(s_.h, s_.n))
        cx.flush()


def stage_qkv(cx, l, kind, j, h_in, wd, QT, KT, Vd, G, ones, BDCd, SelCd, Rm, cosd, sind):
    nc = cx.nc
    T = 512
    NT = S // T
    nh, nkv = CFG[kind]
    nqc, nkc, NV = nh // 2, nkv, nkv * 64
    noc = nqc + nkc
    NW = nh * 64 + nkv * 128 + NV
    voff = nh * 64 + nkv * 128
    hin_v = h_in.rearrange("(c p) t -> p c t", p=128)
    QTv = QT.rearrange("(c p) t -> p c t", p=128)
    KTv = KT.rearrange("(c p) t -> p c t", p=128)
    Vdv = Vd.rearrange("(tb p) f -> p tb f", p=128)
    isA = kind == 0
    with ExitStack() as es:
        sb = lambda name, shape, dt: es.enter_context(nc.sbuf_tensor("Q%d_" % l + name, shape, dt))
        ps = lambda name: es.enter_context(nc.psum_tensor("Q%d_" % l + name, [128, 512], F32))
        W = sb("w", [128, 8, NW], BF16)
        ht = [sb("h%d" % i, [128, 8, T], F32) for i in range(2)]
        hn = [sb("hn%d" % i, [128, 8, T], BF16) for i in range(2)]
        qo = [sb("qo%d" % i, [128, nqc, T], BF16) for i in range(2)]
        ko = [sb("ko%d" % i, [128, nkc, T], BF16) for i in range(2)]
        vo = [sb("vo%d" % i, [128, 4, nkv, 65], BF16) for i in range(2)]
        nrm = RmsNorm(cx, sb, ps, "n", T, ones, G)
        pj = [ps("pj%d" % i) for i in range(3 if not isA else 2)]
        NPJ = len(pj)
        lds = [cx.new_sem("q%d_ld%d" % (l, i)) for i in range(2)]
        sts = [cx.new_sem("q%d_st%d" % (l, i)) for i in range(2)]
        ws = cx.new_sem("q%d_w" % l)
        half = NW // 2
        e_w = load_weights_cast(cx, W, wd, ws, [(0, 4, 0, NW), (4, 8, 0, NW)] if NW <= 2048 else
                                [(0, 8, 0, half), (0, 8, half, NW)])
        e_ms = None
        for b in range(2):
            e_ms = cx.op("gpsimd", mset(vo[b][:], 1.0), sig=True)
        if isA:
            cst = [sb("cs%d" % i, [128, 2, T], F32) for i in range(2)]
            tmpn = ["qg", "sq", "t1", "t2"]
            tmp = {n: [sb(n + "%d" % i, [128, T], F32) for i in range(2)] for n in tmpn}
            tfree = {n: [None, None] for n in tmpn}
            T3 = sb("T3", [128, noc, T], F32)
            t3_free = [None] * noc
            BDC = sb("BDC", [128, noc, 32], F32)
            SelC = sb("SelC", [32, noc, 128], F32)
            sdc = sb("sdc", [32, T], F32)
            rsc = sb("rsc", [32, T], F32)
            aux = [ps("aux%d" % i) for i in range(2)]
            aux_free = [None, None]
            ssc = ps("ssc")
            ssc_free = None
            sdc_free = None
            rsc_free = None
            cx.op("sync", dma(BDC[:], BDCd), dsem=ws)
            e_w = cx.op("sync", dma(SelC[:], SelCd), dsem=ws)

        h_free = [None, None]
        out_free = [None, None]
        hn_free = [None, None]
        pj_free = [None] * NPJ
        gm = gcol_attn(l)
        cnt = [0]

        def load(i):
            b = i % 2
            ev = cx.op("sync", dma(ht[b][:], hin_v[:, :, i * T:(i + 1) * T]), waits=[h_free[b]], dsem=lds[b])
            if isA:
                cx.op("sync", dma(cst[b][:, 0, :], cosd[:, i * T:(i + 1) * T]), dsem=lds[b])
                ev = cx.op("sync", dma(cst[b][:, 1, :], sind[:, i * T:(i + 1) * T]), dsem=lds[b])
            return ev

        def norm(i, h_ready):
            b = i % 2
            e_r = nrm.emit(ht[b], h_ready)
            e_hn = None
            for c in range(8):
                e_hn = cx.op("vector", stt(hn[b][:, c, :], ht[b][:, c, :], G[:, gm + c:gm + c + 1], nrm.rstd[:],
                                           ALU.mult, ALU.mult), waits=[e_r, hn_free[b]], sig=True)
            nrm.users = e_hn
            if not isA:
                h_free[b] = e_hn
            return e_hn

        ld_ev = {0: load(0)}
        e_hn = norm(0, ld_ev[0])
        for i in range(NT):
            b = i % 2
            t0 = i * T
            if i + 1 < NT:
                ld_ev[i + 1] = load(i + 1)
            evs_out = []
            deferred = []
            e_mm = None
            e_t2 = None
            for oc in range(noc):
                isq = oc < nqc
                pb = cnt[0] % NPJ
                cnt[0] += 1
                for c in range(8):
                    e_mm = cx.op("tensor", mm(pj[pb][:], W[:, c, oc * 128:(oc + 1) * 128], hn[b][:, c, :], c == 0, c == 7),
                                 waits=[e_hn, e_w, pj_free[pb] if c == 0 else None], sig=(c == 7))
                dst = qo[b][:, oc, :] if isq else ko[b][:, oc - nqc, :]
                if not isA:
                    e_o = cx.op("scalar", act(dst, pj[pb][:], AF.Copy, scale=0.125 if isq else 1.0),
                                waits=[e_mm, out_free[b]], sig=True)
                    pj_free[pb] = e_o
                    evs_out.append(e_o)
                    continue
                k = oc % 2
                gcol = (GCOL_QG if isq else GCOL_KG) + j
                e_qg = cx.op("scalar", act(tmp["qg"][k][:], pj[pb][:], AF.Copy, scale=G[:, gcol:gcol + 1]),
                             waits=[e_mm, tfree["qg"][k]], sig=True)
                e_sq = cx.op("scalar", act(tmp["sq"][k][:], pj[pb][:], AF.Square),
                             waits=[e_mm, tfree["sq"][k]], sig=True)
                pj_free[pb] = e_sq
                e_t1 = cx.op("gpsimd", tt(tmp["t1"][k][:], tmp["qg"][k][:], cst[b][:, 0, :], ALU.mult),
                             waits=[e_qg, ld_ev[i], tfree["t1"][k]], sig=True)

                def fp32_part(oc=oc, k=k, e_qg=e_qg, e_sq=e_sq, e_t1=e_t1):
                    nonlocal ssc_free
                    e_ss = cx.op("tensor", mm(ssc[0:32, :], BDC[:, oc, :], tmp["sq"][k][:], oc == 0, oc == noc - 1),
                                 waits=[e_sq, e_w, ssc_free if oc == 0 else None], sig=True)
                    tfree["sq"][k] = e_ss
                    e_rot = cx.op("tensor", mm(aux[k][:], Rm, tmp["qg"][k][:], True, True),
                                  waits=[e_qg, aux_free[k]], sig=True)
                    tfree["qg"][k] = (e_rot, e_t1)
                    e_t2_ = cx.op("vector", tt(tmp["t2"][k][:], aux[k][:], cst[b][:, 1, :], ALU.mult),
                                  waits=[e_rot, ld_ev[i], tfree["t2"][k]], sig=True)
                    aux_free[k] = e_t2_
                    e_t3 = cx.op("gpsimd", tt(T3[:, oc, :], tmp["t1"][k][:], tmp["t2"][k][:], ALU.add),
                                 waits=[e_t1, e_t2_, t3_free[oc]], sig=True)
                    tfree["t1"][k] = e_t3
                    tfree["t2"][k] = e_t3
                    return e_ss, e_t3

                deferred.append(fp32_part)
                if len(deferred) > 1:
                    e_ss_last, e_t3_last = deferred.pop(0)()
            if isA:
                while deferred:
                    e_ss_last, e_t3_last = deferred.pop(0)()
            if i + 1 < NT:
                e_hn_next = norm(i + 1, ld_ev[i + 1])
            for tb in range(4):
                pb = cnt[0] % NPJ
                cnt[0] += 1
                for c in range(8):
                    e_mm = cx.op("tensor", mm(pj[pb][:, 0:NV], hn[b][:, c, tb * 128:(tb + 1) * 128], W[:, c, voff:voff + NV],
                                              c == 0, c == 7),
                                 waits=[e_hn, e_w, pj_free[pb] if c == 0 else None], sig=(c == 7))
                e_v = cx.op("vector", cpy(vo[b][:, tb, :, 0:64], pj[pb][:, 0:NV].rearrange("p (k d) -> p k d", d=64)),
                            waits=[e_mm, out_free[b], e_ms], sig=True)
                pj_free[pb] = e_v
                evs_out.append(e_v)
            hn_free[b] = e_mm
            if isA:
                e_sd = cx.op("scalar", act(sdc[:], ssc[0:32, :], AF.Sqrt, scale=G[0:32, GCOL_SCL:GCOL_SCL + 1],
                                           bias=G[0:32, GCOL_BIA:GCOL_BIA + 1]),
                             waits=[e_ss_last, sdc_free], sig=True)
                ssc_free = e_sd
                e_rs = cx.op("vector", recip(rsc[:], sdc[:]), waits=[e_sd, rsc_free], sig=True)
                sdc_free = e_rs
                e_bc = None
                for oc in range(noc):
                    k = oc % 2
                    isq = oc < nqc
                    dst = qo[b][:, oc, :] if isq else ko[b][:, oc - nqc, :]
                    e_bc = cx.op("tensor", mm(aux[k][:], SelC[:, oc, :], rsc[:], True, True),
                                 waits=[e_rs, e_w, aux_free[k]], sig=True)
                    e_o = cx.op("vector", tt(dst, aux[k][:], T3[:, oc, :], ALU.mult),
                                waits=[e_bc, e_t3_last, out_free[b]], sig=True)
                    aux_free[k] = e_o
                    t3_free[oc] = e_o
                    evs_out.append(e_o)
                rsc_free = e_bc
                h_free[b] = (e_hn, e_t3_last)
            cx.op("sync", dma(QTv[:, 0:nqc, t0:t0 + T], qo[b][:]), waits=evs_out, dsem=sts[b])
            cx.op("sync", dma(KTv[:, 0:nkc, t0:t0 + T], ko[b][:]), dsem=sts[b])
            out_free[b] = cx.op("sync", dma(Vdv[:, 4 * i:4 * i + 4, :], vo[b][:].rearrange("p a k d -> p a (k d)")),
                                dsem=sts[b])
            if i + 1 < NT:
                e_hn = e_hn_next
        for s_ in sts:
            cx.op("sync", lambda e, s_=s_: e.wait_ge(s_.h, s_.n))
        cx.flush()


def stage_oproj(cx, l, kind, h_in, h_out, OTf, wod, Seld, M1d):
    nc = cx.nc
    T = 512
    NT = S // T
    nh, nkv = CFG[kind]
    nch = nh // 2
    nr = 6 if kind == 1 else nh
    hin_v = h_in.rearrange("(c p) t -> p c t", p=128)
    hout_v = h_out.rearrange("(c p) t -> p c t", p=128)
    OTv = OTf.rearrange("(c two) r t -> two r c t", two=2)
    with ExitStack() as es:
        sb = lambda name, shape, dt: es.enter_context(nc.sbuf_tensor("O%d_" % l + name, shape, dt))
        ps = lambda name: es.enter_context(nc.psum_tensor("O%d_" % l + name, [128, 512], F32))
        W = sb("w", [128, nch, D], BF16)
        Sel = sb("sel", [nr, nch * 128], F32)
        ht = [sb("h%d" % i, [128, 8, T], F32) for i in range(2)]
        Ut = [sb("u%d" % i, [128, nch, T], F32) for i in range(2)]
        Dt = [sb("d%d" % i, [nh, T], F32) for i in range(2)]
        Rt = [sb("r%d" % i, [nr, T], F32) for i in range(2)]
        on = sb("on", [128, nch, T], BF16)
        bc = [ps("bc%d" % i) for i in range(2)]
        opp = [ps("op%d" % i) for i in range(3)]
        lds = [cx.new_sem("o%d_ld%d" % (l, i)) for i in range(2)]
        sts = [cx.new_sem("o%d_st%d" % (l, i)) for i in range(2)]
        ws = cx.new_sem("o%d_w" % l)
        load_weights_cast(cx, W, wod, ws, [(0, nch, 0, D)])
        e_w = cx.op("sync", dma(Sel[:], Seld), dsem=ws)
        if kind == 1:
            M1 = sb("m1", [nh, nr], F32)
            dsp = ps("dsp")
            e_w = cx.op("sync", dma(M1[:], M1d), dsem=ws)
        h_free = [None, None]
        u_free = [None, None]
        d_free = [None, None]
        r_free = [None, None]
        bc_free = [None, None]
        dsp_free = None
        op_free = [None] * 3
        on_free = None
        cnt = 0

        def load(i):
            b = i % 2
            t0 = i * T
            cx.op("sync", dma(ht[b][:], hin_v[:, :, t0:t0 + T]), waits=[h_free[b]], dsem=lds[b])
            cx.op("sync", dma(Ut[b][0:64, :, :], OTv[0, 0:64, 0:nch, t0:t0 + T]), waits=[u_free[b]], dsem=lds[b])
            cx.op("sync", dma(Ut[b][64:128, :, :], OTv[1, 0:64, 0:nch, t0:t0 + T]), dsem=lds[b])
            return cx.op("sync", dma(Dt[b][:], OTf[0:nh, 64, t0:t0 + T]), waits=[d_free[b]], dsem=lds[b])

        ld_ev = {0: load(0)}
        for i in range(NT):
            b = i % 2
            t0 = i * T
            if i + 1 < NT:
                ld_ev[i + 1] = load(i + 1)
            if kind == 1:
                e_ds = cx.op("tensor", mm(dsp[0:nr, :], M1[:], Dt[b][:], True, True),
                             waits=[ld_ev[i], e_w, dsp_free], sig=True)
                e_r = cx.op("vector", recip(Rt[b][:], dsp[0:nr, :]), waits=[e_ds, r_free[b]], sig=True)
                dsp_free = e_r
                d_free[b] = e_ds
            else:
                e_r = cx.op("vector", recip(Rt[b][:], Dt[b][:]), waits=[ld_ev[i], r_free[b]], sig=True)
                d_free[b] = e_r
            e_on = None
            e_bc = None
            for c in range(nch):
                k = c % 2
                e_bc = cx.op("tensor", mm(bc[k][:], Sel[:, c * 128:(c + 1) * 128], Rt[b][:], True, True),
                             waits=[e_r, e_w, bc_free[k]], sig=True)
                e_on = cx.op("vector", tt(on[:, c, :], bc[k][:], Ut[b][:, c, :], ALU.mult),
                             waits=[e_bc, ld_ev[i], on_free], sig=True)
                bc_free[k] = e_on
            u_free[b] = e_on
            r_free[b] = e_bc
            e_add = None
            e_mm = None
            for n in range(8):
                pb = cnt % 3
                cnt += 1
                for c in range(nch):
                    e_mm = cx.op("tensor", mm(opp[pb][:], W[:, c, n * 128:(n + 1) * 128], on[:, c, :], c == 0, c == nch - 1),
                                 waits=[e_on, e_w, op_free[pb] if c == 0 else None], sig=(c == nch - 1))
                e_add = cx.op("vector", tt(ht[b][:, n, :], opp[pb][:], ht[b][:, n, :], ALU.add), waits=[e_mm, ld_ev[i]], sig=True)
                op_free[pb] = e_add
            on_free = e_mm
            h_free[b] = cx.op("sync", dma(hout_v[:, :, t0:t0 + T], ht[b][:]), waits=[e_add], dsem=sts[b])
        for s_ in sts:
            cx.op("sync", lambda e, s_=s_: e.wait_ge(s_.h, s_.n))
        cx.flush()


def stage_attn(cx, l, kind, QT, KT, Vd, OTf, tabd, sinkd):
    nc = cx.nc
    nh, nkv = CFG[kind]
    has_tab = kind != 0
    NUD = 2 if kind == 1 else 4
    with ExitStack() as es:
        sb = lambda name, shape, dt: es.enter_context(nc.sbuf_tensor("A%d_" % l + name, shape, dt))
        Sps = [es.enter_context(nc.psum_tensor("A%d_S%d" % (l, i), [128, 1024], F32)) for i in range(2)]
        ops = [es.enter_context(nc.psum_tensor("A%d_o%d" % (l, i), [128, 512], F32)) for i in range(4)]
        P = [sb("P%d" % i, [128, 1024], BF16) for i in range(3)]
        UD = [sb("UD%d" % i, [65, S], F32) for i in range(NUD)]
        kls = cx.new_sem("a%d_k" % l)
        qls = [cx.new_sem("a%d_q%d" % (l, i)) for i in range(2)]
        sts = [cx.new_sem("a%d_st%d" % (l, i)) for i in range(NUD)]
        units = []
        e_tab = None
        if has_tab:
            TW = 384 if kind == 2 else 256
            SP = [sb("SP%d" % i, [128, 1024], F32) for i in range(3)]
            tab = sb("tab", [128, nh, TW], F32)
            Z = sb("Z", [128, 512], BF16)
            e_z = cx.op("gpsimd", mset(Z[:], 0.0), sig=True)
            e_tab = cx.op("sync", dma(tab[:], tabd), dsem=kls)
        e_sk = None
        if kind == 2:
            esk = sb("esk", [65, 16], F32)
            cx.op("sync", dma(esk[64:65, :], sinkd), dsem=kls)

        if kind in (0, 2):
            Kd = sb("K", [128, nkv, S], BF16)
            Vs = sb("V", [128, 32, nkv, 65], BF16)
            Qc = [sb("Qc%d" % i, [128, S], BF16) for i in range(2)]
            KTv = KT.rearrange("(c p) t -> p c t", p=128)
            for g in range(nkv):
                cx.op("sync", dma(Kd[:, g, :], KTv[:, g, :]), dsem=kls)
            Vdv = Vd.rearrange("(b p) f -> p b f", p=128)
            e_kv = None
            for q4 in range(4):
                e_kv = cx.op("sync", dma(Vs[:, 8 * q4:8 * q4 + 8, :, :].rearrange("p b k d -> p b (k d)"),
                                         Vdv[:, 8 * q4:8 * q4 + 8, :]), dsem=kls)
            if kind == 2:
                e_sk = cx.op("scalar", act(esk[64:65, :], esk[64:65, :], AF.Exp), waits=[e_kv], sig=True)
            q_free = [None, None]
            for c in range(nh // 2):
                g = c // 2
                for qt in range(8):
                    qs = qt * 512
                    sl = []
                    if kind == 0:
                        kbs = [(kb, qs, qs + 512, 0) for kb in range(32)]
                    else:
                        kbs = []
                        for kb in range(max(0, 4 * qt - 1), min(32, 4 * qt + 5)):
                            lo, hi = max(qs, 128 * kb - 128), min(qs + 512, 128 * kb + 256)
                            kbs.append((kb, lo, hi, lo - (128 * kb - 128)))
                    for (kb, lo, hi, off) in kbs:
                        subs = []
                        for hh in range(2):
                            subs.append(dict(k=Kd[hh * 64:(hh + 1) * 64, g, kb * 128:(kb + 1) * 128],
                                             q=(c, hh, lo, hi), n=hi - lo, v=Vs[:, kb, g, :], c0=lo - qs,
                                             tab=tab[:, 2 * c + hh, off:off + hi - lo] if has_tab else None))
                        sl.append(dict(subs=subs))
                    units.append(dict(hs=(2 * c, 2 * c + 1), ql=512, steps=sl, dst=("plain", qs), qchunk=c,
                                      hfirst=(qt == 0), hlast=(qt == 7)))
        else:
            Qg = sb("Qg", [128, 3, S], BF16)
            Kg = sb("Kg", [128, 2, S], BF16)
            Qp = sb("Qp", [128, 3, S], BF16)
            Kp = sb("Kp", [128, 2, S], BF16)
            Vs = sb("V", [128, 32, 2, 65], BF16)
            for g, (window, d) in enumerate(B_GROUPS):
                L = S // d
                ql = min(512, L)
                nb = L // 128
                src = Kp if d > 1 else Kg
                for ci in range(3):
                    ulist = []
                    for rho in range(d):
                        for qt in range(L // ql):
                            qs = qt * ql
                            sl = []
                            for kb in range(max(0, (qs - 64) // 128), min(nb, (qs + ql + 64 + 127) // 128)):
                                lo, hi = max(qs, 128 * kb - 64), min(qs + ql, 128 * kb + 192)
                                if hi <= lo:
                                    continue
                                off = lo - (128 * kb - 64)
                                subs = []
                                for hh in range(2):
                                    i6 = 2 * ci + hh
                                    kvl = i6 // 3
                                    subs.append(dict(
                                        k=src[hh * 64:(hh + 1) * 64, kvl, rho * L + kb * 128:rho * L + (kb + 1) * 128],
                                        q=(ci, hh, rho * L + lo, rho * L + hi), n=hi - lo,
                                        v=Vs[:, rho * nb + kb, kvl, :], c0=lo - qs,
                                        tab=tab[:, 6 * g + i6, off:off + hi - lo]))
                                sl.append(dict(subs=subs))
                            ulist.append(dict(hs=(6 * g + 2 * ci, 6 * g + 2 * ci + 1), ql=ql, steps=sl,
                                              dst=("perm", d, rho, qs), group=g))
                    ulist[0]["hfirst"] = True
                    ulist[-1]["hlast"] = True
                    units.extend(ulist)

        flat = []
        for ui, u in enumerate(units):
            for si, st in enumerate(u["steps"]):
                flat.append((ui, si == 0, si == len(u["steps"]) - 1, st))
        NS = len(flat)
        e_qk, e_rd, e_exp, e_pv = {}, {}, {}, {}
        e_evac = {}
        ud_free = [None] * NUD
        state = dict(group=-1, qchunk=-1, ready=None, pcount=-1, last_pe=None, qbuf_last={})

        def prepare(ui):
            u = units[ui]
            if kind in (0, 2):
                c = u["qchunk"]
                if c != state["qchunk"]:
                    state["qchunk"] = c
                    ql_ = state["qbuf_last"]
                    for cc in (c, c + 1):
                        if cc not in ql_ and cc < nh // 2:
                            ql_[cc] = cx.op("sync", dma(Qc[cc % 2][:], QT[cc * 128:(cc + 1) * 128, :]),
                                            waits=[q_free[cc % 2]], dsem=qls[cc % 2])
                    state["ready"] = [ql_[c], e_kv, e_tab]
            else:
                g = u["group"]
                if g != state["group"]:
                    state["group"] = g
                    d = B_GROUPS[g][1]
                    L = S // d
                    nb = L // 128
                    wl = [state["last_pe"]]
                    for ci in range(3):
                        cx.op("sync", dma(Qg[:, ci, :], QT[(3 * g + ci) * 128:(3 * g + ci + 1) * 128, :]), waits=wl, dsem=qls[0])
                    for kvl in range(2):
                        cx.op("sync", dma(Kg[:, kvl, :], KT[(2 * g + kvl) * 128:(2 * g + kvl + 1) * 128, :]), dsem=qls[0])
                    Vv = Vd.rearrange("(b p r) f -> r p b f", p=128, r=d)
                    e_l = None
                    for rho in range(d):
                        e_l = cx.op("sync", dma(Vs[:, rho * nb:(rho + 1) * nb, :, :].rearrange("p b k d -> p b (k d)"),
                                                Vv[rho, :, :, 2 * g * 65:(2 * g + 2) * 65]), dsem=qls[0])
                    rdy = [e_l, e_tab]
                    if d > 1:
                        e_p = None
                        for ci in range(3):
                            e_p = cx.op("gpsimd", cpy(Qp[:, ci, :].rearrange("p (r m) -> p r m", r=d),
                                                      Qg[:, ci, :].rearrange("p (m r) -> p r m", r=d)),
                                        waits=[e_l, state["last_pe"]], sig=True)
                        for kvl in range(2):
                            e_p = cx.op("gpsimd", cpy(Kp[:, kvl, :].rearrange("p (r m) -> p r m", r=d),
                                                      Kg[:, kvl, :].rearrange("p (m r) -> p r m", r=d)),
                                        waits=[e_l], sig=True)
                        rdy.append(e_p)
                    state["ready"] = rdy

        def q_ap(q):
            ci, hh, a, b_ = q
            if kind in (0, 2):
                return Qc[ci % 2][hh * 64:(hh + 1) * 64, a:b_]
            src_ = Qp if B_GROUPS[state["group"]][1] > 1 else Qg
            return src_[hh * 64:(hh + 1) * 64, ci, a:b_]

        def emit_qk(s):
            ui, first, last, st = flat[s]
            if first:
                prepare(ui)
            sbi = s % 2
            ev = None
            for k, sub in enumerate(st["subs"]):
                n = sub["n"]
                ev = cx.op("tensor", mm(Sps[sbi][:, k * 512:k * 512 + n], sub["k"], q_ap(sub["q"]), True, True),
                           waits=state["ready"] + [e_rd.get(s - 2)], sig=(k == 1))
            e_qk[s] = ev
            state["last_pe"] = ev
            if kind in (0, 2):
                q_free[units[ui]["qchunk"] % 2] = ev

        def emit_sm(s):
            ui, first, last, st = flat[s]
            sbi, pi = s % 2, s % 3
            n = st["subs"][0]["n"]
            if has_tab:
                e_a = None
                for k, sub in enumerate(st["subs"]):
                    e_a = cx.op("vector", tt(SP[pi][:, k * 512:k * 512 + n], Sps[sbi][:, k * 512:k * 512 + n],
                                             sub["tab"], ALU.add),
                                waits=[e_qk[s], e_exp.get(s - 3), e_tab], sig=True)
                e_rd[s] = e_a
                e_exp[s] = cx.op("scalar", act(P[pi][:].rearrange("p (k n) -> p k n", k=2)[:, :, 0:n],
                                               SP[pi][:].rearrange("p (k n) -> p k n", k=2)[:, :, 0:n], AF.Exp),
                                 waits=[e_a, e_pv.get(s - 3)], sig=True)
            else:
                e_exp[s] = cx.op("scalar", act(P[pi][:], Sps[sbi][:], AF.Exp),
                                 waits=[e_qk[s], e_pv.get(s - 3)], sig=True)
                e_rd[s] = e_exp[s]

        def emit_pv(s):
            ui, first, last, st = flat[s]
            pi = s % 3
            u = units[ui]
            ev = None
            for k, sub in enumerate(st["subs"]):
                ob = 2 * (ui % 2) + k
                n, c0 = sub["n"], sub["c0"]
                if first and has_tab:
                    cx.op("tensor", mm(ops[ob][0:65, 0:u["ql"]], Z[:, 0:65], Z[:, 0:u["ql"]], True, False),
                          waits=[e_z, e_evac.get(ui - 2)])
                ev = cx.op("tensor", mm(ops[ob][0:65, c0:c0 + n], sub["v"], P[pi][:, k * 512:k * 512 + n],
                                        first and not has_tab, last),
                           waits=[e_exp[s], e_evac.get(ui - 2) if first else None], sig=(k == 1))
            e_pv[s] = ev
            state["last_pe"] = ev
            if last:
                if u.get("hfirst"):
                    state["pcount"] += 1
                ql = u["ql"]
                e_ev = None
                for k in range(2):
                    ob = 2 * (ui % 2) + k
                    hb = (2 * state["pcount"] + k) % NUD
                    if u["dst"][0] == "plain":
                        dst = UD[hb][0:65, u["dst"][1]:u["dst"][1] + ql]
                    else:
                        _, d, rho, qs = u["dst"]
                        dst = UD[hb][0:65, :].rearrange("p (m r) -> p r m", r=d)[:, rho, qs:qs + ql]
                    e_ev = cx.op("vector", cpy(dst, ops[ob][0:65, 0:ql]),
                                 waits=[ev, ud_free[hb] if u.get("hfirst") else None], sig=True)
                e_evac[ui] = e_ev
                if u.get("hlast"):
                    for k in range(2):
                        hb = (2 * state["pcount"] + k) % NUD
                        h = u["hs"][k]
                        if kind == 2:
                            e_ev = cx.op("vector", ts(UD[hb][64:65, :], UD[hb][64:65, :], esk[64:65, h:h + 1], ALU.add),
                                         waits=[e_sk], sig=True)
                        ud_free[hb] = cx.op("sync", dma(OTf[h], UD[hb][:]), waits=[e_ev], dsem=sts[hb])

        def new_group(s):
            return kind == 1 and units[flat[s][0]]["group"] != units[flat[s - 1][0]]["group"]

        emit_qk(0)
        for s in range(NS):
            defer = s + 1 < NS and new_group(s + 1)
            if s + 1 < NS and not defer:
                emit_qk(s + 1)
            emit_sm(s)
            emit_pv(s)
            if defer:
                emit_qk(s + 1)
        for s_ in sts:
            cx.op("sync", lambda e, s_=s_: e.wait_ge(s_.h, s_.n))
        cx.flush()


def layer_dims(kind):
    nh, nkv = CFG[kind]
    return nh, nkv, nh * 64 + nkv * 128 + nkv * 64


def build(nlayers=DEPTH, debug=False):
    nc = bass.Bass("TRN2", target_bir_lowering=False)
    ein = lambda name, shape, dt=F32: nc.dram_tensor(name, shape, dt, kind="ExternalInput").ap()
    xT = ein("xT", [D, S])
    yT = nc.dram_tensor("yT", [D, S], F32, kind="ExternalOutput").ap()
    Gd = ein("G", [128, NG])
    onesd = ein("ones", [128, 128])
    BDCd = ein("BDC", [128, 12, 32])
    SelCd = ein("SelC", [32, 12, 128])
    Rd = ein("Rm", [128, 128])
    cosd = ein("cosT", [128, S])
    sind = ein("sinT", [128, S])
    tabBd = ein("tabB", [128, 18, 256])
    tabCd = ein("tabC", [128, 16, 384])
    selAd = ein("selA", [16, 8 * 128])
    selBd = ein("selB", [6, 9 * 128])
    m1Bd = ein("m1B", [18, 6])
    sinkd = ein("sinks", [1, 16])
    wq, wo, w1, w2 = [], [], [], []
    for l in range(nlayers):
        nh, nkv, NW = layer_dims(KINDS[l])
        wq.append(ein("wqkv%d" % l, [128, 8, NW]))
        wo.append(ein("wo%d" % l, [128, nh // 2, D]))
        w1.append(ein("w1_%d" % l, [128, 8, DFF]))
        w2.append(ein("w2_%d" % l, [128, 32, D]))
    sk = "ExternalOutput" if debug else "Internal"
    hA = nc.dram_tensor("hA", [D, S], F32, kind=sk).ap()
    hB = nc.dram_tensor("hB", [D, S], F32, kind=sk).ap()
    QT = nc.dram_tensor("QT", [1152, S], BF16, kind=sk).ap()
    KT = nc.dram_tensor("KT", [768, S], BF16, kind=sk).ap()
    VdA = nc.dram_tensor("VdA", [S, 4 * 65], BF16, kind=sk).ap()
    VdB = nc.dram_tensor("VdB", [S, 6 * 65], BF16, kind=sk).ap()
    OTf = nc.dram_tensor("OTf", [18, 65, S], F32, kind=sk).ap()
    with ExitStack() as es:
        cx = Ctx(nc, es)
        G = es.enter_context(nc.sbuf_tensor("Gsb", [128, NG], F32))
        ones = es.enter_context(nc.sbuf_tensor("onessb", [128, 128], F32))
        Rm = es.enter_context(nc.sbuf_tensor("Rsb", [128, 128], F32))
        cs = cx.new_sem("const")
        cx.op("sync", dma(G[:], Gd), dsem=cs)
        cx.op("sync", dma(ones[:], onesd), dsem=cs)
        cx.op("sync", dma(Rm[:], Rd), dsem=cs)
        for e in ["scalar", "vector", "tensor", "gpsimd"]:
            cx.op(e, lambda en: en.wait_ge(cs.h, cs.n))
        cx.flush()
        h_cur = xT
        used = [0, 0, 0]
        for l in range(nlayers):
            kind = KINDS[l]
            j = used[kind]
            used[kind] += 1
            Vd = VdB if kind == 1 else VdA
            stage_qkv(cx, l, kind, j, h_cur, wq[l], QT, KT, Vd, G[:], ones[:], BDCd, SelCd, Rm[:], cosd, sind)
            if debug == 3 and l == nlayers - 1:
                break
            stage_attn(cx, l, kind, QT, KT, Vd, OTf, tabBd if kind == 1 else tabCd, sinkd)
            if debug == 1 and l == nlayers - 1:
                break
            stage_oproj(cx, l, kind, h_cur, hA, OTf, wo[l], selBd if kind == 1 else selAd, m1Bd)
            last = l == nlayers - 1
            if debug and last:
                break
            stage_mlp(cx, l, hA, yT if last else hB, w1[l], w2[l], G[:], ones[:], last)
            h_cur = hB
    return nc


def alibi_slopes_np(n):
    return (2.0 ** (-8.0 * np.arange(1, n + 1, dtype=np.float32) / n)).astype(np.float32)


def host_consts():
    c = {}
    c["ones"] = np.ones((128, 128), np.float32)
    bdc = np.zeros((128, 12, 32), np.float32)
    selc = np.zeros((32, 12, 128), np.float32)
    for oc in range(12):
        for p in range(128):
            bdc[p, oc, 2 * oc + p // 64] = 1
            selc[2 * oc + p // 64, oc, p] = 1
    c["BDC"] = bdc
    c["SelC"] = selc
    R = np.zeros((128, 128), np.float32)
    for m in range(128):
        jj = (m % 64) % 32
        if jj < 16:
            R[m + 16, m] = -1.0
        else:
            R[m - 16, m] = 1.0
    c["Rm"] = R
    t = np.arange(S)
    row = (t // GRID_W).astype(np.float32)
    col = (t % GRID_W).astype(np.float32)
    inv = (np.float32(ROPE_THETA) ** (-np.arange(0, 32, 2, dtype=np.float32) / np.float32(32))).astype(np.float32)
    cosT = np.zeros((128, S), np.float32)
    sinT = np.zeros((128, S), np.float32)
    for p in range(128):
        dd = p % 64
        pos = row if dd < 32 else col
        ang = (pos * inv[(dd % 32) % 16]).astype(np.float32)
        cosT[p] = np.cos(ang.astype(np.float64)).astype(np.float32)
        sinT[p] = np.sin(ang.astype(np.float64)).astype(np.float32)
    c["cosT"], c["sinT"] = cosT, sinT
    k = np.arange(128)[:, None]
    slB = alibi_slopes_np(18)
    tabB = np.zeros((128, 18, 256), np.float32)
    jB = np.arange(256)[None, :]
    relB = np.abs(k + 64 - jB)
    for h in range(18):
        d = B_GROUPS[h // 6][1]
        tabB[:, h, :] = np.where(relB <= 64, -slB[h] * (relB * d).astype(np.float32), NEG)
    c["tabB"] = tabB
    slC = alibi_slopes_np(16)
    tabC = np.zeros((128, 16, 384), np.float32)
    jC = np.arange(384)[None, :]
    relC = np.abs(k + 128 - jC)
    for h in range(16):
        tabC[:, h, :] = np.where(relC <= 128, -slC[h] * relC.astype(np.float32), NEG)
    c["tabC"] = tabC
    selA = np.zeros((16, 8 * 128), np.float32)
    for col_ in range(8 * 128):
        selA[2 * (col_ // 128) + (col_ % 128) // 64, col_] = 1
    c["selA"] = selA
    selB = np.zeros((6, 9 * 128), np.float32)
    for col_ in range(9 * 128):
        hd = 2 * (col_ // 128) + (col_ % 128) // 64
        selB[hd % 6, col_] = 1
    c["selB"] = selB
    m1 = np.zeros((18, 6), np.float32)
    for h in range(18):
        m1[h, h % 6] = 1
    c["m1B"] = m1
    return c


def arr_k(w, nchunk):
    return np.ascontiguousarray(w.reshape(nchunk, 128, w.shape[1]).transpose(1, 0, 2))


def host_prep(inp, nlayers=DEPTH):
    shared = host_consts()
    G = np.zeros((128, NG), np.float32)
    for l in range(DEPTH):
        G[:, gcol_attn(l):gcol_attn(l) + 8] = inp["attn_norm"][l].reshape(8, 128).T
        G[:, gcol_mlp(l):gcol_mlp(l) + 8] = inp["mlp_norm"][l].reshape(8, 128).T
    G[:, GCOL_FINAL:GCOL_FINAL + 8] = inp["final_norm"].reshape(8, 128).T
    for j in range(2):
        G[:, GCOL_QG + j] = np.tile(inp["a_q_gain"][j], 2)
        G[:, GCOL_KG + j] = np.tile(inp["a_k_gain"][j], 2)
    G[:, GCOL_EPS] = EPS
    G[:, GCOL_EPS64] = 64 * EPS
    G[:, GCOL_SCL] = 1.0 / 64
    G[:, GCOL_BIA] = EPS
    G[0:16, GCOL_SCL] = 1.0
    G[0:16, GCOL_BIA] = 64 * EPS
    shared["G"] = G
    shared["sinks"] = np.ascontiguousarray(inp["c_sinks"][0:1]).astype(np.float32)
    used = [0, 0, 0]
    for l in range(nlayers):
        kind = KINDS[l]
        j = used[kind]
        used[kind] += 1
        nh, nkv = CFG[kind]
        w = [inp["a_w_qkv"], inp["b_w_qkv"], inp["c_w_qkv"]][kind][j]
        wo = [inp["a_w_o"], inp["b_w_o"], inp["c_w_o"]][kind][j]
        nq = nh * 64
        q, k, v = w[:, :nq], w[:, nq:nq + nkv * 64], w[:, nq + nkv * 64:]
        kd = np.concatenate([np.concatenate([k[:, g * 64:(g + 1) * 64]] * 2, axis=1) for g in range(nkv)], axis=1)
        shared["wqkv%d" % l] = arr_k(np.concatenate([q, kd, v], axis=1), 8)
        shared["wo%d" % l] = arr_k(wo, nh // 2)
        shared["w1_%d" % l] = arr_k(inp["mlp_w1"][l], 8)
        shared["w2_%d" % l] = arr_k(inp["mlp_w2"][l], 32)
    return shared


_NC_CACHE = {}


def kernel(**inputs):
    inp = {k: np.asarray(v) for k, v in inputs.items()}
    shared = host_prep(inp)
    x = inp["x"].astype(np.float32)
    if DEPTH not in _NC_CACHE:
        _NC_CACHE[DEPTH] = build(DEPTH)
    nc = _NC_CACHE[DEPTH]
    in_maps = []
    for b in range(NCORES):
        m = dict(shared)
        m["xT"] = np.ascontiguousarray(x[b].T)
        in_maps.append(m)
    res = run_bass_kernel_spmd(nc, in_maps, core_ids=list(range(NCORES)))
    out = np.empty((NCORES, S, D), np.float32)
    for b in range(NCORES):
        out[b] = res.results[b]["yT"].T
    return out
```

```python
import numpy as np
from contextlib import ExitStack
import concourse.bass as bass
import concourse.mybir as mybir
from concourse.bass_utils import run_bass_kernel_spmd

F32 = mybir.dt.float32
BF16 = mybir.dt.bfloat16
AF = mybir.ActivationFunctionType
ALU = mybir.AluOpType

S = 4096
D = 1024
DFF = 4096
HD = 64
NCORES = 8
DEPTH = 4
KINDS = [0, 1, 2, 0]
EPS = 1e-6
GRID_W = 64
ROPE_THETA = 10000.0
B_GROUPS = ((128, 1), (512, 4), (2048, 16))
NEG = -30000.0

CFG = {0: (16, 4), 1: (18, 6), 2: (16, 4)}

ENGS = ["sync", "scalar", "gpsimd", "vector", "tensor"]
STRICT = False


class Sem:
    def __init__(self, h):
        self.h = h
        self.n = 0


class Ctx:
    def __init__(self, nc, es):
        self.nc = nc
        self.es = es
        self.lists = {e: [] for e in ENGS}
        self.waited = {e: {} for e in ENGS}
        self.esem = {}
        self.allsems = []
        for e in ["scalar", "gpsimd", "vector", "tensor"]:
            self.esem[e] = self.new_sem("e_" + e)
        self.nsem = 0

    def new_sem(self, name):
        s = Sem(self.es.enter_context(self.nc.semaphore(name)))
        self.allsems.append(s)
        return s

    def op(self, eng, fn, waits=(), sig=False, dsem=None):
        ws = []
        wd = self.waited[eng]
        flat = []
        for ev in waits:
            if ev is None:
                continue
            if isinstance(ev[0], Sem):
                flat.append(ev)
            else:
                flat.extend(x for x in ev if x is not None)
        for ev in flat:
            sem, val = ev
            if eng in self.esem and sem is self.esem[eng] and not STRICT:
                continue
            if wd.get(id(sem), 0) >= val:
                continue
            wd[id(sem)] = val
            ws.append((sem, val))
        ev = None
        inc = 0
        if dsem is not None:
            dsem.n += 16
            inc = 16
            ev = (dsem, dsem.n)
        elif sig:
            s = self.esem[eng]
            s.n += 1
            inc = 1
            ev = (s, s.n)
        self.lists[eng].append((ws, fn, ev, inc))
        return ev

    def flush(self):
        nc = self.nc
        with nc.Block() as block:
            for eng in ENGS:
                lst = self.lists[eng]
                if not lst:
                    continue

                def body(e, lst=lst):
                    for ws, fn, ev, inc in lst:
                        for sem, val in ws:
                            e.wait_ge(sem.h, val)
                        ins = fn(e)
                        if ev is not None:
                            ins.then_inc(ev[0].h, inc)

                getattr(block, eng)(body)
        self.lists = {e: [] for e in ENGS}


def mm(out, lhsT, rhs, start, stop):
    return lambda e: e.matmul(out, lhsT=lhsT, rhs=rhs, start=start, stop=stop)


def act(out, in_, func, scale=1.0, bias=None):
    if bias is None:
        return lambda e: e.activation(out=out, in_=in_, func=func, scale=scale)
    return lambda e: e.activation(out=out, in_=in_, func=func, scale=scale, bias=bias)


def tt(out, in0, in1, op):
    return lambda e: e.tensor_tensor(out=out, in0=in0, in1=in1, op=op)


def stt(out, in0, scalar, in1, op0, op1):
    return lambda e: e.scalar_tensor_tensor(out=out, in0=in0, scalar=scalar, in1=in1, op0=op0, op1=op1)


def ts(out, in0, s1, op0, s2=None, op1=None):
    if op1 is None:
        return lambda e: e.tensor_scalar(out=out, in0=in0, scalar1=s1, scalar2=None, op0=op0)
    return lambda e: e.tensor_scalar(out=out, in0=in0, scalar1=s1, scalar2=s2, op0=op0, op1=op1)


def recip(out, in_):
    return lambda e: e.reciprocal(out=out, in_=in_)


def cpy(out, in_):
    return lambda e: e.tensor_copy(out=out, in_=in_)


def dma(out, in_):
    return lambda e: e.dma_start(out=out, in_=in_)


def mset(ap, v):
    return lambda e: e.memset(ap, v)


def gcol_attn(l):
    return l * 8


def gcol_mlp(l):
    return 32 + l * 8


GCOL_FINAL = 64
GCOL_QG = 72
GCOL_KG = 74
GCOL_EPS = 76
GCOL_EPS64 = 77
GCOL_SCL = 78
GCOL_BIA = 79
NG = 80


class RmsNorm:
    def __init__(self, cx, sb, ps, name, T, ones, G):
        self.cx, self.T, self.ones, self.G = cx, T, ones, G
        self.sq = [sb(name + "_sq%d" % i, [128, T], F32) for i in range(2)]
        self.sd = sb(name + "_sd", [128, T], F32)
        self.rstd = sb(name + "_rstd", [128, T], F32)
        self.ss = ps(name + "_ss")
        self.sq_free = [None, None]
        self.ss_free = None
        self.sd_free = None
        self.users = None

    def emit(self, ht, h_ready):
        cx, T = self.cx, self.T
        ev_mm = None
        for c in range(8):
            b = c % 2
            e_sq = cx.op("scalar", act(self.sq[b][:], ht[:, c, :], AF.Square),
                         waits=[h_ready, self.sq_free[b]], sig=True)
            ev_mm = cx.op("tensor", mm(self.ss[:, 0:T], self.ones, self.sq[b][:], c == 0, c == 7),
                          waits=[e_sq, self.ss_free if c == 0 else None], sig=True)
            self.sq_free[b] = ev_mm
        e_sd = cx.op("scalar", act(self.sd[:], self.ss[:, 0:T], AF.Sqrt, scale=1.0 / D,
                                   bias=self.G[:, GCOL_EPS:GCOL_EPS + 1]),
                     waits=[ev_mm, self.sd_free], sig=True)
        self.ss_free = e_sd
        e_r = cx.op("vector", recip(self.rstd[:], self.sd[:]), waits=[e_sd, self.users], sig=True)
        self.sd_free = e_r
        return e_r


def load_weights_cast(cx, Wsb, Wdram, wsem, pieces):
    ev = None
    for (c0, c1, n0, n1) in pieces:
        ev = cx.op("gpsimd", dma(Wsb[:, c0:c1, n0:n1], Wdram[:, c0:c1, n0:n1]), dsem=wsem)
    return ev


def stage_mlp(cx, l, h_in, h_out, w1d, w2d, G, ones, last):
    nc = cx.nc
    T = 512
    NT = S // T
    HM = 16
    hin_v = h_in.rearrange("(c p) t -> p c t", p=128)
    hout_v = h_out.rearrange("(c p) t -> p c t", p=128)
    with ExitStack() as es:
        sb = lambda name, shape, dt: es.enter_context(nc.sbuf_tensor("L%d_" % l + name, shape, dt))
        ps = lambda name: es.enter_context(nc.psum_tensor("L%d_" % l + name, [128, 512], F32))
        W1 = sb("w1", [128, 8, DFF], BF16)
        W2 = sb("w2", [128, 32, D], BF16)
        ht = [sb("m_h%d" % i, [128, 8, T], F32) for i in range(2)]
        hn = sb("m_hn", [128, 8, T], BF16)
        u = sb("m_u", [128, HM, T], BF16)
        rl = [sb("m_rl%d" % i, [128, T], F32) for i in range(3)]
        nrm = RmsNorm(cx, sb, ps, "m_n", T, ones, G)
        if last:
            yb = [sb("m_y%d" % i, [128, T], F32) for i in range(2)]
            yb_free = [None, None]
        ups = [ps("m_ups%d" % i) for i in range(4)]
        ops_ = [ps("m_ops%d" % i) for i in range(3)]
        lds = [cx.new_sem("m%d_ld%d" % (l, i)) for i in range(2)]
        sts = [cx.new_sem("m%d_st%d" % (l, i)) for i in range(2)]
        w1s = cx.new_sem("m%d_w1" % l)
        w2s = cx.new_sem("m%d_w2" % l)
        NU, NO = len(ups), len(ops_)

        e_w1 = load_weights_cast(cx, W1, w1d, w1s, [(0, 8, 0, 2048), (0, 8, 2048, 4096)])
        e_w2 = load_weights_cast(cx, W2, w2d, w2s, [(8 * i, 8 * i + 8, 0, D) for i in range(4)])

        h_free = [None, None]
        hn_free = None
        u_free = None
        rl_free = [None, None, None]
        ups_free = [None] * NU
        ops_free = [None] * NO
        gm = gcol_mlp(l)
        cu = [0]
        co = [0]

        def load(i):
            b = i % 2
            return cx.op("sync", dma(ht[b][:], hin_v[:, :, i * T:(i + 1) * T]), waits=[h_free[b]], dsem=lds[b])

        def norm(i, h_ready):
            b = i % 2
            e_r = nrm.emit(ht[b], h_ready)
            e_hn = None
            for c in range(8):
                e_hn = cx.op("vector", stt(hn[:, c, :], ht[b][:, c, :], G[:, gm + c:gm + c + 1], nrm.rstd[:],
                                           ALU.mult, ALU.mult), waits=[e_r, hn_free], sig=True)
            nrm.users = e_hn
            return e_hn

        ld_ev = {0: load(0)}
        e_hn = norm(0, ld_ev[0])
        for i in range(NT):
            b = i % 2
            t0 = i * T
            if i + 1 < NT:
                ld_ev[i + 1] = load(i + 1)
            e_add = None
            for half in range(2):
                e_u = None
                e_mm = None
                for mm_ in range(HM):
                    m = half * HM + mm_
                    pb = cu[0] % NU
                    cu[0] += 1
                    for c in range(8):
                        e_mm = cx.op("tensor", mm(ups[pb][:], W1[:, c, m * 128:(m + 1) * 128], hn[:, c, :],
                                                  c == 0, c == 7),
                                     waits=[e_hn, e_w1, ups_free[pb] if c == 0 else None], sig=(c == 7))
                    rb = m % 3
                    e_rl = cx.op("scalar", act(rl[rb][:], ups[pb][:], AF.Relu), waits=[e_mm, rl_free[rb]], sig=True)
                    ups_free[pb] = e_rl
                    e_u = cx.op("gpsimd", tt(u[:, mm_, :], rl[rb][:], rl[rb][:], ALU.mult), waits=[e_rl, u_free], sig=True)
                    rl_free[rb] = e_u
                if half == 1:
                    hn_free = e_mm
                    if i + 1 < NT:
                        e_hn_next = norm(i + 1, ld_ev[i + 1])
                for n in range(8):
                    pb = co[0] % NO
                    co[0] += 1
                    for mm_ in range(HM):
                        m = half * HM + mm_
                        e_mm = cx.op("tensor", mm(ops_[pb][:], W2[:, m, n * 128:(n + 1) * 128], u[:, mm_, :],
                                                  mm_ == 0, mm_ == HM - 1),
                                     waits=[e_u, e_w2, ops_free[pb] if mm_ == 0 else None], sig=(mm_ == HM - 1))
                    e_add = cx.op("vector", tt(ht[b][:, n, :], ops_[pb][:], ht[b][:, n, :], ALU.add),
                                  waits=[e_mm, ld_ev[i]], sig=True)
                    ops_free[pb] = e_add
                u_free = e_mm
            if not last:
                h_free[b] = cx.op("sync", dma(hout_v[:, :, t0:t0 + T], ht[b][:]), waits=[e_add], dsem=sts[b])
            else:
                e_r2 = nrm.emit(ht[b], e_add)
                e_y = None
                for c in range(8):
                    yi = c % 2
                    e_y = cx.op("vector", stt(yb[yi][:], ht[b][:, c, :], G[:, GCOL_FINAL + c:GCOL_FINAL + c + 1],
                                              nrm.rstd[:], ALU.mult, ALU.mult), waits=[e_r2, yb_free[yi]], sig=True)
                    yb_free[yi] = cx.op("sync", dma(hout_v[:, c, t0:t0 + T], yb[yi][:]), waits=[e_y], dsem=sts[yi])
                nrm.users = e_y
                h_free[b] = e_y
            if i + 1 < NT:
                e_hn = e_hn_next
        for s_ in sts:
            if s_.n:
                cx.op("sync", lambda e, s_=s_: e.wait_ge(s_.h, s_.n))
        cx.flush()


def stage_qkv(cx, l, kind, j, h_in, wd, QT, KT, Vd, G, ones, BDCd, SelCd, Rm, cosd, sind):
    nc = cx.nc
    T = 512
    NT = S // T
    nh, nkv = CFG[kind]
    nqc, nkc, NV = nh // 2, nkv, nkv * 64
    noc = nqc + nkc
    NW = nh * 64 + nkv * 128 + NV
    voff = nh * 64 + nkv * 128
    hin_v = h_in.rearrange("(c p) t -> p c t", p=128)
    QTv = QT.rearrange("(c p) t -> p c t", p=128)
    KTv = KT.rearrange("(c p) t -> p c t", p=128)
    Vdv = Vd.rearrange("(tb p) f -> p tb f", p=128)
    isA = kind == 0
    with ExitStack() as es:
        sb = lambda name, shape, dt: es.enter_context(nc.sbuf_tensor("Q%d_" % l + name, shape, dt))
        ps = lambda name: es.enter_context(nc.psum_tensor("Q%d_" % l + name, [128, 512], F32))
        W = sb("w", [128, 8, NW], BF16)
        ht = [sb("h%d" % i, [128, 8, T], F32) for i in range(2)]
        hn = [sb("hn%d" % i, [128, 8, T], BF16) for i in range(2)]
        qo = [sb("qo%d" % i, [128, nqc, T], BF16) for i in range(2)]
        ko = [sb("ko%d" % i, [128, nkc, T], BF16) for i in range(2)]
        vo = [sb("vo%d" % i, [128, 4, nkv, 65], BF16) for i in range(2)]
        nrm = RmsNorm(cx, sb, ps, "n", T, ones, G)
        pj = [ps("pj%d" % i) for i in range(3 if not isA else 2)]
        NPJ = len(pj)
        lds = [cx.new_sem("q%d_ld%d" % (l, i)) for i in range(2)]
        sts = [cx.new_sem("q%d_st%d" % (l, i)) for i in range(2)]
        ws = cx.new_sem("q%d_w" % l)
        half = NW // 2
        e_w = load_weights_cast(cx, W, wd, ws, [(0, 4, 0, NW), (4, 8, 0, NW)] if NW <= 2048 else
                                [(0, 8, 0, half), (0, 8, half, NW)])
        e_ms = None
        for b in range(2):
            e_ms = cx.op("gpsimd", mset(vo[b][:], 1.0), sig=True)
        if isA:
            cst = [sb("cs%d" % i, [128, 2, T], F32) for i in range(2)]
            tmpn = ["qg", "sq", "t1", "t2"]
            tmp = {n: [sb(n + "%d" % i, [128, T], F32) for i in range(2)] for n in tmpn}
            tfree = {n: [None, None] for n in tmpn}
            T3 = sb("T3", [128, noc, T], F32)
            t3_free = [None] * noc
            BDC = sb("BDC", [128, noc, 32], F32)
            SelC = sb("SelC", [32, noc, 128], F32)
            sdc = sb("sdc", [32, T], F32)
            rsc = sb("rsc", [32, T], F32)
            aux = [ps("aux%d" % i) for i in range(2)]
            aux_free = [None, None]
            ssc = ps("ssc")
            ssc_free = None
            sdc_free = None
            rsc_free = None
            cx.op("sync", dma(BDC[:], BDCd), dsem=ws)
            e_w = cx.op("sync", dma(SelC[:], SelCd), dsem=ws)

        h_free = [None, None]
        out_free = [None, None]
        hn_free = [None, None]
        pj_free = [None] * NPJ
        gm = gcol_attn(l)
        cnt = [0]

        def load(i):
            b = i % 2
            ev = cx.op("sync", dma(ht[b][:], hin_v[:, :, i * T:(i + 1) * T]), waits=[h_free[b]], dsem=lds[b])
            if isA:
                cx.op("sync", dma(cst[b][:, 0, :], cosd[:, i * T:(i + 1) * T]), dsem=lds[b])
                ev = cx.op("sync", dma(cst[b][:, 1, :], sind[:, i * T:(i + 1) * T]), dsem=lds[b])
            return ev

        def norm(i, h_ready):
            b = i % 2
            e_r = nrm.emit(ht[b], h_ready)
            e_hn = None
            for c in range(8):
                e_hn = cx.op("vector", stt(hn[b][:, c, :], ht[b][:, c, :], G[:, gm + c:gm + c + 1], nrm.rstd[:],
                                           ALU.mult, ALU.mult), waits=[e_r, hn_free[b]], sig=True)
            nrm.users = e_hn
            if not isA:
                h_free[b] = e_hn
            return e_hn

        ld_ev = {0: load(0)}
        e_hn = norm(0, ld_ev[0])
        for i in range(NT):
            b = i % 2
            t0 = i * T
            if i + 1 < NT:
                ld_ev[i + 1] = load(i + 1)
            evs_out = []
            deferred = []
            e_mm = None
            e_t2 = None
            for oc in range(noc):
                isq = oc < nqc
                pb = cnt[0] % NPJ
                cnt[0] += 1
                for c in range(8):
                    e_mm = cx.op("tensor", mm(pj[pb][:], W[:, c, oc * 128:(oc + 1) * 128], hn[b][:, c, :], c == 0, c == 7),
                                 waits=[e_hn, e_w, pj_free[pb] if c == 0 else None], sig=(c == 7))
                dst = qo[b][:, oc, :] if isq else ko[b][:, oc - nqc, :]
                if not isA:
                    e_o = cx.op("scalar", act(dst, pj[pb][:], AF.Copy, scale=0.125 if isq else 1.0),
                                waits=[e_mm, out_free[b]], sig=True)
                    pj_free[pb] = e_o
                    evs_out.append(e_o)
                    continue
                k = oc % 2
                gcol = (GCOL_QG if isq else GCOL_KG) + j
                e_qg = cx.op("scalar", act(tmp["qg"][k][:], pj[pb][:], AF.Copy, scale=G[:, gcol:gcol + 1]),
                             waits=[e_mm, tfree["qg"][k]], sig=True)
                e_sq = cx.op("scalar", act(tmp["sq"][k][:], pj[pb][:], AF.Square),
                             waits=[e_mm, tfree["sq"][k]], sig=True)
                pj_free[pb] = e_sq
                e_t1 = cx.op("gpsimd", tt(tmp["t1"][k][:], tmp["qg"][k][:], cst[b][:, 0, :], ALU.mult),
                             waits=[e_qg, ld_ev[i], tfree["t1"][k]], sig=True)

                def fp32_part(oc=oc, k=k, e_qg=e_qg, e_sq=e_sq, e_t1=e_t1):
                    nonlocal ssc_free
                    e_ss = cx.op("tensor", mm(ssc[0:32, :], BDC[:, oc, :], tmp["sq"][k][:], oc == 0, oc == noc - 1),
                                 waits=[e_sq, e_w, ssc_free if oc == 0 else None], sig=True)
                    tfree["sq"][k] = e_ss
                    e_rot = cx.op("tensor", mm(aux[k][:], Rm, tmp["qg"][k][:], True, True),
                                  waits=[e_qg, aux_free[k]], sig=True)
                    tfree["qg"][k] = (e_rot, e_t1)
                    e_t2_ = cx.op("vector", tt(tmp["t2"][k][:], aux[k][:], cst[b][:, 1, :], ALU.mult),
                                  waits=[e_rot, ld_ev[i], tfree["t2"][k]], sig=True)
                    aux_free[k] = e_t2_
                    e_t3 = cx.op("gpsimd", tt(T3[:, oc, :], tmp["t1"][k][:], tmp["t2"][k][:], ALU.add),
                                 waits=[e_t1, e_t2_, t3_free[oc]], sig=True)
                    tfree["t1"][k] = e_t3
                    tfree["t2"][k] = e_t3
                    return e_ss, e_t3

                deferred.append(fp32_part)
                if len(deferred) > 1:
                    e_ss_last, e_t3_last = deferred.pop(0)()
            if isA:
                while deferred:
                    e_ss_last, e_t3_last = deferred.pop(0)()
            if i + 1 < NT:
                e_hn_next = norm(i + 1, ld_ev[i + 1])
            for tb in range(4):
                pb = cnt[0] % NPJ
                cnt[0] += 1
                for c in range(8):
                    e_mm = cx.op("tensor", mm(pj[pb][:, 0:NV], hn[b][:, c, tb * 128:(tb + 1) * 128], W[:, c, voff:voff + NV],
                                              c == 0, c == 7),
                                 waits=[e_hn, e_w, pj_free[pb] if c == 0 else None], sig=(c == 7))
                e_v = cx.op("vector", cpy(vo[b][:, tb, :, 0:64], pj[pb][:, 0:NV].rearrange("p (k d) -> p k d", d=64)),
                            waits=[e_mm, out_free[b], e_ms], sig=True)
                pj_free[pb] = e_v
                evs_out.append(e_v)
            hn_free[b] = e_mm
            if isA:
                e_sd = cx.op("scalar", act(sdc[:], ssc[0:32, :], AF.Sqrt, scale=G[0:32, GCOL_SCL:GCOL_SCL + 1],
                                           bias=G[0:32, GCOL_BIA:GCOL_BIA + 1]),
                             waits=[e_ss_last, sdc_free], sig=True)
                ssc_free = e_sd
                e_rs = cx.op("vector", recip(rsc[:], sdc[:]), waits=[e_sd, rsc_free], sig=True)
                sdc_free = e_rs
                e_bc = None
                for oc in range(noc):
                    k = oc % 2
                    isq = oc < nqc
                    dst = qo[b][:, oc, :] if isq else ko[b][:, oc - nqc, :]
                    e_bc = cx.op("tensor", mm(aux[k][:], SelC[:, oc, :], rsc[:], True, True),
                                 waits=[e_rs, e_w, aux_free[k]], sig=True)
                    e_o = cx.op("vector", tt(dst, aux[k][:], T3[:, oc, :], ALU.mult),
                                waits=[e_bc, e_t3_last, out_free[b]], sig=True)
                    aux_free[k] = e_o
                    t3_free[oc] = e_o
                    evs_out.append(e_o)
                rsc_free = e_bc
                h_free[b] = (e_hn, e_t3_last)
            cx.op("sync", dma(QTv[:, 0:nqc, t0:t0 + T], qo[b][:]), waits=evs_out, dsem=sts[b])
            cx.op("sync", dma(KTv[:, 0:nkc, t0:t0 + T], ko[b][:]), dsem=sts[b])
            out_free[b] = cx.op("sync", dma(Vdv[:, 4 * i:4 * i + 4, :], vo[b][:].rearrange("p a k d -> p a (k d)")),
                                dsem=sts[b])
            if i + 1 < NT:
                e_hn = e_hn_next
        for s_ in sts:
            cx.op("sync", lambda e, s_=s_: e.wait_ge(s_.h, s_.n))
        cx.flush()


def stage_oproj(cx, l, kind, h_in, h_out, OTf, wod, Seld, M1d):
    nc = cx.nc
    T = 512
    NT = S // T
    nh, nkv = CFG[kind]
    nch = nh // 2
    nr = 6 if kind == 1 else nh
    hin_v = h_in.rearrange("(c p) t -> p c t", p=128)
    hout_v = h_out.rearrange("(c p) t -> p c t", p=128)
    OTv = OTf.rearrange("(c two) r t -> two r c t", two=2)
    with ExitStack() as es:
        sb = lambda name, shape, dt: es.enter_context(nc.sbuf_tensor("O%d_" % l + name, shape, dt))
        ps = lambda name: es.enter_context(nc.psum_tensor("O%d_" % l + name, [128, 512], F32))
        W = sb("w", [128, nch, D], BF16)
        Sel = sb("sel", [nr, nch * 128], F32)
        ht = [sb("h%d" % i, [128, 8, T], F32) for i in range(2)]
        Ut = [sb("u%d" % i, [128, nch, T], F32) for i in range(2)]
        Dt = [sb("d%d" % i, [nh, T], F32) for i in range(2)]
        Rt = [sb("r%d" % i, [nr, T], F32) for i in range(2)]
        on = sb("on", [128, nch, T], BF16)
        bc = [ps("bc%d" % i) for i in range(2)]
        opp = [ps("op%d" % i) for i in range(3)]
        lds = [cx.new_sem("o%d_ld%d" % (l, i)) for i in range(2)]
        sts = [cx.new_sem("o%d_st%d" % (l, i)) for i in range(2)]
        ws = cx.new_sem("o%d_w" % l)
        load_weights_cast(cx, W, wod, ws, [(0, nch, 0, D)])
        e_w = cx.op("sync", dma(Sel[:], Seld), dsem=ws)
        if kind == 1:
            M1 = sb("m1", [nh, nr], F32)
            dsp = ps("dsp")
            e_w = cx.op("sync", dma(M1[:], M1d), dsem=ws)
        h_free = [None, None]
        u_free = [None, None]
        d_free = [None, None]
        r_free = [None, None]
        bc_free = [None, None]
        dsp_free = None
        op_free = [None] * 3
        on_free = None
        cnt = 0

        def load(i):
            b = i % 2
            t0 = i * T
            cx.op("sync", dma(ht[b][:], hin_v[:, :, t0:t0 + T]), waits=[h_free[b]], dsem=lds[b])
            cx.op("sync", dma(Ut[b][0:64, :, :], OTv[0, 0:64, 0:nch, t0:t0 + T]), waits=[u_free[b]], dsem=lds[b])
            cx.op("sync", dma(Ut[b][64:128, :, :], OTv[1, 0:64, 0:nch, t0:t0 + T]), dsem=lds[b])
            return cx.op("sync", dma(Dt[b][:], OTf[0:nh, 64, t0:t0 + T]), waits=[d_free[b]], dsem=lds[b])

        ld_ev = {0: load(0)}
        for i in range(NT):
            b = i % 2
            t0 = i * T
            if i + 1 < NT:
                ld_ev[i + 1] = load(i + 1)
            if kind == 1:
                e_ds = cx.op("tensor", mm(dsp[0:nr, :], M1[:], Dt[b][:], True, True),
                             waits=[ld_ev[i], e_w, dsp_free], sig=True)
                e_r = cx.op("vector", recip(Rt[b][:], dsp[0:nr, :]), waits=[e_ds, r_free[b]], sig=True)
                dsp_free = e_r
                d_free[b] = e_ds
            else:
                e_r = cx.op("vector", recip(Rt[b][:], Dt[b][:]), waits=[ld_ev[i], r_free[b]], sig=True)
                d_free[b] = e_r
            e_on = None
            e_bc = None
            for c in range(nch):
                k = c % 2
                e_bc = cx.op("tensor", mm(bc[k][:], Sel[:, c * 128:(c + 1) * 128], Rt[b][:], True, True),
                             waits=[e_r, e_w, bc_free[k]], sig=True)
                e_on = cx.op("vector", tt(on[:, c, :], bc[k][:], Ut[b][:, c, :], ALU.mult),
                             waits=[e_bc, ld_ev[i], on_free], sig=True)
                bc_free[k] = e_on
            u_free[b] = e_on
            r_free[b] = e_bc
            e_add = None
            e_mm = None
            for n in range(8):
                pb = cnt % 3
                cnt += 1
                for c in range(nch):
                    e_mm = cx.op("tensor", mm(opp[pb][:], W[:, c, n * 128:(n + 1) * 128], on[:, c, :], c == 0, c == nch - 1),
                                 waits=[e_on, e_w, op_free[pb] if c == 0 else None], sig=(c == nch - 1))
                e_add = cx.op("vector", tt(ht[b][:, n, :], opp[pb][:], ht[b][:, n, :], ALU.add), waits=[e_mm, ld_ev[i]], sig=True)
                op_free[pb] = e_add
            on_free = e_mm
            h_free[b] = cx.op("sync", dma(hout_v[:, :, t0:t0 + T], ht[b][:]), waits=[e_add], dsem=sts[b])
        for s_ in sts:
            cx.op("sync", lambda e, s_=s_: e.wait_ge(s_.h, s_.n))
        cx.flush()


def stage_attn(cx, l, kind, QT, KT, Vd, OTf, tabd, sinkd):
    nc = cx.nc
    nh, nkv = CFG[kind]
    has_tab = kind != 0
    NUD = 2 if kind == 1 else 4
    with ExitStack() as es:
        sb = lambda name, shape, dt: es.enter_context(nc.sbuf_tensor("A%d_" % l + name, shape, dt))
        Sps = [es.enter_context(nc.psum_tensor("A%d_S%d" % (l, i), [128, 1024], F32)) for i in range(2)]
        ops = [es.enter_context(nc.psum_tensor("A%d_o%d" % (l, i), [128, 512], F32)) for i in range(4)]
        P = [sb("P%d" % i, [128, 1024], BF16) for i in range(3)]
        UD = [sb("UD%d" % i, [65, S], F32) for i in range(NUD)]
        kls = cx.new_sem("a%d_k" % l)
        qls = [cx.new_sem("a%d_q%d" % (l, i)) for i in range(2)]
        sts = [cx.new_sem("a%d_st%d" % (l, i)) for i in range(NUD)]
        units = []
        e_tab = None
        if has_tab:
            TW = 384 if kind == 2 else 256
            SP = [sb("SP%d" % i, [128, 1024], F32) for i in range(3)]
            tab = sb("tab", [128, nh, TW], F32)
            Z = sb("Z", [128, 512], BF16)
            e_z = cx.op("gpsimd", mset(Z[:], 0.0), sig=True)
            e_tab = cx.op("sync", dma(tab[:], tabd), dsem=kls)
        e_sk = None
        if kind == 2:
            esk = sb("esk", [65, 16], F32)
            cx.op("sync", dma(esk[64:65, :], sinkd), dsem=kls)

        if kind in (0, 2):
            Kd = sb("K", [128, nkv, S], BF16)
            Vs = sb("V", [128, 32, nkv, 65], BF16)
            Qc = [sb("Qc%d" % i, [128, S], BF16) for i in range(2)]
            KTv = KT.rearrange("(c p) t -> p c t", p=128)
            for g in range(nkv):
                cx.op("sync", dma(Kd[:, g, :], KTv[:, g, :]), dsem=kls)
            Vdv = Vd.rearrange("(b p) f -> p b f", p=128)
            e_kv = None
            for q4 in range(4):
                e_kv = cx.op("sync", dma(Vs[:, 8 * q4:8 * q4 + 8, :, :].rearrange("p b k d -> p b (k d)"),
                                         Vdv[:, 8 * q4:8 * q4 + 8, :]), dsem=kls)
            if kind == 2:
                e_sk = cx.op("scalar", act(esk[64:65, :], esk[64:65, :], AF.Exp), waits=[e_kv], sig=True)
            q_free = [None, None]
            for c in range(nh // 2):
                g = c // 2
                for qt in range(8):
                    qs = qt * 512
                    sl = []
                    if kind == 0:
                        kbs = [(kb, qs, qs + 512, 0) for kb in range(32)]
                    else:
                        kbs = []
                        for kb in range(max(0, 4 * qt - 1), min(32, 4 * qt + 5)):
                            lo, hi = max(qs, 128 * kb - 128), min(qs + 512, 128 * kb + 256)
                            kbs.append((kb, lo, hi, lo - (128 * kb - 128)))
                    for (kb, lo, hi, off) in kbs:
                        subs = []
                        for hh in range(2):
                            subs.append(dict(k=Kd[hh * 64:(hh + 1) * 64, g, kb * 128:(kb + 1) * 128],
                                             q=(c, hh, lo, hi), n=hi - lo, v=Vs[:, kb, g, :], c0=lo - qs,
                                             tab=tab[:, 2 * c + hh, off:off + hi - lo] if has_tab else None))
                        sl.append(dict(subs=subs, tab2=tab[:, 2 * c:2 * c + 2, off:off + hi - lo] if has_tab else None))
                    units.append(dict(hs=(2 * c, 2 * c + 1), ql=512, steps=sl, dst=("plain", qs), qchunk=c,
                                      hfirst=(qt == 0), hlast=(qt == 7)))
        else:
            Qg = sb("Qg", [128, 3, S], BF16)
            Kg = sb("Kg", [128, 2, S], BF16)
            Qp = sb("Qp", [128, 3, S], BF16)
            Kp = sb("Kp", [128, 2, S], BF16)
            Vs = sb("V", [128, 32, 2, 65], BF16)
            for g, (window, d) in enumerate(B_GROUPS):
                L = S // d
                ql = min(512, L)
                nb = L // 128
                src = Kp if d > 1 else Kg
                for ci in range(3):
                    ulist = []
                    for rho in range(d):
                        for qt in range(L // ql):
                            qs = qt * ql
                            sl = []
                            for kb in range(max(0, (qs - 64) // 128), min(nb, (qs + ql + 64 + 127) // 128)):
                                lo, hi = max(qs, 128 * kb - 64), min(qs + ql, 128 * kb + 192)
                                if hi <= lo:
                                    continue
                                off = lo - (128 * kb - 64)
                                subs = []
                                for hh in range(2):
                                    i6 = 2 * ci + hh
                                    kvl = i6 // 3
                                    subs.append(dict(
                                        k=src[hh * 64:(hh + 1) * 64, kvl, rho * L + kb * 128:rho * L + (kb + 1) * 128],
                                        q=(ci, hh, rho * L + lo, rho * L + hi), n=hi - lo,
                                        v=Vs[:, rho * nb + kb, kvl, :], c0=lo - qs,
                                        tab=tab[:, 6 * g + i6, off:off + hi - lo]))
                                h0 = 6 * g + 2 * ci
                                sl.append(dict(subs=subs, tab2=tab[:, h0:h0 + 2, off:off + hi - lo]))
                            ulist.append(dict(hs=(6 * g + 2 * ci, 6 * g + 2 * ci + 1), ql=ql, steps=sl,
                                              dst=("perm", d, rho, qs), group=g))
                    ulist[0]["hfirst"] = True
                    ulist[-1]["hlast"] = True
                    units.extend(ulist)

        flat = []
        for ui, u in enumerate(units):
            for si, st in enumerate(u["steps"]):
                flat.append((ui, si == 0, si == len(u["steps"]) - 1, st))
        NS = len(flat)
        e_qk, e_rd, e_exp, e_pv = {}, {}, {}, {}
        e_evac = {}
        ud_free = [None] * NUD
        state = dict(group=-1, qchunk=-1, ready=None, pcount=-1, last_pe=None, qbuf_last={})

        def prepare(ui):
            u = units[ui]
            if kind in (0, 2):
                c = u["qchunk"]
                if c != state["qchunk"]:
                    state["qchunk"] = c
                    ql_ = state["qbuf_last"]
                    for cc in (c, c + 1):
                        if cc not in ql_ and cc < nh // 2:
                            ql_[cc] = cx.op("sync", dma(Qc[cc % 2][:], QT[cc * 128:(cc + 1) * 128, :]),
                                            waits=[q_free[cc % 2]], dsem=qls[cc % 2])
                    state["ready"] = [ql_[c], e_kv, e_tab]
            else:
                g = u["group"]
                if g != state["group"]:
                    state["group"] = g
                    d = B_GROUPS[g][1]
                    L = S // d
                    nb = L // 128
                    wl = [state["last_pe"]]
                    for ci in range(3):
                        cx.op("sync", dma(Qg[:, ci, :], QT[(3 * g + ci) * 128:(3 * g + ci + 1) * 128, :]), waits=wl, dsem=qls[0])
                    for kvl in range(2):
                        cx.op("sync", dma(Kg[:, kvl, :], KT[(2 * g + kvl) * 128:(2 * g + kvl + 1) * 128, :]), dsem=qls[0])
                    Vv = Vd.rearrange("(b p r) f -> r p b f", p=128, r=d)
                    e_l = None
                    for rho in range(d):
                        e_l = cx.op("sync", dma(Vs[:, rho * nb:(rho + 1) * nb, :, :].rearrange("p b k d -> p b (k d)"),
                                                Vv[rho, :, :, 2 * g * 65:(2 * g + 2) * 65]), dsem=qls[0])
                    rdy = [e_l, e_tab]
                    if d > 1:
                        e_p = None
                        for ci in range(3):
                            e_p = cx.op("gpsimd", cpy(Qp[:, ci, :].rearrange("p (r m) -> p r m", r=d),
                                                      Qg[:, ci, :].rearrange("p (m r) -> p r m", r=d)),
                                        waits=[e_l, state["last_pe"]], sig=True)
                        for kvl in range(2):
                            e_p = cx.op("gpsimd", cpy(Kp[:, kvl, :].rearrange("p (r m) -> p r m", r=d),
                                                      Kg[:, kvl, :].rearrange("p (m r) -> p r m", r=d)),
                                        waits=[e_l], sig=True)
                        rdy.append(e_p)
                    state["ready"] = rdy

        def q_ap(q):
            ci, hh, a, b_ = q
            if kind in (0, 2):
                return Qc[ci % 2][hh * 64:(hh + 1) * 64, a:b_]
            src_ = Qp if B_GROUPS[state["group"]][1] > 1 else Qg
            return src_[hh * 64:(hh + 1) * 64, ci, a:b_]

        def emit_qk(s):
            ui, first, last, st = flat[s]
            if first:
                prepare(ui)
            sbi = s % 2
            ev = None
            for k, sub in enumerate(st["subs"]):
                n = sub["n"]
                ev = cx.op("tensor", mm(Sps[sbi][:, k * 512:k * 512 + n], sub["k"], q_ap(sub["q"]), True, True),
                           waits=state["ready"] + [e_rd.get(s - 2)], sig=(k == 1))
            e_qk[s] = ev
            state["last_pe"] = ev
            if kind in (0, 2):
                q_free[units[ui]["qchunk"] % 2] = ev

        def emit_sm(s):
            ui, first, last, st = flat[s]
            sbi, pi = s % 2, s % 3
            n = st["subs"][0]["n"]
            if has_tab:
                e_a = cx.op("vector", tt(SP[pi][:].rearrange("p (k n) -> p k n", k=2)[:, :, 0:n],
                                         Sps[sbi][:].rearrange("p (k n) -> p k n", k=2)[:, :, 0:n],
                                         st["tab2"], ALU.add),
                            waits=[e_qk[s], e_exp.get(s - 3), e_tab], sig=True)
                e_rd[s] = e_a
                e_exp[s] = cx.op("scalar", act(P[pi][:].rearrange("p (k n) -> p k n", k=2)[:, :, 0:n],
                                               SP[pi][:].rearrange("p (k n) -> p k n", k=2)[:, :, 0:n], AF.Exp),
                                 waits=[e_a, e_pv.get(s - 3)], sig=True)
            else:
                e_exp[s] = cx.op("scalar", act(P[pi][:], Sps[sbi][:], AF.Exp),
                                 waits=[e_qk[s], e_pv.get(s - 3)], sig=True)
                e_rd[s] = e_exp[s]

        def emit_pv(s):
            ui, first, last, st = flat[s]
            pi = s % 3
            u = units[ui]
            ev = None
            for k, sub in enumerate(st["subs"]):
                ob = 2 * (ui % 2) + k
                n, c0 = sub["n"], sub["c0"]
                if first and has_tab:
                    cx.op("tensor", mm(ops[ob][0:65, 0:u["ql"]], Z[:, 0:65], Z[:, 0:u["ql"]], True, False),
                          waits=[e_z, e_evac.get(ui - 2)])
                ev = cx.op("tensor", mm(ops[ob][0:65, c0:c0 + n], sub["v"], P[pi][:, k * 512:k * 512 + n],
                                        first and not has_tab, last),
                           waits=[e_exp[s], e_evac.get(ui - 2) if first else None], sig=(k == 1))
            e_pv[s] = ev
            state["last_pe"] = ev
            if last:
                if u.get("hfirst"):
                    state["pcount"] += 1
                ql = u["ql"]
                e_ev = None
                for k in range(2):
                    ob = 2 * (ui % 2) + k
                    hb = (2 * state["pcount"] + k) % NUD
                    if u["dst"][0] == "plain":
                        dst = UD[hb][0:65, u["dst"][1]:u["dst"][1] + ql]
                    else:
                        _, d, rho, qs = u["dst"]
                        dst = UD[hb][0:65, :].rearrange("p (m r) -> p r m", r=d)[:, rho, qs:qs + ql]
                    e_ev = cx.op("vector", cpy(dst, ops[ob][0:65, 0:ql]),
                                 waits=[ev, ud_free[hb] if u.get("hfirst") else None], sig=True)
                e_evac[ui] = e_ev
                if u.get("hlast"):
                    for k in range(2):
                        hb = (2 * state["pcount"] + k) % NUD
                        h = u["hs"][k]
                        if kind == 2:
                            e_ev = cx.op("vector", ts(UD[hb][64:65, :], UD[hb][64:65, :], esk[64:65, h:h + 1], ALU.add),
                                         waits=[e_sk], sig=True)
                        ud_free[hb] = cx.op("sync", dma(OTf[h], UD[hb][:]), waits=[e_ev], dsem=sts[hb])

        def new_group(s):
            return kind == 1 and units[flat[s][0]]["group"] != units[flat[s - 1][0]]["group"]

        emit_qk(0)
        for s in range(NS):
            defer = s + 1 < NS and new_group(s + 1)
            if s + 1 < NS and not defer:
                emit_qk(s + 1)
            emit_sm(s)
            emit_pv(s)
            if defer:
                emit_qk(s + 1)
        for s_ in sts:
            cx.op("sync", lambda e, s_=s_: e.wait_ge(s_.h, s_.n))
        cx.flush()


def layer_dims(kind):
    nh, nkv = CFG[kind]
    return nh, nkv, nh * 64 + nkv * 128 + nkv * 64


def build(nlayers=DEPTH, debug=False):
    nc = bass.Bass("TRN2", target_bir_lowering=False)
    ein = lambda name, shape, dt=F32: nc.dram_tensor(name, shape, dt, kind="ExternalInput").ap()
    xT = ein("xT", [D, S])
    yT = nc.dram_tensor("yT", [D, S], F32, kind="ExternalOutput").ap()
    Gd = ein("G", [128, NG])
    onesd = ein("ones", [128, 128])
    BDCd = ein("BDC", [128, 12, 32])
    SelCd = ein("SelC", [32, 12, 128])
    Rd = ein("Rm", [128, 128])
    cosd = ein("cosT", [128, S])
    sind = ein("sinT", [128, S])
    tabBd = ein("tabB", [128, 18, 256])
    tabCd = ein("tabC", [128, 16, 384])
    selAd = ein("selA", [16, 8 * 128])
    selBd = ein("selB", [6, 9 * 128])
    m1Bd = ein("m1B", [18, 6])
    sinkd = ein("sinks", [1, 16])
    wq, wo, w1, w2 = [], [], [], []
    for l in range(nlayers):
        nh, nkv, NW = layer_dims(KINDS[l])
        wq.append(ein("wqkv%d" % l, [128, 8, NW]))
        wo.append(ein("wo%d" % l, [128, nh // 2, D]))
        w1.append(ein("w1_%d" % l, [128, 8, DFF]))
        w2.append(ein("w2_%d" % l, [128, 32, D]))
    sk = "ExternalOutput" if debug else "Internal"
    hA = nc.dram_tensor("hA", [D, S], F32, kind=sk).ap()
    hB = nc.dram_tensor("hB", [D, S], F32, kind=sk).ap()
    QT = nc.dram_tensor("QT", [1152, S], BF16, kind=sk).ap()
    KT = nc.dram_tensor("KT", [768, S], BF16, kind=sk).ap()
    VdA = nc.dram_tensor("VdA", [S, 4 * 65], BF16, kind=sk).ap()
    VdB = nc.dram_tensor("VdB", [S, 6 * 65], BF16, kind=sk).ap()
    OTf = nc.dram_tensor("OTf", [18, 65, S], F32, kind=sk).ap()
    with ExitStack() as es:
        cx = Ctx(nc, es)
        G = es.enter_context(nc.sbuf_tensor("Gsb", [128, NG], F32))
        ones = es.enter_context(nc.sbuf_tensor("onessb", [128, 128], F32))
        Rm = es.enter_context(nc.sbuf_tensor("Rsb", [128, 128], F32))
        cs = cx.new_sem("const")
        cx.op("sync", dma(G[:], Gd), dsem=cs)
        cx.op("sync", dma(ones[:], onesd), dsem=cs)
        cx.op("sync", dma(Rm[:], Rd), dsem=cs)
        for e in ["scalar", "vector", "tensor", "gpsimd"]:
            cx.op(e, lambda en: en.wait_ge(cs.h, cs.n))
        cx.flush()
        h_cur = xT
        used = [0, 0, 0]
        for l in range(nlayers):
            kind = KINDS[l]
            j = used[kind]
            used[kind] += 1
            Vd = VdB if kind == 1 else VdA
            stage_qkv(cx, l, kind, j, h_cur, wq[l], QT, KT, Vd, G[:], ones[:], BDCd, SelCd, Rm[:], cosd, sind)
            if debug == 3 and l == nlayers - 1:
                break
            stage_attn(cx, l, kind, QT, KT, Vd, OTf, tabBd if kind == 1 else tabCd, sinkd)
            if debug == 1 and l == nlayers - 1:
                break
            stage_oproj(cx, l, kind, h_cur, hA, OTf, wo[l], selBd if kind == 1 else selAd, m1Bd)
            last = l == nlayers - 1
            if debug and last:
                break
            stage_mlp(cx, l, hA, yT if last else hB, w1[l], w2[l], G[:], ones[:], last)
            h_cur = hB
    return nc


def alibi_slopes_np(n):
    return (2.0 ** (-8.0 * np.arange(1, n + 1, dtype=np.float32) / n)).astype(np.float32)


def host_consts():
    c = {}
    c["ones"] = np.ones((128, 128), np.float32)
    bdc = np.zeros((128, 12, 32), np.float32)
    selc = np.zeros((32, 12, 128), np.float32)
    for oc in range(12):
        for p in range(128):
            bdc[p, oc, 2 * oc + p // 64] = 1
            selc[2 * oc + p // 64, oc, p] = 1
    c["BDC"] = bdc
    c["SelC"] = selc
    R = np.zeros((128, 128), np.float32)
    for m in range(128):
        jj = (m % 64) % 32
        if jj < 16:
            R[m + 16, m] = -1.0
        else:
            R[m - 16, m] = 1.0
    c["Rm"] = R
    t = np.arange(S)
    row = (t // GRID_W).astype(np.float32)
    col = (t % GRID_W).astype(np.float32)
    inv = (np.float32(ROPE_THETA) ** (-np.arange(0, 32, 2, dtype=np.float32) / np.float32(32))).astype(np.float32)
    cosT = np.zeros((128, S), np.float32)
    sinT = np.zeros((128, S), np.float32)
    for p in range(128):
        dd = p % 64
        pos = row if dd < 32 else col
        ang = (pos * inv[(dd % 32) % 16]).astype(np.float32)
        cosT[p] = np.cos(ang.astype(np.float64)).astype(np.float32)
        sinT[p] = np.sin(ang.astype(np.float64)).astype(np.float32)
    c["cosT"], c["sinT"] = cosT, sinT
    k = np.arange(128)[:, None]
    slB = alibi_slopes_np(18)
    tabB = np.zeros((128, 18, 256), np.float32)
    jB = np.arange(256)[None, :]
    relB = np.abs(k + 64 - jB)
    for h in range(18):
        d = B_GROUPS[h // 6][1]
        tabB[:, h, :] = np.where(relB <= 64, -slB[h] * (relB * d).astype(np.float32), NEG)
    c["tabB"] = tabB
    slC = alibi_slopes_np(16)
    tabC = np.zeros((128, 16, 384), np.float32)
    jC = np.arange(384)[None, :]
    relC = np.abs(k + 128 - jC)
    for h in range(16):
        tabC[:, h, :] = np.where(relC <= 128, -slC[h] * relC.astype(np.float32), NEG)
    c["tabC"] = tabC
    selA = np.zeros((16, 8 * 128), np.float32)
    for col_ in range(8 * 128):
        selA[2 * (col_ // 128) + (col_ % 128) // 64, col_] = 1
    c["selA"] = selA
    selB = np.zeros((6, 9 * 128), np.float32)
    for col_ in range(9 * 128):
        hd = 2 * (col_ // 128) + (col_ % 128) // 64
        selB[hd % 6, col_] = 1
    c["selB"] = selB
    m1 = np.zeros((18, 6), np.float32)
    for h in range(18):
        m1[h, h % 6] = 1
    c["m1B"] = m1
    return c


def arr_k(w, nchunk):
    return np.ascontiguousarray(w.reshape(nchunk, 128, w.shape[1]).transpose(1, 0, 2))


def host_prep(inp, nlayers=DEPTH):
    shared = host_consts()
    G = np.zeros((128, NG), np.float32)
    for l in range(DEPTH):
        G[:, gcol_attn(l):gcol_attn(l) + 8] = inp["attn_norm"][l].reshape(8, 128).T
        G[:, gcol_mlp(l):gcol_mlp(l) + 8] = inp["mlp_norm"][l].reshape(8, 128).T
    G[:, GCOL_FINAL:GCOL_FINAL + 8] = inp["final_norm"].reshape(8, 128).T
    for j in range(2):
        G[:, GCOL_QG + j] = np.tile(inp["a_q_gain"][j], 2)
        G[:, GCOL_KG + j] = np.tile(inp["a_k_gain"][j], 2)
    G[:, GCOL_EPS] = EPS
    G[:, GCOL_EPS64] = 64 * EPS
    G[:, GCOL_SCL] = 1.0 / 64
    G[:, GCOL_BIA] = EPS
    G[0:16, GCOL_SCL] = 1.0
    G[0:16, GCOL_BIA] = 64 * EPS
    shared["G"] = G
    shared["sinks"] = np.ascontiguousarray(inp["c_sinks"][0:1]).astype(np.float32)
    used = [0, 0, 0]
    for l in range(nlayers):
        kind = KINDS[l]
        j = used[kind]
        used[kind] += 1
        nh, nkv = CFG[kind]
        w = [inp["a_w_qkv"], inp["b_w_qkv"], inp["c_w_qkv"]][kind][j]
        wo = [inp["a_w_o"], inp["b_w_o"], inp["c_w_o"]][kind][j]
        nq = nh * 64
        q, k, v = w[:, :nq], w[:, nq:nq + nkv * 64], w[:, nq + nkv * 64:]
        kd = np.concatenate([np.concatenate([k[:, g * 64:(g + 1) * 64]] * 2, axis=1) for g in range(nkv)], axis=1)
        shared["wqkv%d" % l] = arr_k(np.concatenate([q, kd, v], axis=1), 8)
        shared["wo%d" % l] = arr_k(wo, nh // 2)
        shared["w1_%d" % l] = arr_k(inp["mlp_w1"][l], 8)
        shared["w2_%d" % l] = arr_k(inp["mlp_w2"][l], 32)
    return shared


_NC_CACHE = {}


def kernel(**inputs):
    inp = {k: np.asarray(v) for k, v in inputs.items()}
    shared = host_prep(inp)
    x = inp["x"].astype(np.float32)
    if DEPTH not in _NC_CACHE:
        _NC_CACHE[DEPTH] = build(DEPTH)
    nc = _NC_CACHE[DEPTH]
    in_maps = []
    for b in range(NCORES):
        m = dict(shared)
        m["xT"] = np.ascontiguousarray(x[b].T)
        in_maps.append(m)
    res = run_bass_kernel_spmd(nc, in_maps, core_ids=list(range(NCORES)))
    out = np.empty((NCORES, S, D), np.float32)
    for b in range(NCORES):
        out[b] = res.results[b]["yT"].T
    return out
```

```python
import numpy as np
import ml_dtypes
from contextlib import ExitStack
import concourse.bass as bass
import concourse.mybir as mybir
from concourse.bass_utils import run_bass_kernel_spmd

F32 = mybir.dt.float32
BF16 = mybir.dt.bfloat16
AF = mybir.ActivationFunctionType
ALU = mybir.AluOpType

S = 4096
D = 1024
DFF = 4096
HD = 64
NCORES = 8
DEPTH = 4
KINDS = [0, 1, 2, 0]
EPS = 1e-6
GRID_W = 64
ROPE_THETA = 10000.0
B_GROUPS = ((128, 1), (512, 4), (2048, 16))
NEG = -30000.0

CFG = {0: (16, 4), 1: (18, 6), 2: (16, 4)}

ENGS = ["sync", "scalar", "gpsimd", "vector", "tensor"]
STRICT = True


class Sem:
    def __init__(self, h):
        self.h = h
        self.n = 0


class Ctx:
    def __init__(self, nc, es):
        self.nc = nc
        self.es = es
        self.lists = {e: [] for e in ENGS}
        self.waited = {e: {} for e in ENGS}
        self.esem = {}
        self.allsems = []
        for e in ["scalar", "gpsimd", "vector", "tensor"]:
            self.esem[e] = self.new_sem("e_" + e)
        self.nsem = 0

    def new_sem(self, name):
        s = Sem(self.es.enter_context(self.nc.semaphore(name)))
        self.allsems.append(s)
        return s

    def op(self, eng, fn, waits=(), sig=False, dsem=None):
        ws = []
        wd = self.waited[eng]
        flat = []
        for ev in waits:
            if ev is None:
                continue
            if isinstance(ev[0], Sem):
                flat.append(ev)
            else:
                flat.extend(x for x in ev if x is not None)
        for ev in flat:
            sem, val = ev
            if eng in self.esem and sem is self.esem[eng] and not STRICT:
                continue
            if wd.get(id(sem), 0) >= val:
                continue
            wd[id(sem)] = val
            ws.append((sem, val))
        ev = None
        inc = 0
        if dsem is not None:
            dsem.n += 16
            inc = 16
            ev = (dsem, dsem.n)
        elif sig:
            s = self.esem[eng]
            s.n += 1
            inc = 1
            ev = (s, s.n)
        self.lists[eng].append((ws, fn, ev, inc))
        return ev

    def flush(self):
        nc = self.nc
        with nc.Block() as block:
            for eng in ENGS:
                lst = self.lists[eng]
                if not lst:
                    continue

                def body(e, lst=lst):
                    for ws, fn, ev, inc in lst:
                        for sem, val in ws:
                            e.wait_ge(sem.h, val)
                        ins = fn(e)
                        if ev is not None:
                            ins.then_inc(ev[0].h, inc)

                getattr(block, eng)(body)
        self.lists = {e: [] for e in ENGS}


def mm(out, lhsT, rhs, start, stop):
    return lambda e: e.matmul(out, lhsT=lhsT, rhs=rhs, start=start, stop=stop)


def act(out, in_, func, scale=1.0, bias=None):
    if bias is None:
        return lambda e: e.activation(out=out, in_=in_, func=func, scale=scale)
    return lambda e: e.activation(out=out, in_=in_, func=func, scale=scale, bias=bias)


def tt(out, in0, in1, op):
    return lambda e: e.tensor_tensor(out=out, in0=in0, in1=in1, op=op)


def stt(out, in0, scalar, in1, op0, op1):
    return lambda e: e.scalar_tensor_tensor(out=out, in0=in0, scalar=scalar, in1=in1, op0=op0, op1=op1)


def ts(out, in0, s1, op0, s2=None, op1=None):
    if op1 is None:
        return lambda e: e.tensor_scalar(out=out, in0=in0, scalar1=s1, scalar2=None, op0=op0)
    return lambda e: e.tensor_scalar(out=out, in0=in0, scalar1=s1, scalar2=s2, op0=op0, op1=op1)


def recip(out, in_):
    return lambda e: e.reciprocal(out=out, in_=in_)


def cpy(out, in_):
    return lambda e: e.tensor_copy(out=out, in_=in_)


def dma(out, in_):
    return lambda e: e.dma_start(out=out, in_=in_)


def mset(ap, v):
    return lambda e: e.memset(ap, v)


def gcol_attn(l):
    return l * 8


def gcol_mlp(l):
    return 32 + l * 8


GCOL_FINAL = 64
GCOL_QG = 72
GCOL_KG = 74
GCOL_EPS = 76
GCOL_EPS64 = 77
GCOL_SCL = 78
GCOL_BIA = 79
NG = 80


class RmsNorm:
    def __init__(self, cx, sb, ps, name, T, ones, G):
        self.cx, self.T, self.ones, self.G = cx, T, ones, G
        self.sq = [sb(name + "_sq%d" % i, [128, T], BF16) for i in range(2)]
        self.sd = sb(name + "_sd", [128, T], F32)
        self.rstd = sb(name + "_rstd", [128, T], F32)
        self.ss = ps(name + "_ss")
        self.sq_free = [None, None]
        self.ss_free = None
        self.sd_free = None
        self.users = None

    def emit(self, ht, h_ready):
        cx, T = self.cx, self.T
        ev_mm = None
        for c in range(8):
            b = c % 2
            e_sq = cx.op("scalar", act(self.sq[b][:], ht[:, c, :], AF.Square),
                         waits=[h_ready, self.sq_free[b]], sig=True)
            ev_mm = cx.op("tensor", mm(self.ss[:, 0:T], self.ones, self.sq[b][:], c == 0, c == 7),
                          waits=[e_sq, self.ss_free if c == 0 else None], sig=True)
            self.sq_free[b] = ev_mm
        e_sd = cx.op("scalar", act(self.sd[:], self.ss[:, 0:T], AF.Sqrt, scale=1.0 / D,
                                   bias=self.G[:, GCOL_EPS:GCOL_EPS + 1]),
                     waits=[ev_mm, self.sd_free], sig=True)
        self.ss_free = e_sd
        e_r = cx.op("vector", recip(self.rstd[:], self.sd[:]), waits=[e_sd, self.users], sig=True)
        self.sd_free = e_r
        return e_r


def load_weights_cast(cx, Wsb, Wdram, wsem, pieces):
    ev = None
    for (c0, c1, n0, n1) in pieces:
        ev = cx.op("gpsimd", dma(Wsb[:, c0:c1, n0:n1], Wdram[:, c0:c1, n0:n1]), dsem=wsem)
    return ev


def stage_mlp(cx, l, h_in, h_out, w1d, w2d, G, ones, last):
    nc = cx.nc
    T = 512
    NT = S // T
    HM = 16
    hin_v = h_in.rearrange("(c p) t -> p c t", p=128)
    hout_v = h_out.rearrange("(c p) t -> p c t", p=128)
    with ExitStack() as es:
        sb = lambda name, shape, dt: es.enter_context(nc.sbuf_tensor("L%d_" % l + name, shape, dt))
        ps = lambda name: es.enter_context(nc.psum_tensor("L%d_" % l + name, [128, 512], F32))
        W1 = sb("w1", [128, 8, DFF], BF16)
        W2 = sb("w2", [128, 32, D], BF16)
        ht = [sb("m_h%d" % i, [128, 8, T], F32) for i in range(2)]
        hn = sb("m_hn", [128, 8, T], BF16)
        u = sb("m_u", [128, HM, T], BF16)
        rl = [sb("m_rl%d" % i, [128, T], F32) for i in range(3)]
        nrm = RmsNorm(cx, sb, ps, "m_n", T, ones, G)
        if last:
            yb = [sb("m_y%d" % i, [128, T], F32) for i in range(2)]
            yb_free = [None, None]
        ups = [ps("m_ups%d" % i) for i in range(4)]
        ops_ = [ps("m_ops%d" % i) for i in range(3)]
        lds = [cx.new_sem("m%d_ld%d" % (l, i)) for i in range(2)]
        sts = [cx.new_sem("m%d_st%d" % (l, i)) for i in range(2)]
        w1s = cx.new_sem("m%d_w1" % l)
        w2s = cx.new_sem("m%d_w2" % l)
        NU, NO = len(ups), len(ops_)

        e_w1 = load_weights_cast(cx, W1, w1d, w1s, [(0, 8, 0, 2048), (0, 8, 2048, 4096)])
        e_w2 = load_weights_cast(cx, W2, w2d, w2s, [(8 * i, 8 * i + 8, 0, D) for i in range(4)])

        h_free = [None, None]
        hn_free = None
        u_free = None
        rl_free = [None, None, None]
        ups_free = [None] * NU
        ops_free = [None] * NO
        gm = gcol_mlp(l)
        cu = [0]
        co = [0]

        def load(i):
            b = i % 2
            return cx.op("sync", dma(ht[b][:], hin_v[:, :, i * T:(i + 1) * T]), waits=[h_free[b]], dsem=lds[b])

        def norm(i, h_ready):
            b = i % 2
            e_r = nrm.emit(ht[b], h_ready)
            e_hn = None
            for c in range(8):
                e_hn = cx.op("vector", stt(hn[:, c, :], ht[b][:, c, :], G[:, gm + c:gm + c + 1], nrm.rstd[:],
                                           ALU.mult, ALU.mult), waits=[e_r, hn_free], sig=True)
            nrm.users = e_hn
            return e_hn

        ld_ev = {0: load(0)}
        e_hn = norm(0, ld_ev[0])
        for i in range(NT):
            b = i % 2
            t0 = i * T
            if i + 1 < NT:
                ld_ev[i + 1] = load(i + 1)
            e_add = None
            for half in range(2):
                e_u = None
                e_mm = None
                for mm_ in range(HM):
                    m = half * HM + mm_
                    pb = cu[0] % NU
                    cu[0] += 1
                    for c in range(8):
                        e_mm = cx.op("tensor", mm(ups[pb][:], W1[:, c, m * 128:(m + 1) * 128], hn[:, c, :],
                                                  c == 0, c == 7),
                                     waits=[e_hn, e_w1, ups_free[pb] if c == 0 else None], sig=(c == 7))
                    rb = m % 3
                    e_rl = cx.op("scalar", act(rl[rb][:], ups[pb][:], AF.Relu), waits=[e_mm, rl_free[rb]], sig=True)
                    ups_free[pb] = e_rl
                    e_u = cx.op("gpsimd", tt(u[:, mm_, :], rl[rb][:], rl[rb][:], ALU.mult), waits=[e_rl, u_free], sig=True)
                    rl_free[rb] = e_u
                if half == 1:
                    hn_free = e_mm
                    if i + 1 < NT:
                        e_hn_next = norm(i + 1, ld_ev[i + 1])
                for n in range(8):
                    pb = co[0] % NO
                    co[0] += 1
                    for mm_ in range(HM):
                        m = half * HM + mm_
                        e_mm = cx.op("tensor", mm(ops_[pb][:], W2[:, m, n * 128:(n + 1) * 128], u[:, mm_, :],
                                                  mm_ == 0, mm_ == HM - 1),
                                     waits=[e_u, e_w2, ops_free[pb] if mm_ == 0 else None], sig=(mm_ == HM - 1))
                    e_add = cx.op("vector", tt(ht[b][:, n, :], ops_[pb][:], ht[b][:, n, :], ALU.add),
                                  waits=[e_mm, ld_ev[i]], sig=True)
                    ops_free[pb] = e_add
                u_free = e_mm
            if not last:
                h_free[b] = cx.op("sync", dma(hout_v[:, :, t0:t0 + T], ht[b][:]), waits=[e_add], dsem=sts[b])
            else:
                e_r2 = nrm.emit(ht[b], e_add)
                e_y = None
                for c in range(8):
                    yi = c % 2
                    e_y = cx.op("vector", stt(yb[yi][:], ht[b][:, c, :], G[:, GCOL_FINAL + c:GCOL_FINAL + c + 1],
                                              nrm.rstd[:], ALU.mult, ALU.mult), waits=[e_r2, yb_free[yi]], sig=True)
                    yb_free[yi] = cx.op("sync", dma(hout_v[:, c, t0:t0 + T], yb[yi][:]), waits=[e_y], dsem=sts[yi])
                nrm.users = e_y
                h_free[b] = e_y
            if i + 1 < NT:
                e_hn = e_hn_next
        for s_ in sts:
            if s_.n:
                cx.op("sync", lambda e, s_=s_: e.wait_ge(s_.h, s_.n))
        cx.flush()


def stage_qkv(cx, l, kind, j, h_in, wd, QT, KT, Vd, G, ones, BDCd, SelCd, Rm, cosd, sind):
    nc = cx.nc
    T = 512
    NT = S // T
    nh, nkv = CFG[kind]
    nqc, nkc, NV = nh // 2, nkv, nkv * 64
    noc = nqc + nkc
    NW = nh * 64 + nkv * 128 + NV
    voff = nh * 64 + nkv * 128
    hin_v = h_in.rearrange("(c p) t -> p c t", p=128)
    QTv = QT.rearrange("(c p) t -> p c t", p=128)
    KTv = KT.rearrange("(c p) t -> p c t", p=128)
    Vdv = Vd.rearrange("(tb p) f -> p tb f", p=128)
    isA = kind == 0
    with ExitStack() as es:
        sb = lambda name, shape, dt: es.enter_context(nc.sbuf_tensor("Q%d_" % l + name, shape, dt))
        ps = lambda name: es.enter_context(nc.psum_tensor("Q%d_" % l + name, [128, 512], F32))
        W = sb("w", [128, 8, NW], BF16)
        ht = [sb("h%d" % i, [128, 8, T], F32) for i in range(2)]
        hn = [sb("hn%d" % i, [128, 8, T], BF16) for i in range(2)]
        qo = [sb("qo%d" % i, [128, nqc, T], BF16) for i in range(2)]
        ko = [sb("ko%d" % i, [128, nkc, T], BF16) for i in range(2)]
        vo = [sb("vo%d" % i, [128, 4, nkv, 65], BF16) for i in range(2)]
        nrm = RmsNorm(cx, sb, ps, "n", T, ones, G)
        pj = [ps("pj%d" % i) for i in range(3 if not isA else 2)]
        NPJ = len(pj)
        lds = [cx.new_sem("q%d_ld%d" % (l, i)) for i in range(2)]
        sts = [cx.new_sem("q%d_st%d" % (l, i)) for i in range(2)]
        ws = cx.new_sem("q%d_w" % l)
        half = NW // 2
        e_w = load_weights_cast(cx, W, wd, ws, [(0, 4, 0, NW), (4, 8, 0, NW)] if NW <= 2048 else
                                [(0, 8, 0, half), (0, 8, half, NW)])
        e_ms = None
        for b in range(2):
            e_ms = cx.op("gpsimd", mset(vo[b][:], 1.0), sig=True)
        if isA:
            cst = [sb("cs%d" % i, [128, 2, T], F32) for i in range(2)]
            tmpn = ["qg", "sq", "t1", "t2"]
            tmp = {n: [sb(n + "%d" % i, [128, T], BF16 if n == "sq" else F32) for i in range(2)] for n in tmpn}
            tfree = {n: [None, None] for n in tmpn}
            T3 = sb("T3", [128, noc, T], F32)
            t3_free = [None] * noc
            BDC = sb("BDC", [128, noc, 32], BF16)
            SelC = sb("SelC", [32, noc, 128], F32)
            sdc = sb("sdc", [32, T], F32)
            rsc = sb("rsc", [32, T], F32)
            aux = [ps("aux%d" % i) for i in range(2)]
            aux_free = [None, None]
            ssc = ps("ssc")
            ssc_free = None
            sdc_free = None
            rsc_free = None
            cx.op("sync", dma(BDC[:], BDCd), dsem=ws)
            e_w = cx.op("sync", dma(SelC[:], SelCd), dsem=ws)

        h_free = [None, None]
        out_free = [None, None]
        hn_free = [None, None]
        pj_free = [None] * NPJ
        gm = gcol_attn(l)
        cnt = [0]

        def load(i):
            b = i % 2
            ev = cx.op("sync", dma(ht[b][:], hin_v[:, :, i * T:(i + 1) * T]), waits=[h_free[b]], dsem=lds[b])
            if isA:
                cx.op("sync", dma(cst[b][:, 0, :], cosd[:, i * T:(i + 1) * T]), dsem=lds[b])
                ev = cx.op("sync", dma(cst[b][:, 1, :], sind[:, i * T:(i + 1) * T]), dsem=lds[b])
            return ev

        def norm(i, h_ready):
            b = i % 2
            e_r = nrm.emit(ht[b], h_ready)
            e_hn = None
            for c in range(8):
                e_hn = cx.op("vector", stt(hn[b][:, c, :], ht[b][:, c, :], G[:, gm + c:gm + c + 1], nrm.rstd[:],
                                           ALU.mult, ALU.mult), waits=[e_r, hn_free[b]], sig=True)
            nrm.users = e_hn
            if not isA:
                h_free[b] = e_hn
            return e_hn

        ld_ev = {0: load(0)}
        e_hn = norm(0, ld_ev[0])
        for i in range(NT):
            b = i % 2
            t0 = i * T
            if i + 1 < NT:
                ld_ev[i + 1] = load(i + 1)
            evs_out = []
            deferred = []
            e_mm = None
            e_t2 = None
            for oc in range(noc):
                isq = oc < nqc
                pb = cnt[0] % NPJ
                cnt[0] += 1
                for c in range(8):
                    e_mm = cx.op("tensor", mm(pj[pb][:], W[:, c, oc * 128:(oc + 1) * 128], hn[b][:, c, :], c == 0, c == 7),
                                 waits=[e_hn, e_w, pj_free[pb] if c == 0 else None], sig=(c == 7))
                dst = qo[b][:, oc, :] if isq else ko[b][:, oc - nqc, :]
                if not isA:
                    e_o = cx.op("scalar", act(dst, pj[pb][:], AF.Copy, scale=0.125 if isq else 1.0),
                                waits=[e_mm, out_free[b]], sig=True)
                    pj_free[pb] = e_o
                    evs_out.append(e_o)
                    continue
                k = oc % 2
                gcol = (GCOL_QG if isq else GCOL_KG) + j
                e_qg = cx.op("scalar", act(tmp["qg"][k][:], pj[pb][:], AF.Copy, scale=G[:, gcol:gcol + 1]),
                             waits=[e_mm, tfree["qg"][k]], sig=True)
                e_sq = cx.op("scalar", act(tmp["sq"][k][:], pj[pb][:], AF.Square),
                             waits=[e_mm, tfree["sq"][k]], sig=True)
                pj_free[pb] = e_sq
                e_t1 = cx.op("gpsimd", tt(tmp["t1"][k][:], tmp["qg"][k][:], cst[b][:, 0, :], ALU.mult),
                             waits=[e_qg, ld_ev[i], tfree["t1"][k]], sig=True)

                def fp32_part(oc=oc, k=k, e_qg=e_qg, e_sq=e_sq, e_t1=e_t1):
                    nonlocal ssc_free
                    e_ss = cx.op("tensor", mm(ssc[0:32, :], BDC[:, oc, :], tmp["sq"][k][:], oc == 0, oc == noc - 1),
                                 waits=[e_sq, e_w, ssc_free if oc == 0 else None], sig=True)
                    tfree["sq"][k] = e_ss
                    e_rot = cx.op("tensor", mm(aux[k][:], Rm, tmp["qg"][k][:], True, True),
                                  waits=[e_qg, aux_free[k]], sig=True)
                    tfree["qg"][k] = (e_rot, e_t1)
                    e_t2_ = cx.op("vector", tt(tmp["t2"][k][:], aux[k][:], cst[b][:, 1, :], ALU.mult),
                                  waits=[e_rot, ld_ev[i], tfree["t2"][k]], sig=True)
                    aux_free[k] = e_t2_
                    e_t3 = cx.op("gpsimd", tt(T3[:, oc, :], tmp["t1"][k][:], tmp["t2"][k][:], ALU.add),
                                 waits=[e_t1, e_t2_, t3_free[oc]], sig=True)
                    tfree["t1"][k] = e_t3
                    tfree["t2"][k] = e_t3
                    return e_ss, e_t3

                deferred.append(fp32_part)
                if len(deferred) > 1:
                    e_ss_last, e_t3_last = deferred.pop(0)()
            if isA:
                while deferred:
                    e_ss_last, e_t3_last = deferred.pop(0)()
            if i + 1 < NT:
                e_hn_next = norm(i + 1, ld_ev[i + 1])
            for tb in range(4):
                pb = cnt[0] % NPJ
                cnt[0] += 1
                for c in range(8):
                    e_mm = cx.op("tensor", mm(pj[pb][:, 0:NV], hn[b][:, c, tb * 128:(tb + 1) * 128], W[:, c, voff:voff + NV],
                                              c == 0, c == 7),
                                 waits=[e_hn, e_w, pj_free[pb] if c == 0 else None], sig=(c == 7))
                e_v = cx.op("vector", cpy(vo[b][:, tb, :, 0:64], pj[pb][:, 0:NV].rearrange("p (k d) -> p k d", d=64)),
                            waits=[e_mm, out_free[b], e_ms], sig=True)
                pj_free[pb] = e_v
                evs_out.append(e_v)
            hn_free[b] = e_mm
            if isA:
                e_sd = cx.op("scalar", act(sdc[:], ssc[0:32, :], AF.Sqrt, scale=G[0:32, GCOL_SCL:GCOL_SCL + 1],
                                           bias=G[0:32, GCOL_BIA:GCOL_BIA + 1]),
                             waits=[e_ss_last, sdc_free], sig=True)
                ssc_free = e_sd
                e_rs = cx.op("vector", recip(rsc[:], sdc[:]), waits=[e_sd, rsc_free], sig=True)
                sdc_free = e_rs
                e_bc = None
                for oc in range(noc):
                    k = oc % 2
                    isq = oc < nqc
                    dst = qo[b][:, oc, :] if isq else ko[b][:, oc - nqc, :]
                    e_bc = cx.op("tensor", mm(aux[k][:], SelC[:, oc, :], rsc[:], True, True),
                                 waits=[e_rs, e_w, aux_free[k]], sig=True)
                    e_o = cx.op("vector", tt(dst, aux[k][:], T3[:, oc, :], ALU.mult),
                                waits=[e_bc, e_t3_last, out_free[b]], sig=True)
                    aux_free[k] = e_o
                    t3_free[oc] = e_o
                    evs_out.append(e_o)
                rsc_free = e_bc
                h_free[b] = (e_hn, e_t3_last)
            cx.op("sync", dma(QTv[:, 0:nqc, t0:t0 + T], qo[b][:]), waits=evs_out, dsem=sts[b])
            cx.op("sync", dma(KTv[:, 0:nkc, t0:t0 + T], ko[b][:]), dsem=sts[b])
            out_free[b] = cx.op("sync", dma(Vdv[:, 4 * i:4 * i + 4, :], vo[b][:].rearrange("p a k d -> p a (k d)")),
                                dsem=sts[b])
            if i + 1 < NT:
                e_hn = e_hn_next
        for s_ in sts:
            cx.op("sync", lambda e, s_=s_: e.wait_ge(s_.h, s_.n))
        cx.flush()


def stage_oproj(cx, l, kind, h_in, h_out, OTf, wod, Seld, M1d):
    nc = cx.nc
    T = 512
    NT = S // T
    nh, nkv = CFG[kind]
    nch = nh // 2
    nr = 6 if kind == 1 else nh
    hin_v = h_in.rearrange("(c p) t -> p c t", p=128)
    hout_v = h_out.rearrange("(c p) t -> p c t", p=128)
    OTv = OTf.rearrange("(c two) r t -> two r c t", two=2)
    with ExitStack() as es:
        sb = lambda name, shape, dt: es.enter_context(nc.sbuf_tensor("O%d_" % l + name, shape, dt))
        ps = lambda name: es.enter_context(nc.psum_tensor("O%d_" % l + name, [128, 512], F32))
        W = sb("w", [128, nch, D], BF16)
        Sel = sb("sel", [nr, nch * 128], F32)
        ht = [sb("h%d" % i, [128, 8, T], F32) for i in range(2)]
        Ut = [sb("u%d" % i, [128, nch, T], F32) for i in range(2)]
        Dt = [sb("d%d" % i, [nh, T], F32) for i in range(2)]
        Rt = [sb("r%d" % i, [nr, T], F32) for i in range(2)]
        on = sb("on", [128, nch, T], BF16)
        bc = [ps("bc%d" % i) for i in range(2)]
        opp = [ps("op%d" % i) for i in range(3)]
        lds = [cx.new_sem("o%d_ld%d" % (l, i)) for i in range(2)]
        sts = [cx.new_sem("o%d_st%d" % (l, i)) for i in range(2)]
        ws = cx.new_sem("o%d_w" % l)
        load_weights_cast(cx, W, wod, ws, [(0, nch, 0, D)])
        e_w = cx.op("sync", dma(Sel[:], Seld), dsem=ws)
        if kind == 1:
            M1 = sb("m1", [nh, nr], F32)
            dsp = ps("dsp")
            e_w = cx.op("sync", dma(M1[:], M1d), dsem=ws)
        h_free = [None, None]
        u_free = [None, None]
        d_free = [None, None]
        r_free = [None, None]
        bc_free = [None, None]
        dsp_free = None
        op_free = [None] * 3
        on_free = None
        cnt = 0

        def load(i):
            b = i % 2
            t0 = i * T
            cx.op("sync", dma(ht[b][:], hin_v[:, :, t0:t0 + T]), waits=[h_free[b]], dsem=lds[b])
            cx.op("sync", dma(Ut[b][0:64, :, :], OTv[0, 0:64, 0:nch, t0:t0 + T]), waits=[u_free[b]], dsem=lds[b])
            cx.op("sync", dma(Ut[b][64:128, :, :], OTv[1, 0:64, 0:nch, t0:t0 + T]), dsem=lds[b])
            return cx.op("sync", dma(Dt[b][:], OTf[0:nh, 64, t0:t0 + T]), waits=[d_free[b]], dsem=lds[b])

        ld_ev = {0: load(0)}
        for i in range(NT):
            b = i % 2
            t0 = i * T
            if i + 1 < NT:
                ld_ev[i + 1] = load(i + 1)
            if kind == 1:
                e_ds = cx.op("tensor", mm(dsp[0:nr, :], M1[:], Dt[b][:], True, True),
                             waits=[ld_ev[i], e_w, dsp_free], sig=True)
                e_r = cx.op("vector", recip(Rt[b][:], dsp[0:nr, :]), waits=[e_ds, r_free[b]], sig=True)
                dsp_free = e_r
                d_free[b] = e_ds
            else:
                e_r = cx.op("vector", recip(Rt[b][:], Dt[b][:]), waits=[ld_ev[i], r_free[b]], sig=True)
                d_free[b] = e_r
            e_on = None
            e_bc = None
            for c in range(nch):
                k = c % 2
                e_bc = cx.op("tensor", mm(bc[k][:], Sel[:, c * 128:(c + 1) * 128], Rt[b][:], True, True),
                             waits=[e_r, e_w, bc_free[k]], sig=True)
                e_on = cx.op("vector", tt(on[:, c, :], bc[k][:], Ut[b][:, c, :], ALU.mult),
                             waits=[e_bc, ld_ev[i], on_free], sig=True)
                bc_free[k] = e_on
            u_free[b] = e_on
            r_free[b] = e_bc
            e_add = None
            e_mm = None
            for n in range(8):
                pb = cnt % 3
                cnt += 1
                for c in range(nch):
                    e_mm = cx.op("tensor", mm(opp[pb][:], W[:, c, n * 128:(n + 1) * 128], on[:, c, :], c == 0, c == nch - 1),
                                 waits=[e_on, e_w, op_free[pb] if c == 0 else None], sig=(c == nch - 1))
                e_add = cx.op("vector", tt(ht[b][:, n, :], opp[pb][:], ht[b][:, n, :], ALU.add), waits=[e_mm, ld_ev[i]], sig=True)
                op_free[pb] = e_add
            on_free = e_mm
            h_free[b] = cx.op("sync", dma(hout_v[:, :, t0:t0 + T], ht[b][:]), waits=[e_add], dsem=sts[b])
        for s_ in sts:
            cx.op("sync", lambda e, s_=s_: e.wait_ge(s_.h, s_.n))
        cx.flush()


def stage_attn(cx, l, kind, QT, KT, Vd, OTf, tabd, sinkd):
    nc = cx.nc
    nh, nkv = CFG[kind]
    has_tab = kind != 0
    NUD = 2 if kind == 1 else 4
    with ExitStack() as es:
        sb = lambda name, shape, dt: es.enter_context(nc.sbuf_tensor("A%d_" % l + name, shape, dt))
        Sps = [es.enter_context(nc.psum_tensor("A%d_S%d" % (l, i), [128, 1024], F32)) for i in range(2)]
        ops = [es.enter_context(nc.psum_tensor("A%d_o%d" % (l, i), [128, 512], F32)) for i in range(4)]
        P = [sb("P%d" % i, [128, 1024], BF16) for i in range(3)]
        UD = [sb("UD%d" % i, [65, S], F32) for i in range(NUD)]
        kls = cx.new_sem("a%d_k" % l)
        qls = [cx.new_sem("a%d_q%d" % (l, i)) for i in range(2)]
        sts = [cx.new_sem("a%d_st%d" % (l, i)) for i in range(NUD)]
        units = []
        e_tab = None
        if has_tab:
            TW = 384 if kind == 2 else 256
            SP = [sb("SP%d" % i, [128, 1024], F32) for i in range(3)]
            tab = sb("tab", [128, nh, TW], F32)
            Z = sb("Z", [128, 512], BF16)
            e_z = cx.op("gpsimd", mset(Z[:], 0.0), sig=True)
            e_tab = cx.op("sync", dma(tab[:], tabd), dsem=kls)
        e_sk = None
        if kind == 2:
            esk = sb("esk", [65, 16], F32)
            cx.op("sync", dma(esk[64:65, :], sinkd), dsem=kls)

        if kind in (0, 2):
            Kd = sb("K", [128, nkv, S], BF16)
            Vs = sb("V", [128, 32, nkv, 65], BF16)
            Qc = [sb("Qc%d" % i, [128, S], BF16) for i in range(2)]
            KTv = KT.rearrange("(c p) t -> p c t", p=128)
            for g in range(nkv):
                cx.op("sync", dma(Kd[:, g, :], KTv[:, g, :]), dsem=kls)
            Vdv = Vd.rearrange("(b p) f -> p b f", p=128)
            e_kv = None
            for q4 in range(4):
                e_kv = cx.op("sync", dma(Vs[:, 8 * q4:8 * q4 + 8, :, :].rearrange("p b k d -> p b (k d)"),
                                         Vdv[:, 8 * q4:8 * q4 + 8, :]), dsem=kls)
            if kind == 2:
                e_sk = cx.op("scalar", act(esk[64:65, :], esk[64:65, :], AF.Exp), waits=[e_kv], sig=True)
            q_free = [None, None]
            for c in range(nh // 2):
                g = c // 2
                for qt in range(8):
                    qs = qt * 512
                    sl = []
                    if kind == 0:
                        kbs = [(kb, qs, qs + 512, 0) for kb in range(32)]
                    else:
                        kbs = []
                        for kb in range(max(0, 4 * qt - 1), min(32, 4 * qt + 5)):
                            lo, hi = max(qs, 128 * kb - 128), min(qs + 512, 128 * kb + 256)
                            kbs.append((kb, lo, hi, lo - (128 * kb - 128)))
                    for (kb, lo, hi, off) in kbs:
                        subs = []
                        for hh in range(2):
                            subs.append(dict(k=Kd[hh * 64:(hh + 1) * 64, g, kb * 128:(kb + 1) * 128],
                                             q=(c, hh, lo, hi), n=hi - lo, v=Vs[:, kb, g, :], c0=lo - qs,
                                             tab=tab[:, 2 * c + hh, off:off + hi - lo] if has_tab else None))
                        sl.append(dict(subs=subs, tab2=tab[:, 2 * c:2 * c + 2, off:off + hi - lo] if has_tab else None))
                    units.append(dict(hs=(2 * c, 2 * c + 1), ql=512, steps=sl, dst=("plain", qs), qchunk=c,
                                      hfirst=(qt == 0), hlast=(qt == 7)))
        else:
            Qg = sb("Qg", [128, 3, S], BF16)
            Kg = sb("Kg", [128, 2, S], BF16)
            Qp = sb("Qp", [128, 3, S], BF16)
            Kp = sb("Kp", [128, 2, S], BF16)
            Vs = sb("V", [128, 32, 2, 65], BF16)
            for g, (window, d) in enumerate(B_GROUPS):
                L = S // d
                ql = min(512, L)
                nb = L // 128
                src = Kp if d > 1 else Kg
                for ci in range(3):
                    ulist = []
                    for rho in range(d):
                        for qt in range(L // ql):
                            qs = qt * ql
                            sl = []
                            for kb in range(max(0, (qs - 64) // 128), min(nb, (qs + ql + 64 + 127) // 128)):
                                lo, hi = max(qs, 128 * kb - 64), min(qs + ql, 128 * kb + 192)
                                if hi <= lo:
                                    continue
                                off = lo - (128 * kb - 64)
                                subs = []
                                for hh in range(2):
                                    i6 = 2 * ci + hh
                                    kvl = i6 // 3
                                    subs.append(dict(
                                        k=src[hh * 64:(hh + 1) * 64, kvl, rho * L + kb * 128:rho * L + (kb + 1) * 128],
                                        q=(ci, hh, rho * L + lo, rho * L + hi), n=hi - lo,
                                        v=Vs[:, rho * nb + kb, kvl, :], c0=lo - qs,
                                        tab=tab[:, 6 * g + i6, off:off + hi - lo]))
                                h0 = 6 * g + 2 * ci
                                sl.append(dict(subs=subs, tab2=tab[:, h0:h0 + 2, off:off + hi - lo]))
                            ulist.append(dict(hs=(6 * g + 2 * ci, 6 * g + 2 * ci + 1), ql=ql, steps=sl,
                                              dst=("perm", d, rho, qs), group=g))
                    ulist[0]["hfirst"] = True
                    ulist[-1]["hlast"] = True
                    units.extend(ulist)

        flat = []
        for ui, u in enumerate(units):
            for si, st in enumerate(u["steps"]):
                flat.append((ui, si == 0, si == len(u["steps"]) - 1, st))
        NS = len(flat)
        e_qk, e_rd, e_exp, e_pv = {}, {}, {}, {}
        e_evac = {}
        ud_free = [None] * NUD
        state = dict(group=-1, qchunk=-1, ready=None, pcount=-1, last_pe=None, qbuf_last={})

        def prepare(ui):
            u = units[ui]
            if kind in (0, 2):
                c = u["qchunk"]
                if c != state["qchunk"]:
                    state["qchunk"] = c
                    ql_ = state["qbuf_last"]
                    for cc in (c, c + 1):
                        if cc not in ql_ and cc < nh // 2:
                            ql_[cc] = cx.op("sync", dma(Qc[cc % 2][:], QT[cc * 128:(cc + 1) * 128, :]),
                                            waits=[q_free[cc % 2]], dsem=qls[cc % 2])
                    state["ready"] = [ql_[c], e_kv, e_tab]
            else:
                g = u["group"]
                if g != state["group"]:
                    state["group"] = g
                    d = B_GROUPS[g][1]
                    L = S // d
                    nb = L // 128
                    wl = [state["last_pe"]]
                    for ci in range(3):
                        cx.op("sync", dma(Qg[:, ci, :], QT[(3 * g + ci) * 128:(3 * g + ci + 1) * 128, :]), waits=wl, dsem=qls[0])
                    for kvl in range(2):
                        cx.op("sync", dma(Kg[:, kvl, :], KT[(2 * g + kvl) * 128:(2 * g + kvl + 1) * 128, :]), dsem=qls[0])
                    Vv = Vd.rearrange("(b p r) f -> r p b f", p=128, r=d)
                    e_l = None
                    for rho in range(d):
                        e_l = cx.op("sync", dma(Vs[:, rho * nb:(rho + 1) * nb, :, :].rearrange("p b k d -> p b (k d)"),
                                                Vv[rho, :, :, 2 * g * 65:(2 * g + 2) * 65]), dsem=qls[0])
                    rdy = [e_l, e_tab]
                    if d > 1:
                        e_p = None
                        for ci in range(3):
                            e_p = cx.op("gpsimd", cpy(Qp[:, ci, :].rearrange("p (r m) -> p r m", r=d),
                                                      Qg[:, ci, :].rearrange("p (m r) -> p r m", r=d)),
                                        waits=[e_l, state["last_pe"]], sig=True)
                        for kvl in range(2):
                            e_p = cx.op("gpsimd", cpy(Kp[:, kvl, :].rearrange("p (r m) -> p r m", r=d),
                                                      Kg[:, kvl, :].rearrange("p (m r) -> p r m", r=d)),
                                        waits=[e_l], sig=True)
                        rdy.append(e_p)
                    state["ready"] = rdy

        def q_ap(q):
            ci, hh, a, b_ = q
            if kind in (0, 2):
                return Qc[ci % 2][hh * 64:(hh + 1) * 64, a:b_]
            src_ = Qp if B_GROUPS[state["group"]][1] > 1 else Qg
            return src_[hh * 64:(hh + 1) * 64, ci, a:b_]

        def emit_qk(s):
            ui, first, last, st = flat[s]
            if first:
                prepare(ui)
            sbi = s % 2
            ev = None
            for k, sub in enumerate(st["subs"]):
                n = sub["n"]
                ev = cx.op("tensor", mm(Sps[sbi][:, k * 512:k * 512 + n], sub["k"], q_ap(sub["q"]), True, True),
                           waits=state["ready"] + [e_rd.get(s - 2)], sig=(k == 1))
            e_qk[s] = ev
            state["last_pe"] = ev
            if kind in (0, 2):
                q_free[units[ui]["qchunk"] % 2] = ev

        def emit_sm(s):
            ui, first, last, st = flat[s]
            sbi, pi = s % 2, s % 3
            n = st["subs"][0]["n"]
            if has_tab:
                e_a = cx.op("vector", tt(SP[pi][:].rearrange("p (k n) -> p k n", k=2)[:, :, 0:n],
                                         Sps[sbi][:].rearrange("p (k n) -> p k n", k=2)[:, :, 0:n],
                                         st["tab2"], ALU.add),
                            waits=[e_qk[s], e_exp.get(s - 3), e_tab], sig=True)
                e_rd[s] = e_a
                e_exp[s] = cx.op("scalar", act(P[pi][:].rearrange("p (k n) -> p k n", k=2)[:, :, 0:n],
                                               SP[pi][:].rearrange("p (k n) -> p k n", k=2)[:, :, 0:n], AF.Exp),
                                 waits=[e_a, e_pv.get(s - 3)], sig=True)
            else:
                e_exp[s] = cx.op("scalar", act(P[pi][:], Sps[sbi][:], AF.Exp),
                                 waits=[e_qk[s], e_pv.get(s - 3)], sig=True)
                e_rd[s] = e_exp[s]

        def emit_pv(s):
            ui, first, last, st = flat[s]
            pi = s % 3
            u = units[ui]
            ev = None
            for k, sub in enumerate(st["subs"]):
                ob = 2 * (ui % 2) + k
                n, c0 = sub["n"], sub["c0"]
                if first and has_tab:
                    cx.op("tensor", mm(ops[ob][0:65, 0:u["ql"]], Z[:, 0:65], Z[:, 0:u["ql"]], True, False),
                          waits=[e_z, e_evac.get(ui - 2)])
                ev = cx.op("tensor", mm(ops[ob][0:65, c0:c0 + n], sub["v"], P[pi][:, k * 512:k * 512 + n],
                                        first and not has_tab, last),
                           waits=[e_exp[s], e_evac.get(ui - 2) if first else None], sig=(k == 1))
            e_pv[s] = ev
            state["last_pe"] = ev
            if last:
                if u.get("hfirst"):
                    state["pcount"] += 1
                ql = u["ql"]
                e_ev = None
                for k in range(2):
                    ob = 2 * (ui % 2) + k
                    hb = (2 * state["pcount"] + k) % NUD
                    if u["dst"][0] == "plain":
                        dst = UD[hb][0:65, u["dst"][1]:u["dst"][1] + ql]
                    else:
                        _, d, rho, qs = u["dst"]
                        dst = UD[hb][0:65, :].rearrange("p (m r) -> p r m", r=d)[:, rho, qs:qs + ql]
                    e_ev = cx.op("vector", cpy(dst, ops[ob][0:65, 0:ql]),
                                 waits=[ev, ud_free[hb] if u.get("hfirst") else None], sig=True)
                e_evac[ui] = e_ev
                if u.get("hlast"):
                    for k in range(2):
                        hb = (2 * state["pcount"] + k) % NUD
                        h = u["hs"][k]
                        if kind == 2:
                            e_ev = cx.op("vector", ts(UD[hb][64:65, :], UD[hb][64:65, :], esk[64:65, h:h + 1], ALU.add),
                                         waits=[e_sk], sig=True)
                        ud_free[hb] = cx.op("sync", dma(OTf[h], UD[hb][:]), waits=[e_ev], dsem=sts[hb])

        def new_group(s):
            return kind == 1 and units[flat[s][0]]["group"] != units[flat[s - 1][0]]["group"]

        emit_qk(0)
        for s in range(NS):
            defer = s + 1 < NS and new_group(s + 1)
            if s + 1 < NS and not defer:
                emit_qk(s + 1)
            emit_sm(s)
            emit_pv(s)
            if defer:
                emit_qk(s + 1)
        for s_ in sts:
            cx.op("sync", lambda e, s_=s_: e.wait_ge(s_.h, s_.n))
        cx.flush()


def layer_dims(kind):
    nh, nkv = CFG[kind]
    return nh, nkv, nh * 64 + nkv * 128 + nkv * 64


def build(nlayers=DEPTH, debug=False):
    nc = bass.Bass("TRN2", target_bir_lowering=False)
    ein = lambda name, shape, dt=F32: nc.dram_tensor(name, shape, dt, kind="ExternalInput").ap()
    xT = ein("xT", [D, S])
    yT = nc.dram_tensor("yT", [D, S], F32, kind="ExternalOutput").ap()
    Gd = ein("G", [128, NG])
    onesd = ein("ones", [128, 128], BF16)
    BDCd = ein("BDC", [128, 12, 32], BF16)
    SelCd = ein("SelC", [32, 12, 128])
    Rd = ein("Rm", [128, 128])
    cosd = ein("cosT", [128, S])
    sind = ein("sinT", [128, S])
    tabBd = ein("tabB", [128, 18, 256])
    tabCd = ein("tabC", [128, 16, 384])
    selAd = ein("selA", [16, 8 * 128])
    selBd = ein("selB", [6, 9 * 128])
    m1Bd = ein("m1B", [18, 6])
    sinkd = ein("sinks", [1, 16])
    wq, wo, w1, w2 = [], [], [], []
    for l in range(nlayers):
        nh, nkv, NW = layer_dims(KINDS[l])
        wq.append(ein("wqkv%d" % l, [128, 8, NW]))
        wo.append(ein("wo%d" % l, [128, nh // 2, D]))
        w1.append(ein("w1_%d" % l, [128, 8, DFF]))
        w2.append(ein("w2_%d" % l, [128, 32, D]))
    sk = "ExternalOutput" if debug else "Internal"
    hA = nc.dram_tensor("hA", [D, S], F32, kind=sk).ap()
    hB = nc.dram_tensor("hB", [D, S], F32, kind=sk).ap()
    QT = nc.dram_tensor("QT", [1152, S], BF16, kind=sk).ap()
    KT = nc.dram_tensor("KT", [768, S], BF16, kind=sk).ap()
    VdA = nc.dram_tensor("VdA", [S, 4 * 65], BF16, kind=sk).ap()
    VdB = nc.dram_tensor("VdB", [S, 6 * 65], BF16, kind=sk).ap()
    OTf = nc.dram_tensor("OTf", [18, 65, S], F32, kind=sk).ap()
    with ExitStack() as es:
        cx = Ctx(nc, es)
        G = es.enter_context(nc.sbuf_tensor("Gsb", [128, NG], F32))
        ones = es.enter_context(nc.sbuf_tensor("onessb", [128, 128], BF16))
        Rm = es.enter_context(nc.sbuf_tensor("Rsb", [128, 128], F32))
        cs = cx.new_sem("const")
        cx.op("sync", dma(G[:], Gd), dsem=cs)
        cx.op("sync", dma(ones[:], onesd), dsem=cs)
        cx.op("sync", dma(Rm[:], Rd), dsem=cs)
        for e in ["scalar", "vector", "tensor", "gpsimd"]:
            cx.op(e, lambda en: en.wait_ge(cs.h, cs.n))
        cx.flush()
        h_cur = xT
        used = [0, 0, 0]
        for l in range(nlayers):
            kind = KINDS[l]
            j = used[kind]
            used[kind] += 1
            Vd = VdB if kind == 1 else VdA
            stage_qkv(cx, l, kind, j, h_cur, wq[l], QT, KT, Vd, G[:], ones[:], BDCd, SelCd, Rm[:], cosd, sind)
            if debug == 3 and l == nlayers - 1:
                break
            stage_attn(cx, l, kind, QT, KT, Vd, OTf, tabBd if kind == 1 else tabCd, sinkd)
            if debug == 1 and l == nlayers - 1:
                break
            stage_oproj(cx, l, kind, h_cur, hA, OTf, wo[l], selBd if kind == 1 else selAd, m1Bd)
            last = l == nlayers - 1
            if debug and last:
                break
            stage_mlp(cx, l, hA, yT if last else hB, w1[l], w2[l], G[:], ones[:], last)
            h_cur = hB
    return nc


def alibi_slopes_np(n):
    return (2.0 ** (-8.0 * np.arange(1, n + 1, dtype=np.float32) / n)).astype(np.float32)


def host_consts():
    c = {}
    c["ones"] = np.ones((128, 128), ml_dtypes.bfloat16)
    bdc = np.zeros((128, 12, 32), np.float32)
    selc = np.zeros((32, 12, 128), np.float32)
    for oc in range(12):
        for p in range(128):
            bdc[p, oc, 2 * oc + p // 64] = 1
            selc[2 * oc + p // 64, oc, p] = 1
    c["BDC"] = bdc.astype(ml_dtypes.bfloat16)
    c["SelC"] = selc
    R = np.zeros((128, 128), np.float32)
    for m in range(128):
        jj = (m % 64) % 32
        if jj < 16:
            R[m + 16, m] = -1.0
        else:
            R[m - 16, m] = 1.0
    c["Rm"] = R
    t = np.arange(S)
    row = (t // GRID_W).astype(np.float32)
    col = (t % GRID_W).astype(np.float32)
    inv = (np.float32(ROPE_THETA) ** (-np.arange(0, 32, 2, dtype=np.float32) / np.float32(32))).astype(np.float32)
    cosT = np.zeros((128, S), np.float32)
    sinT = np.zeros((128, S), np.float32)
    for p in range(128):
        dd = p % 64
        pos = row if dd < 32 else col
        ang = (pos * inv[(dd % 32) % 16]).astype(np.float32)
        cosT[p] = np.cos(ang.astype(np.float64)).astype(np.float32)
        sinT[p] = np.sin(ang.astype(np.float64)).astype(np.float32)
    c["cosT"], c["sinT"] = cosT, sinT
    k = np.arange(128)[:, None]
    slB = alibi_slopes_np(18)
    tabB = np.zeros((128, 18, 256), np.float32)
    jB = np.arange(256)[None, :]
    relB = np.abs(k + 64 - jB)
    for h in range(18):
        d = B_GROUPS[h // 6][1]
        tabB[:, h, :] = np.where(relB <= 64, -slB[h] * (relB * d).astype(np.float32), NEG)
    c["tabB"] = tabB
    slC = alibi_slopes_np(16)
    tabC = np.zeros((128, 16, 384), np.float32)
    jC = np.arange(384)[None, :]
    relC = np.abs(k + 128 - jC)
    for h in range(16):
        tabC[:, h, :] = np.where(relC <= 128, -slC[h] * relC.astype(np.float32), NEG)
    c["tabC"] = tabC
    selA = np.zeros((16, 8 * 128), np.float32)
    for col_ in range(8 * 128):
        selA[2 * (col_ // 128) + (col_ % 128) // 64, col_] = 1
    c["selA"] = selA
    selB = np.zeros((6, 9 * 128), np.float32)
    for col_ in range(9 * 128):
        hd = 2 * (col_ // 128) + (col_ % 128) // 64
        selB[hd % 6, col_] = 1
    c["selB"] = selB
    m1 = np.zeros((18, 6), np.float32)
    for h in range(18):
        m1[h, h % 6] = 1
    c["m1B"] = m1
    return c


def arr_k(w, nchunk):
    return np.ascontiguousarray(w.reshape(nchunk, 128, w.shape[1]).transpose(1, 0, 2))


def host_prep(inp, nlayers=DEPTH):
    shared = host_consts()
    G = np.zeros((128, NG), np.float32)
    for l in range(DEPTH):
        G[:, gcol_attn(l):gcol_attn(l) + 8] = inp["attn_norm"][l].reshape(8, 128).T
        G[:, gcol_mlp(l):gcol_mlp(l) + 8] = inp["mlp_norm"][l].reshape(8, 128).T
    G[:, GCOL_FINAL:GCOL_FINAL + 8] = inp["final_norm"].reshape(8, 128).T
    for j in range(2):
        G[:, GCOL_QG + j] = np.tile(inp["a_q_gain"][j], 2)
        G[:, GCOL_KG + j] = np.tile(inp["a_k_gain"][j], 2)
    G[:, GCOL_EPS] = EPS
    G[:, GCOL_EPS64] = 64 * EPS
    G[:, GCOL_SCL] = 1.0 / 64
    G[:, GCOL_BIA] = EPS
    G[0:16, GCOL_SCL] = 1.0
    G[0:16, GCOL_BIA] = 64 * EPS
    shared["G"] = G
    shared["sinks"] = np.ascontiguousarray(inp["c_sinks"][0:1]).astype(np.float32)
    used = [0, 0, 0]
    for l in range(nlayers):
        kind = KINDS[l]
        j = used[kind]
        used[kind] += 1
        nh, nkv = CFG[kind]
        w = [inp["a_w_qkv"], inp["b_w_qkv"], inp["c_w_qkv"]][kind][j]
        wo = [inp["a_w_o"], inp["b_w_o"], inp["c_w_o"]][kind][j]
        nq = nh * 64
        q, k, v = w[:, :nq], w[:, nq:nq + nkv * 64], w[:, nq + nkv * 64:]
        kd = np.concatenate([np.concatenate([k[:, g * 64:(g + 1) * 64]] * 2, axis=1) for g in range(nkv)], axis=1)
        shared["wqkv%d" % l] = arr_k(np.concatenate([q, kd, v], axis=1), 8)
        shared["wo%d" % l] = arr_k(wo, nh // 2)
        shared["w1_%d" % l] = arr_k(inp["mlp_w1"][l], 8)
        shared["w2_%d" % l] = arr_k(inp["mlp_w2"][l], 32)
    return shared


_NC_CACHE = {}


def kernel(**inputs):
    inp = {k: np.asarray(v) for k, v in inputs.items()}
    shared = host_prep(inp)
    x = inp["x"].astype(np.float32)
    if DEPTH not in _NC_CACHE:
        _NC_CACHE[DEPTH] = build(DEPTH)
    nc = _NC_CACHE[DEPTH]
    in_maps = []
    for b in range(NCORES):
        m = dict(shared)
        m["xT"] = np.ascontiguousarray(x[b].T)
        in_maps.append(m)
    res = run_bass_kernel_spmd(nc, in_maps, core_ids=list(range(NCORES)))
    out = np.empty((NCORES, S, D), np.float32)
    for b in range(NCORES):
        out[b] = res.results[b]["yT"].T
    return out
```

```python
import numpy as np
import ml_dtypes
from contextlib import ExitStack
import concourse.bass as bass
import concourse.mybir as mybir
from concourse.bass_utils import run_bass_kernel_spmd

F32 = mybir.dt.float32
BF16 = mybir.dt.bfloat16
AF = mybir.ActivationFunctionType
ALU = mybir.AluOpType

S = 4096
D = 1024
DFF = 4096
HD = 64
NCORES = 8
DEPTH = 4
KINDS = [0, 1, 2, 0]
EPS = 1e-6
GRID_W = 64
ROPE_THETA = 10000.0
B_GROUPS = ((128, 1), (512, 4), (2048, 16))
NEG = -30000.0

CFG = {0: (16, 4), 1: (18, 6), 2: (16, 4)}

ENGS = ["sync", "scalar", "gpsimd", "vector", "tensor"]
STRICT = True


class Sem:
    def __init__(self, h):
        self.h = h
        self.n = 0


class Ctx:
    def __init__(self, nc, es):
        self.nc = nc
        self.es = es
        self.lists = {e: [] for e in ENGS}
        self.waited = {e: {} for e in ENGS}
        self.esem = {}
        self.allsems = []
        for e in ["scalar", "gpsimd", "vector", "tensor"]:
            self.esem[e] = self.new_sem("e_" + e)
        self.nsem = 0

    def new_sem(self, name):
        s = Sem(self.es.enter_context(self.nc.semaphore(name)))
        self.allsems.append(s)
        return s

    def op(self, eng, fn, waits=(), sig=False, dsem=None):
        ws = []
        wd = self.waited[eng]
        flat = []
        for ev in waits:
            if ev is None:
                continue
            if isinstance(ev[0], Sem):
                flat.append(ev)
            else:
                flat.extend(x for x in ev if x is not None)
        for ev in flat:
            sem, val = ev
            if eng in self.esem and sem is self.esem[eng] and not STRICT:
                continue
            if wd.get(id(sem), 0) >= val:
                continue
            wd[id(sem)] = val
            ws.append((sem, val))
        ev = None
        inc = 0
        if dsem is not None:
            dsem.n += 16
            inc = 16
            ev = (dsem, dsem.n)
        elif sig:
            s = self.esem[eng]
            s.n += 1
            inc = 1
            ev = (s, s.n)
        self.lists[eng].append((ws, fn, ev, inc))
        return ev

    def flush(self):
        nc = self.nc
        with nc.Block() as block:
            for eng in ENGS:
                lst = self.lists[eng]
                if not lst:
                    continue

                def body(e, lst=lst):
                    for ws, fn, ev, inc in lst:
                        for sem, val in ws:
                            e.wait_ge(sem.h, val)
                        ins = fn(e)
                        if ev is not None:
                            ins.then_inc(ev[0].h, inc)

                getattr(block, eng)(body)
        self.lists = {e: [] for e in ENGS}


def mm(out, lhsT, rhs, start, stop):
    return lambda e: e.matmul(out, lhsT=lhsT, rhs=rhs, start=start, stop=stop)


def act(out, in_, func, scale=1.0, bias=None):
    if bias is None:
        return lambda e: e.activation(out=out, in_=in_, func=func, scale=scale)
    return lambda e: e.activation(out=out, in_=in_, func=func, scale=scale, bias=bias)


def tt(out, in0, in1, op):
    return lambda e: e.tensor_tensor(out=out, in0=in0, in1=in1, op=op)


def stt(out, in0, scalar, in1, op0, op1):
    return lambda e: e.scalar_tensor_tensor(out=out, in0=in0, scalar=scalar, in1=in1, op0=op0, op1=op1)


def ts(out, in0, s1, op0, s2=None, op1=None):
    if op1 is None:
        return lambda e: e.tensor_scalar(out=out, in0=in0, scalar1=s1, scalar2=None, op0=op0)
    return lambda e: e.tensor_scalar(out=out, in0=in0, scalar1=s1, scalar2=s2, op0=op0, op1=op1)


def recip(out, in_):
    return lambda e: e.reciprocal(out=out, in_=in_)


def cpy(out, in_):
    return lambda e: e.tensor_copy(out=out, in_=in_)


def dma(out, in_):
    return lambda e: e.dma_start(out=out, in_=in_)


def mset(ap, v):
    return lambda e: e.memset(ap, v)


def gcol_attn(l):
    return l * 8


def gcol_mlp(l):
    return 32 + l * 8


GCOL_FINAL = 64
GCOL_QG = 72
GCOL_KG = 74
GCOL_EPS = 76
GCOL_EPS64 = 77
GCOL_SCL = 78
GCOL_BIA = 79
NG = 80


class RmsNorm:
    def __init__(self, cx, sb, ps, name, T, ones, G):
        self.cx, self.T, self.ones, self.G = cx, T, ones, G
        self.sq = [sb(name + "_sq%d" % i, [128, T], BF16) for i in range(2)]
        self.sd = sb(name + "_sd", [128, T], F32)
        self.rstd = sb(name + "_rstd", [128, T], F32)
        self.ss = ps(name + "_ss")
        self.sq_free = [None, None]
        self.ss_free = None
        self.sd_free = None
        self.users = None

    def emit(self, ht, h_ready):
        cx, T = self.cx, self.T
        ev_mm = None
        for c in range(8):
            b = c % 2
            e_sq = cx.op("scalar", act(self.sq[b][:], ht[:, c, :], AF.Square),
                         waits=[h_ready, self.sq_free[b]], sig=True)
            ev_mm = cx.op("tensor", mm(self.ss[:, 0:T], self.ones, self.sq[b][:], c == 0, c == 7),
                          waits=[e_sq, self.ss_free if c == 0 else None], sig=True)
            self.sq_free[b] = ev_mm
        e_sd = cx.op("scalar", act(self.sd[:], self.ss[:, 0:T], AF.Sqrt, scale=1.0 / D,
                                   bias=self.G[:, GCOL_EPS:GCOL_EPS + 1]),
                     waits=[ev_mm, self.sd_free], sig=True)
        self.ss_free = e_sd
        e_r = cx.op("vector", recip(self.rstd[:], self.sd[:]), waits=[e_sd, self.users], sig=True)
        self.sd_free = e_r
        return e_r


def load_weights_cast(cx, Wsb, Wdram, wsem, pieces):
    ev = None
    for (c0, c1, n0, n1) in pieces:
        ev = cx.op("gpsimd", dma(Wsb[:, c0:c1, n0:n1], Wdram[:, c0:c1, n0:n1]), dsem=wsem)
    return ev


def stage_mlp(cx, l, h_in, h_out, w1d, w2d, G, ones, last):
    nc = cx.nc
    T = 512
    NT = S // T
    HM = 16
    hin_v = h_in.rearrange("(c p) t -> p c t", p=128)
    hout_v = h_out.rearrange("(c p) t -> p c t", p=128)
    with ExitStack() as es:
        sb = lambda name, shape, dt: es.enter_context(nc.sbuf_tensor("L%d_" % l + name, shape, dt))
        ps = lambda name: es.enter_context(nc.psum_tensor("L%d_" % l + name, [128, 512], F32))
        W1 = sb("w1", [128, 8, DFF], BF16)
        W2 = sb("w2", [128, 32, D], BF16)
        ht = [sb("m_h%d" % i, [128, 8, T], F32) for i in range(2)]
        hn = sb("m_hn", [128, 8, T], BF16)
        u = sb("m_u", [128, HM, T], BF16)
        rl = [sb("m_rl%d" % i, [128, T], F32) for i in range(3)]
        nrm = RmsNorm(cx, sb, ps, "m_n", T, ones, G)
        if last:
            yb = [sb("m_y%d" % i, [128, T], F32) for i in range(2)]
            yb_free = [None, None]
        ups = [ps("m_ups%d" % i) for i in range(4)]
        ops_ = [ps("m_ops%d" % i) for i in range(3)]
        lds = [cx.new_sem("m%d_ld%d" % (l, i)) for i in range(2)]
        sts = [cx.new_sem("m%d_st%d" % (l, i)) for i in range(2)]
        w1s = cx.new_sem("m%d_w1" % l)
        w2s = cx.new_sem("m%d_w2" % l)
        NU, NO = len(ups), len(ops_)

        e_w1 = load_weights_cast(cx, W1, w1d, w1s, [(0, 8, 0, 2048), (0, 8, 2048, 4096)])
        e_w2 = load_weights_cast(cx, W2, w2d, w2s, [(8 * i, 8 * i + 8, 0, D) for i in range(4)])

        h_free = [None, None]
        hn_free = None
        u_free = None
        rl_free = [None, None, None]
        ups_free = [None] * NU
        ops_free = [None] * NO
        gm = gcol_mlp(l)
        cu = [0]
        co = [0]

        def load(i):
            b = i % 2
            return cx.op("sync", dma(ht[b][:], hin_v[:, :, i * T:(i + 1) * T]), waits=[h_free[b]], dsem=lds[b])

        def norm(i, h_ready):
            b = i % 2
            e_r = nrm.emit(ht[b], h_ready)
            e_hn = None
            for c in range(8):
                e_hn = cx.op("vector", stt(hn[:, c, :], ht[b][:, c, :], G[:, gm + c:gm + c + 1], nrm.rstd[:],
                                           ALU.mult, ALU.mult), waits=[e_r, hn_free], sig=True)
            nrm.users = e_hn
            return e_hn

        ld_ev = {0: load(0)}
        e_hn = norm(0, ld_ev[0])
        for i in range(NT):
            b = i % 2
            t0 = i * T
            if i + 1 < NT:
                ld_ev[i + 1] = load(i + 1)
            e_add = None
            for half in range(2):
                e_u = None
                e_mm = None
                for mm_ in range(HM):
                    m = half * HM + mm_
                    pb = cu[0] % NU
                    cu[0] += 1
                    for c in range(8):
                        e_mm = cx.op("tensor", mm(ups[pb][:], W1[:, c, m * 128:(m + 1) * 128], hn[:, c, :],
                                                  c == 0, c == 7),
                                     waits=[e_hn, e_w1, ups_free[pb] if c == 0 else None], sig=(c == 7))
                    rb = m % 3
                    e_rl = cx.op("scalar", act(rl[rb][:], ups[pb][:], AF.Relu), waits=[e_mm, rl_free[rb]], sig=True)
                    ups_free[pb] = e_rl
                    e_u = cx.op("gpsimd", tt(u[:, mm_, :], rl[rb][:], rl[rb][:], ALU.mult), waits=[e_rl, u_free], sig=True)
                    rl_free[rb] = e_u
                if half == 1:
                    hn_free = e_mm
                    if i + 1 < NT:
                        e_hn_next = norm(i + 1, ld_ev[i + 1])
                for n in range(8):
                    pb = co[0] % NO
                    co[0] += 1
                    for mm_ in range(HM):
                        m = half * HM + mm_
                        e_mm = cx.op("tensor", mm(ops_[pb][:], W2[:, m, n * 128:(n + 1) * 128], u[:, mm_, :],
                                                  mm_ == 0, mm_ == HM - 1),
                                     waits=[e_u, e_w2, ops_free[pb] if mm_ == 0 else None], sig=(mm_ == HM - 1))
                    e_add = cx.op("vector", tt(ht[b][:, n, :], ops_[pb][:], ht[b][:, n, :], ALU.add),
                                  waits=[e_mm, ld_ev[i]], sig=True)
                    ops_free[pb] = e_add
                u_free = e_mm
            if not last:
                h_free[b] = cx.op("sync", dma(hout_v[:, :, t0:t0 + T], ht[b][:]), waits=[e_add], dsem=sts[b])
            else:
                e_r2 = nrm.emit(ht[b], e_add)
                e_y = None
                for c in range(8):
                    yi = c % 2
                    e_y = cx.op("vector", stt(yb[yi][:], ht[b][:, c, :], G[:, GCOL_FINAL + c:GCOL_FINAL + c + 1],
                                              nrm.rstd[:], ALU.mult, ALU.mult), waits=[e_r2, yb_free[yi]], sig=True)
                    yb_free[yi] = cx.op("sync", dma(hout_v[:, c, t0:t0 + T], yb[yi][:]), waits=[e_y], dsem=sts[yi])
                nrm.users = e_y
                h_free[b] = e_y
            if i + 1 < NT:
                e_hn = e_hn_next
        for s_ in sts:
            if s_.n:
                cx.op("sync", lambda e, s_=s_: e.wait_ge(s_.h, s_.n))
        cx.flush()


def stage_qkv(cx, l, kind, j, h_in, wd, QT, KT, Vd, G, ones, BDCd, SelCd, Rm, cosd, sind):
    nc = cx.nc
    T = 512
    NT = S // T
    nh, nkv = CFG[kind]
    nqc, nkc, NV = nh // 2, nkv, nkv * 64
    noc = nqc + nkc
    NW = nh * 64 + nkv * 128 + NV
    voff = nh * 64 + nkv * 128
    hin_v = h_in.rearrange("(c p) t -> p c t", p=128)
    QTv = QT.rearrange("(c p) t -> p c t", p=128)
    KTv = KT.rearrange("(c p) t -> p c t", p=128)
    Vdv = Vd.rearrange("(tb p) f -> p tb f", p=128)
    isA = kind == 0
    with ExitStack() as es:
        sb = lambda name, shape, dt: es.enter_context(nc.sbuf_tensor("Q%d_" % l + name, shape, dt))
        ps = lambda name: es.enter_context(nc.psum_tensor("Q%d_" % l + name, [128, 512], F32))
        W = sb("w", [128, 8, NW], BF16)
        ht = [sb("h%d" % i, [128, 8, T], F32) for i in range(2)]
        hn = [sb("hn%d" % i, [128, 8, T], BF16) for i in range(2)]
        qo = [sb("qo%d" % i, [128, nqc, T], BF16) for i in range(2)]
        ko = [sb("ko%d" % i, [128, nkc, T], BF16) for i in range(2)]
        vo = [sb("vo%d" % i, [128, 4, nkv, 65], BF16) for i in range(2)]
        nrm = RmsNorm(cx, sb, ps, "n", T, ones, G)
        pj = [ps("pj%d" % i) for i in range(3 if not isA else 2)]
        NPJ = len(pj)
        lds = [cx.new_sem("q%d_ld%d" % (l, i)) for i in range(2)]
        sts = [cx.new_sem("q%d_st%d" % (l, i)) for i in range(2)]
        ws = cx.new_sem("q%d_w" % l)
        half = NW // 2
        e_w = load_weights_cast(cx, W, wd, ws, [(0, 4, 0, NW), (4, 8, 0, NW)] if NW <= 2048 else
                                [(0, 8, 0, half), (0, 8, half, NW)])
        e_ms = None
        for b in range(2):
            e_ms = cx.op("gpsimd", mset(vo[b][:], 1.0), sig=True)
        if isA:
            cst = [sb("cs%d" % i, [128, 2, T], F32) for i in range(2)]
            tmpn = ["qg", "sq", "t1", "t2"]
            tmp = {n: [sb(n + "%d" % i, [128, T], BF16 if n == "sq" else F32) for i in range(2)] for n in tmpn}
            tfree = {n: [None, None] for n in tmpn}
            T3 = sb("T3", [128, noc, T], F32)
            t3_free = [None] * noc
            BDC = sb("BDC", [128, noc, 32], BF16)
            SelC = sb("SelC", [32, noc, 128], F32)
            sdc = sb("sdc", [32, T], F32)
            rsc = sb("rsc", [32, T], F32)
            aux = [ps("aux%d" % i) for i in range(2)]
            aux_free = [None, None]
            ssc = ps("ssc")
            ssc_free = None
            sdc_free = None
            rsc_free = None
            cx.op("sync", dma(BDC[:], BDCd), dsem=ws)
            e_w = cx.op("sync", dma(SelC[:], SelCd), dsem=ws)

        h_free = [None, None]
        out_free = [None, None]
        hn_free = [None, None]
        pj_free = [None] * NPJ
        gm = gcol_attn(l)
        cnt = [0]

        def load(i):
            b = i % 2
            ev = cx.op("sync", dma(ht[b][:], hin_v[:, :, i * T:(i + 1) * T]), waits=[h_free[b]], dsem=lds[b])
            if isA:
                cx.op("sync", dma(cst[b][:, 0, :], cosd[:, i * T:(i + 1) * T]), dsem=lds[b])
                ev = cx.op("sync", dma(cst[b][:, 1, :], sind[:, i * T:(i + 1) * T]), dsem=lds[b])
            return ev

        def norm(i, h_ready):
            b = i % 2
            e_r = nrm.emit(ht[b], h_ready)
            e_hn = None
            for c in range(8):
                e_hn = cx.op("vector", stt(hn[b][:, c, :], ht[b][:, c, :], G[:, gm + c:gm + c + 1], nrm.rstd[:],
                                           ALU.mult, ALU.mult), waits=[e_r, hn_free[b]], sig=True)
            nrm.users = e_hn
            if not isA:
                h_free[b] = e_hn
            return e_hn

        ld_ev = {0: load(0)}
        e_hn = norm(0, ld_ev[0])
        for i in range(NT):
            b = i % 2
            t0 = i * T
            if i + 1 < NT:
                ld_ev[i + 1] = load(i + 1)
            evs_out = []
            deferred = []
            e_mm = None
            e_t2 = None
            for oc in range(noc):
                isq = oc < nqc
                pb = cnt[0] % NPJ
                cnt[0] += 1
                for c in range(8):
                    e_mm = cx.op("tensor", mm(pj[pb][:], W[:, c, oc * 128:(oc + 1) * 128], hn[b][:, c, :], c == 0, c == 7),
                                 waits=[e_hn, e_w, pj_free[pb] if c == 0 else None], sig=(c == 7))
                dst = qo[b][:, oc, :] if isq else ko[b][:, oc - nqc, :]
                if not isA:
                    e_o = cx.op("scalar", act(dst, pj[pb][:], AF.Copy, scale=0.125 if isq else 1.0),
                                waits=[e_mm, out_free[b]], sig=True)
                    pj_free[pb] = e_o
                    evs_out.append(e_o)
                    continue
                k = oc % 2
                gcol = (GCOL_QG if isq else GCOL_KG) + j
                e_qg = cx.op("scalar", act(tmp["qg"][k][:], pj[pb][:], AF.Copy, scale=G[:, gcol:gcol + 1]),
                             waits=[e_mm, tfree["qg"][k]], sig=True)
                e_sq = cx.op("scalar", act(tmp["sq"][k][:], pj[pb][:], AF.Square),
                             waits=[e_mm, tfree["sq"][k]], sig=True)
                pj_free[pb] = e_sq
                e_t1 = cx.op("gpsimd", tt(tmp["t1"][k][:], tmp["qg"][k][:], cst[b][:, 0, :], ALU.mult),
                             waits=[e_qg, ld_ev[i], tfree["t1"][k]], sig=True)

                def fp32_part(oc=oc, k=k, e_qg=e_qg, e_sq=e_sq, e_t1=e_t1):
                    nonlocal ssc_free
                    e_ss = cx.op("tensor", mm(ssc[0:32, :], BDC[:, oc, :], tmp["sq"][k][:], oc == 0, oc == noc - 1),
                                 waits=[e_sq, e_w, ssc_free if oc == 0 else None], sig=True)
                    tfree["sq"][k] = e_ss
                    e_rot = cx.op("tensor", mm(aux[k][:], Rm, tmp["qg"][k][:], True, True),
                                  waits=[e_qg, aux_free[k]], sig=True)
                    tfree["qg"][k] = (e_rot, e_t1)
                    e_t2_ = cx.op("vector", tt(tmp["t2"][k][:], aux[k][:], cst[b][:, 1, :], ALU.mult),
                                  waits=[e_rot, ld_ev[i], tfree["t2"][k]], sig=True)
                    aux_free[k] = e_t2_
                    e_t3 = cx.op("gpsimd", tt(T3[:, oc, :], tmp["t1"][k][:], tmp["t2"][k][:], ALU.add),
                                 waits=[e_t1, e_t2_, t3_free[oc]], sig=True)
                    tfree["t1"][k] = e_t3
                    tfree["t2"][k] = e_t3
                    return e_ss, e_t3

                deferred.append(fp32_part)
                if len(deferred) > 1:
                    e_ss_last, e_t3_last = deferred.pop(0)()
            if isA:
                while deferred:
                    e_ss_last, e_t3_last = deferred.pop(0)()
            if i + 1 < NT:
                e_hn_next = norm(i + 1, ld_ev[i + 1])
            for tb in range(4):
                pb = cnt[0] % NPJ
                cnt[0] += 1
                for c in range(8):
                    e_mm = cx.op("tensor", mm(pj[pb][:, 0:NV], hn[b][:, c, tb * 128:(tb + 1) * 128], W[:, c, voff:voff + NV],
                                              c == 0, c == 7),
                                 waits=[e_hn, e_w, pj_free[pb] if c == 0 else None], sig=(c == 7))
                e_v = cx.op("vector", cpy(vo[b][:, tb, :, 0:64], pj[pb][:, 0:NV].rearrange("p (k d) -> p k d", d=64)),
                            waits=[e_mm, out_free[b], e_ms], sig=True)
                pj_free[pb] = e_v
                evs_out.append(e_v)
            hn_free[b] = e_mm
            if isA:
                e_sd = cx.op("scalar", act(sdc[:], ssc[0:32, :], AF.Sqrt, scale=G[0:32, GCOL_SCL:GCOL_SCL + 1],
                                           bias=G[0:32, GCOL_BIA:GCOL_BIA + 1]),
                             waits=[e_ss_last, sdc_free], sig=True)
                ssc_free = e_sd
                e_rs = cx.op("vector", recip(rsc[:], sdc[:]), waits=[e_sd, rsc_free], sig=True)
                sdc_free = e_rs
                e_bc = None
                for oc in range(noc):
                    k = oc % 2
                    isq = oc < nqc
                    dst = qo[b][:, oc, :] if isq else ko[b][:, oc - nqc, :]
                    e_bc = cx.op("tensor", mm(aux[k][:], SelC[:, oc, :], rsc[:], True, True),
                                 waits=[e_rs, e_w, aux_free[k]], sig=True)
                    e_o = cx.op("vector", tt(dst, aux[k][:], T3[:, oc, :], ALU.mult),
                                waits=[e_bc, e_t3_last, out_free[b]], sig=True)
                    aux_free[k] = e_o
                    t3_free[oc] = e_o
                    evs_out.append(e_o)
                rsc_free = e_bc
                h_free[b] = (e_hn, e_t3_last)
            cx.op("sync", dma(QTv[:, 0:nqc, t0:t0 + T], qo[b][:]), waits=evs_out, dsem=sts[b])
            cx.op("sync", dma(KTv[:, 0:nkc, t0:t0 + T], ko[b][:]), dsem=sts[b])
            out_free[b] = cx.op("sync", dma(Vdv[:, 4 * i:4 * i + 4, :], vo[b][:].rearrange("p a k d -> p a (k d)")),
                                dsem=sts[b])
            if i + 1 < NT:
                e_hn = e_hn_next
        for s_ in sts:
            cx.op("sync", lambda e, s_=s_: e.wait_ge(s_.h, s_.n))
        cx.flush()


def stage_oproj(cx, l, kind, h_in, h_out, OTf, wod, Seld, M1d):
    nc = cx.nc
    T = 512
    NT = S // T
    nh, nkv = CFG[kind]
    nch = nh // 2
    nr = 6 if kind == 1 else nh
    hin_v = h_in.rearrange("(c p) t -> p c t", p=128)
    hout_v = h_out.rearrange("(c p) t -> p c t", p=128)
    OTv = OTf.rearrange("(c two) r t -> two r c t", two=2)
    with ExitStack() as es:
        sb = lambda name, shape, dt: es.enter_context(nc.sbuf_tensor("O%d_" % l + name, shape, dt))
        ps = lambda name: es.enter_context(nc.psum_tensor("O%d_" % l + name, [128, 512], F32))
        W = sb("w", [128, nch, D], BF16)
        Sel = sb("sel", [nr, nch * 128], F32)
        ht = [sb("h%d" % i, [128, 8, T], F32) for i in range(2)]
        Ut = [sb("u%d" % i, [128, nch, T], F32) for i in range(2)]
        Dt = [sb("d%d" % i, [nh, T], F32) for i in range(2)]
        Rt = [sb("r%d" % i, [nr, T], F32) for i in range(2)]
        on = sb("on", [128, nch, T], BF16)
        bc = [ps("bc%d" % i) for i in range(2)]
        opp = [ps("op%d" % i) for i in range(3)]
        lds = [cx.new_sem("o%d_ld%d" % (l, i)) for i in range(2)]
        sts = [cx.new_sem("o%d_st%d" % (l, i)) for i in range(2)]
        ws = cx.new_sem("o%d_w" % l)
        load_weights_cast(cx, W, wod, ws, [(0, nch, 0, D)])
        e_w = cx.op("sync", dma(Sel[:], Seld), dsem=ws)
        if kind == 1:
            M1 = sb("m1", [nh, nr], F32)
            dsp = ps("dsp")
            e_w = cx.op("sync", dma(M1[:], M1d), dsem=ws)
        h_free = [None, None]
        u_free = [None, None]
        d_free = [None, None]
        r_free = [None, None]
        bc_free = [None, None]
        dsp_free = None
        op_free = [None] * 3
        on_free = None
        cnt = 0

        def load(i):
            b = i % 2
            t0 = i * T
            cx.op("sync", dma(ht[b][:], hin_v[:, :, t0:t0 + T]), waits=[h_free[b]], dsem=lds[b])
            cx.op("sync", dma(Ut[b][0:64, :, :], OTv[0, 0:64, 0:nch, t0:t0 + T]), waits=[u_free[b]], dsem=lds[b])
            cx.op("sync", dma(Ut[b][64:128, :, :], OTv[1, 0:64, 0:nch, t0:t0 + T]), dsem=lds[b])
            return cx.op("sync", dma(Dt[b][:], OTf[0:nh, 64, t0:t0 + T]), waits=[d_free[b]], dsem=lds[b])

        ld_ev = {0: load(0)}
        for i in range(NT):
            b = i % 2
            t0 = i * T
            if i + 1 < NT:
                ld_ev[i + 1] = load(i + 1)
            if kind == 1:
                e_ds = cx.op("tensor", mm(dsp[0:nr, :], M1[:], Dt[b][:], True, True),
                             waits=[ld_ev[i], e_w, dsp_free], sig=True)
                e_r = cx.op("vector", recip(Rt[b][:], dsp[0:nr, :]), waits=[e_ds, r_free[b]], sig=True)
                dsp_free = e_r
                d_free[b] = e_ds
            else:
                e_r = cx.op("vector", recip(Rt[b][:], Dt[b][:]), waits=[ld_ev[i], r_free[b]], sig=True)
                d_free[b] = e_r
            e_on = None
            e_bc = None
            for c in range(nch):
                k = c % 2
                e_bc = cx.op("tensor", mm(bc[k][:], Sel[:, c * 128:(c + 1) * 128], Rt[b][:], True, True),
                             waits=[e_r, e_w, bc_free[k]], sig=True)
                e_on = cx.op("vector", tt(on[:, c, :], bc[k][:], Ut[b][:, c, :], ALU.mult),
                             waits=[e_bc, ld_ev[i], on_free], sig=True)
                bc_free[k] = e_on
            u_free[b] = e_on
            r_free[b] = e_bc
            e_add = None
            e_mm = None
            for n in range(8):
                pb = cnt % 3
                cnt += 1
                for c in range(nch):
                    e_mm = cx.op("tensor", mm(opp[pb][:], W[:, c, n * 128:(n + 1) * 128], on[:, c, :], c == 0, c == nch - 1),
                                 waits=[e_on, e_w, op_free[pb] if c == 0 else None], sig=(c == nch - 1))
                e_add = cx.op("vector", tt(ht[b][:, n, :], opp[pb][:], ht[b][:, n, :], ALU.add), waits=[e_mm, ld_ev[i]], sig=True)
                op_free[pb] = e_add
            on_free = e_mm
            h_free[b] = cx.op("sync", dma(hout_v[:, :, t0:t0 + T], ht[b][:]), waits=[e_add], dsem=sts[b])
        for s_ in sts:
            cx.op("sync", lambda e, s_=s_: e.wait_ge(s_.h, s_.n))
        cx.flush()


def stage_attn(cx, l, kind, QT, KT, Vd, OTf, tabd, sinkd):
    nc = cx.nc
    nh, nkv = CFG[kind]
    has_tab = kind != 0
    NUD = 2 if kind == 1 else 4
    with ExitStack() as es:
        sb = lambda name, shape, dt: es.enter_context(nc.sbuf_tensor("A%d_" % l + name, shape, dt))
        NSB = 3 if has_tab else 2
        NOB = 2 if has_tab else 4
        NPB = 4 if has_tab else 3
        LA = NSB - 1
        Sps = [es.enter_context(nc.psum_tensor("A%d_S%d" % (l, i), [128, 1024], F32)) for i in range(NSB)]
        ops = [es.enter_context(nc.psum_tensor("A%d_o%d" % (l, i), [128, 512], F32)) for i in range(NOB)]
        P = [sb("P%d" % i, [128, 1024], BF16) for i in range(NPB)]
        UD = [sb("UD%d" % i, [65, S], F32) for i in range(NUD)]
        kls = cx.new_sem("a%d_k" % l)
        qls = [cx.new_sem("a%d_q%d" % (l, i)) for i in range(2)]
        sts = [cx.new_sem("a%d_st%d" % (l, i)) for i in range(NUD)]
        units = []
        e_tab = None
        if has_tab:
            TW = 384 if kind == 2 else 256
            SP = [sb("SP%d" % i, [128, 1024], F32) for i in range(NPB)]
            tab = sb("tab", [128, nh, TW], F32)
            Z = sb("Z", [128, 512], BF16)
            e_z = cx.op("gpsimd", mset(Z[:], 0.0), sig=True)
            e_tab = cx.op("sync", dma(tab[:], tabd), dsem=kls)
        e_sk = None
        if kind == 2:
            esk = sb("esk", [65, 16], F32)
            cx.op("sync", dma(esk[64:65, :], sinkd), dsem=kls)

        if kind in (0, 2):
            Kd = sb("K", [128, nkv, S], BF16)
            Vs = sb("V", [128, 32, nkv, 65], BF16)
            Qc = [sb("Qc%d" % i, [128, S], BF16) for i in range(2)]
            KTv = KT.rearrange("(c p) t -> p c t", p=128)
            for g in range(nkv):
                cx.op("sync", dma(Kd[:, g, :], KTv[:, g, :]), dsem=kls)
            Vdv = Vd.rearrange("(b p) f -> p b f", p=128)
            e_kv = None
            for q4 in range(4):
                e_kv = cx.op("sync", dma(Vs[:, 8 * q4:8 * q4 + 8, :, :].rearrange("p b k d -> p b (k d)"),
                                         Vdv[:, 8 * q4:8 * q4 + 8, :]), dsem=kls)
            if kind == 2:
                e_sk = cx.op("scalar", act(esk[64:65, :], esk[64:65, :], AF.Exp), waits=[e_kv], sig=True)
            q_free = [None, None]
            for c in range(nh // 2):
                g = c // 2
                for qt in range(8):
                    qs = qt * 512
                    sl = []
                    if kind == 0:
                        kbs = [(kb, qs, qs + 512, 0) for kb in range(32)]
                    else:
                        kbs = []
                        for kb in range(max(0, 4 * qt - 1), min(32, 4 * qt + 5)):
                            lo, hi = max(qs, 128 * kb - 128), min(qs + 512, 128 * kb + 256)
                            kbs.append((kb, lo, hi, lo - (128 * kb - 128)))
                    for (kb, lo, hi, off) in kbs:
                        subs = []
                        for hh in range(2):
                            subs.append(dict(k=Kd[hh * 64:(hh + 1) * 64, g, kb * 128:(kb + 1) * 128],
                                             q=(c, hh, lo, hi), n=hi - lo, v=Vs[:, kb, g, :], c0=lo - qs,
                                             tab=tab[:, 2 * c + hh, off:off + hi - lo] if has_tab else None))
                        sl.append(dict(subs=subs, tab2=tab[:, 2 * c:2 * c + 2, off:off + hi - lo] if has_tab else None))
                    units.append(dict(hs=(2 * c, 2 * c + 1), ql=512, steps=sl, dst=("plain", qs), qchunk=c,
                                      hfirst=(qt == 0), hlast=(qt == 7)))
        else:
            Qg = sb("Qg", [128, 3, S], BF16)
            Kg = sb("Kg", [128, 2, S], BF16)
            Qp = sb("Qp", [128, 3, S], BF16)
            Kp = sb("Kp", [128, 2, S], BF16)
            Vs = sb("V", [128, 32, 2, 65], BF16)
            for g, (window, d) in enumerate(B_GROUPS):
                L = S // d
                ql = min(512, L)
                nb = L // 128
                src = Kp if d > 1 else Kg
                for ci in range(3):
                    ulist = []
                    for rho in range(d):
                        for qt in range(L // ql):
                            qs = qt * ql
                            sl = []
                            for kb in range(max(0, (qs - 64) // 128), min(nb, (qs + ql + 64 + 127) // 128)):
                                lo, hi = max(qs, 128 * kb - 64), min(qs + ql, 128 * kb + 192)
                                if hi <= lo:
                                    continue
                                off = lo - (128 * kb - 64)
                                subs = []
                                for hh in range(2):
                                    i6 = 2 * ci + hh
                                    kvl = i6 // 3
                                    subs.append(dict(
                                        k=src[hh * 64:(hh + 1) * 64, kvl, rho * L + kb * 128:rho * L + (kb + 1) * 128],
                                        q=(ci, hh, rho * L + lo, rho * L + hi), n=hi - lo,
                                        v=Vs[:, rho * nb + kb, kvl, :], c0=lo - qs,
                                        tab=tab[:, 6 * g + i6, off:off + hi - lo]))
                                h0 = 6 * g + 2 * ci
                                sl.append(dict(subs=subs, tab2=tab[:, h0:h0 + 2, off:off + hi - lo]))
                            ulist.append(dict(hs=(6 * g + 2 * ci, 6 * g + 2 * ci + 1), ql=ql, steps=sl,
                                              dst=("perm", d, rho, qs), group=g))
                    ulist[0]["hfirst"] = True
                    ulist[-1]["hlast"] = True
                    units.extend(ulist)

        flat = []
        for ui, u in enumerate(units):
            for si, st in enumerate(u["steps"]):
                flat.append((ui, si == 0, si == len(u["steps"]) - 1, st))
        NS = len(flat)
        e_qk, e_rd, e_exp, e_pv = {}, {}, {}, {}
        e_evac = {}
        ud_free = [None] * NUD
        state = dict(group=-1, qchunk=-1, ready=None, pcount=-1, last_pe=None, qbuf_last={})

        def prepare(ui):
            u = units[ui]
            if kind in (0, 2):
                c = u["qchunk"]
                if c != state["qchunk"]:
                    state["qchunk"] = c
                    ql_ = state["qbuf_last"]
                    for cc in (c, c + 1):
                        if cc not in ql_ and cc < nh // 2:
                            ql_[cc] = cx.op("sync", dma(Qc[cc % 2][:], QT[cc * 128:(cc + 1) * 128, :]),
                                            waits=[q_free[cc % 2]], dsem=qls[cc % 2])
                    state["ready"] = [ql_[c], e_kv, e_tab]
            else:
                g = u["group"]
                if g != state["group"]:
                    state["group"] = g
                    d = B_GROUPS[g][1]
                    L = S // d
                    nb = L // 128
                    wl = [state["last_pe"]]
                    for ci in range(3):
                        cx.op("sync", dma(Qg[:, ci, :], QT[(3 * g + ci) * 128:(3 * g + ci + 1) * 128, :]), waits=wl, dsem=qls[0])
                    for kvl in range(2):
                        cx.op("sync", dma(Kg[:, kvl, :], KT[(2 * g + kvl) * 128:(2 * g + kvl + 1) * 128, :]), dsem=qls[0])
                    Vv = Vd.rearrange("(b p r) f -> r p b f", p=128, r=d)
                    e_l = None
                    for rho in range(d):
                        e_l = cx.op("sync", dma(Vs[:, rho * nb:(rho + 1) * nb, :, :].rearrange("p b k d -> p b (k d)"),
                                                Vv[rho, :, :, 2 * g * 65:(2 * g + 2) * 65]), dsem=qls[0])
                    rdy = [e_l, e_tab]
                    if d > 1:
                        e_p = None
                        for ci in range(3):
                            e_p = cx.op("gpsimd", cpy(Qp[:, ci, :].rearrange("p (r m) -> p r m", r=d),
                                                      Qg[:, ci, :].rearrange("p (m r) -> p r m", r=d)),
                                        waits=[e_l, state["last_pe"]], sig=True)
                        for kvl in range(2):
                            e_p = cx.op("gpsimd", cpy(Kp[:, kvl, :].rearrange("p (r m) -> p r m", r=d),
                                                      Kg[:, kvl, :].rearrange("p (m r) -> p r m", r=d)),
                                        waits=[e_l], sig=True)
                        rdy.append(e_p)
                    state["ready"] = rdy

        def q_ap(q):
            ci, hh, a, b_ = q
            if kind in (0, 2):
                return Qc[ci % 2][hh * 64:(hh + 1) * 64, a:b_]
            src_ = Qp if B_GROUPS[state["group"]][1] > 1 else Qg
            return src_[hh * 64:(hh + 1) * 64, ci, a:b_]

        def emit_qk(s):
            ui, first, last, st = flat[s]
            if first:
                prepare(ui)
            sbi = s % NSB
            ev = None
            for k, sub in enumerate(st["subs"]):
                n = sub["n"]
                ev = cx.op("tensor", mm(Sps[sbi][:, k * 512:k * 512 + n], sub["k"], q_ap(sub["q"]), True, True),
                           waits=state["ready"] + [e_rd.get(s - NSB)], sig=(k == 1))
            e_qk[s] = ev
            state["last_pe"] = ev
            if kind in (0, 2):
                q_free[units[ui]["qchunk"] % 2] = ev

        def emit_sm(s):
            ui, first, last, st = flat[s]
            sbi, pi = s % NSB, s % NPB
            n = st["subs"][0]["n"]
            if has_tab:
                e_a = cx.op("vector", tt(SP[pi][:].rearrange("p (k n) -> p k n", k=2)[:, :, 0:n],
                                         Sps[sbi][:].rearrange("p (k n) -> p k n", k=2)[:, :, 0:n],
                                         st["tab2"], ALU.add),
                            waits=[e_qk[s], e_exp.get(s - NPB), e_tab], sig=True)
                e_rd[s] = e_a
                e_exp[s] = cx.op("scalar", act(P[pi][:].rearrange("p (k n) -> p k n", k=2)[:, :, 0:n],
                                               SP[pi][:].rearrange("p (k n) -> p k n", k=2)[:, :, 0:n], AF.Exp),
                                 waits=[e_a, e_pv.get(s - NPB)], sig=True)
            else:
                e_exp[s] = cx.op("scalar", act(P[pi][:], Sps[sbi][:], AF.Exp),
                                 waits=[e_qk[s], e_pv.get(s - NPB)], sig=True)
                e_rd[s] = e_exp[s]

        def emit_pv(s):
            ui, first, last, st = flat[s]
            pi = s % NPB
            u = units[ui]
            ev = None
            UB = NOB // 2
            for k, sub in enumerate(st["subs"]):
                ob = 2 * (ui % UB) + k
                n, c0 = sub["n"], sub["c0"]
                if first and has_tab:
                    cx.op("tensor", mm(ops[ob][0:65, 0:u["ql"]], Z[:, 0:65], Z[:, 0:u["ql"]], True, False),
                          waits=[e_z, e_evac.get(ui - UB)])
                ev = cx.op("tensor", mm(ops[ob][0:65, c0:c0 + n], sub["v"], P[pi][:, k * 512:k * 512 + n],
                                        first and not has_tab, last),
                           waits=[e_exp[s], e_evac.get(ui - UB) if first else None], sig=(k == 1))
            e_pv[s] = ev
            state["last_pe"] = ev
            if last:
                if u.get("hfirst"):
                    state["pcount"] += 1
                ql = u["ql"]
                e_ev = None
                for k in range(2):
                    ob = 2 * (ui % UB) + k
                    hb = (2 * state["pcount"] + k) % NUD
                    if u["dst"][0] == "plain":
                        dst = UD[hb][0:65, u["dst"][1]:u["dst"][1] + ql]
                    else:
                        _, d, rho, qs = u["dst"]
                        dst = UD[hb][0:65, :].rearrange("p (m r) -> p r m", r=d)[:, rho, qs:qs + ql]
                    e_ev = cx.op("vector", cpy(dst, ops[ob][0:65, 0:ql]),
                                 waits=[ev, ud_free[hb] if u.get("hfirst") else None], sig=True)
                e_evac[ui] = e_ev
                if u.get("hlast"):
                    for k in range(2):
                        hb = (2 * state["pcount"] + k) % NUD
                        h = u["hs"][k]
                        if kind == 2:
                            e_ev = cx.op("vector", ts(UD[hb][64:65, :], UD[hb][64:65, :], esk[64:65, h:h + 1], ALU.add),
                                         waits=[e_sk], sig=True)
                        ud_free[hb] = cx.op("sync", dma(OTf[h], UD[hb][:]), waits=[e_ev], dsem=sts[hb])

        def new_group(s):
            return kind == 1 and units[flat[s][0]]["group"] != units[flat[s - 1][0]]["group"]

        def grp(s):
            return units[flat[s][0]].get("group", 0)

        nq = 0
        for s in range(NS):
            while nq < NS and nq <= s + LA and (nq <= s or grp(nq) == grp(s)):
                emit_qk(nq)
                nq += 1
            emit_sm(s)
            emit_pv(s)
        for s_ in sts:
            cx.op("sync", lambda e, s_=s_: e.wait_ge(s_.h, s_.n))
        cx.flush()


def layer_dims(kind):
    nh, nkv = CFG[kind]
    return nh, nkv, nh * 64 + nkv * 128 + nkv * 64


def build(nlayers=DEPTH, debug=False):
    nc = bass.Bass("TRN2", target_bir_lowering=False)
    ein = lambda name, shape, dt=F32: nc.dram_tensor(name, shape, dt, kind="ExternalInput").ap()
    xT = ein("xT", [D, S])
    yT = nc.dram_tensor("yT", [D, S], F32, kind="ExternalOutput").ap()
    Gd = ein("G", [128, NG])
    onesd = ein("ones", [128, 128], BF16)
    BDCd = ein("BDC", [128, 12, 32], BF16)
    SelCd = ein("SelC", [32, 12, 128])
    Rd = ein("Rm", [128, 128])
    cosd = ein("cosT", [128, S])
    sind = ein("sinT", [128, S])
    tabBd = ein("tabB", [128, 18, 256])
    tabCd = ein("tabC", [128, 16, 384])
    selAd = ein("selA", [16, 8 * 128])
    selBd = ein("selB", [6, 9 * 128])
    m1Bd = ein("m1B", [18, 6])
    sinkd = ein("sinks", [1, 16])
    wq, wo, w1, w2 = [], [], [], []
    for l in range(nlayers):
        nh, nkv, NW = layer_dims(KINDS[l])
        wq.append(ein("wqkv%d" % l, [128, 8, NW]))
        wo.append(ein("wo%d" % l, [128, nh // 2, D]))
        w1.append(ein("w1_%d" % l, [128, 8, DFF]))
        w2.append(ein("w2_%d" % l, [128, 32, D]))
    sk = "ExternalOutput" if debug else "Internal"
    hA = nc.dram_tensor("hA", [D, S], F32, kind=sk).ap()
    hB = nc.dram_tensor("hB", [D, S], F32, kind=sk).ap()
    QT = nc.dram_tensor("QT", [1152, S], BF16, kind=sk).ap()
    KT = nc.dram_tensor("KT", [768, S], BF16, kind=sk).ap()
    VdA = nc.dram_tensor("VdA", [S, 4 * 65], BF16, kind=sk).ap()
    VdB = nc.dram_tensor("VdB", [S, 6 * 65], BF16, kind=sk).ap()
    OTf = nc.dram_tensor("OTf", [18, 65, S], F32, kind=sk).ap()
    with ExitStack() as es:
        cx = Ctx(nc, es)
        G = es.enter_context(nc.sbuf_tensor("Gsb", [128, NG], F32))
        ones = es.enter_context(nc.sbuf_tensor("onessb", [128, 128], BF16))
        Rm = es.enter_context(nc.sbuf_tensor("Rsb", [128, 128], F32))
        cs = cx.new_sem("const")
        cx.op("sync", dma(G[:], Gd), dsem=cs)
        cx.op("sync", dma(ones[:], onesd), dsem=cs)
        cx.op("sync", dma(Rm[:], Rd), dsem=cs)
        for e in ["scalar", "vector", "tensor", "gpsimd"]:
            cx.op(e, lambda en: en.wait_ge(cs.h, cs.n))
        cx.flush()
        h_cur = xT
        used = [0, 0, 0]
        for l in range(nlayers):
            kind = KINDS[l]
            j = used[kind]
            used[kind] += 1
            Vd = VdB if kind == 1 else VdA
            stage_qkv(cx, l, kind, j, h_cur, wq[l], QT, KT, Vd, G[:], ones[:], BDCd, SelCd, Rm[:], cosd, sind)
            if debug == 3 and l == nlayers - 1:
                break
            stage_attn(cx, l, kind, QT, KT, Vd, OTf, tabBd if kind == 1 else tabCd, sinkd)
            if debug == 1 and l == nlayers - 1:
                break
            stage_oproj(cx, l, kind, h_cur, hA, OTf, wo[l], selBd if kind == 1 else selAd, m1Bd)
            last = l == nlayers - 1
            if debug and last:
                break
            stage_mlp(cx, l, hA, yT if last else hB, w1[l], w2[l], G[:], ones[:], last)
            h_cur = hB
    return nc


def alibi_slopes_np(n):
    return (2.0 ** (-8.0 * np.arange(1, n + 1, dtype=np.float32) / n)).astype(np.float32)


def host_consts():
    c = {}
    c["ones"] = np.ones((128, 128), ml_dtypes.bfloat16)
    bdc = np.zeros((128, 12, 32), np.float32)
    selc = np.zeros((32, 12, 128), np.float32)
    for oc in range(12):
        for p in range(128):
            bdc[p, oc, 2 * oc + p // 64] = 1
            selc[2 * oc + p // 64, oc, p] = 1
    c["BDC"] = bdc.astype(ml_dtypes.bfloat16)
    c["SelC"] = selc
    R = np.zeros((128, 128), np.float32)
    for m in range(128):
        jj = (m % 64) % 32
        if jj < 16:
            R[m + 16, m] = -1.0
        else:
            R[m - 16, m] = 1.0
    c["Rm"] = R
    t = np.arange(S)
    row = (t // GRID_W).astype(np.float32)
    col = (t % GRID_W).astype(np.float32)
    inv = (np.float32(ROPE_THETA) ** (-np.arange(0, 32, 2, dtype=np.float32) / np.float32(32))).astype(np.float32)
    cosT = np.zeros((128, S), np.float32)
    sinT = np.zeros((128, S), np.float32)
    for p in range(128):
        dd = p % 64
        pos = row if dd < 32 else col
        ang = (pos * inv[(dd % 32) % 16]).astype(np.float32)
        cosT[p] = np.cos(ang.astype(np.float64)).astype(np.float32)
        sinT[p] = np.sin(ang.astype(np.float64)).astype(np.float32)
    c["cosT"], c["sinT"] = cosT, sinT
    k = np.arange(128)[:, None]
    slB = alibi_slopes_np(18)
    tabB = np.zeros((128, 18, 256), np.float32)
    jB = np.arange(256)[None, :]
    relB = np.abs(k + 64 - jB)
    for h in range(18):
        d = B_GROUPS[h // 6][1]
        tabB[:, h, :] = np.where(relB <= 64, -slB[h] * (relB * d).astype(np.float32), NEG)
    c["tabB"] = tabB
    slC = alibi_slopes_np(16)
    tabC = np.zeros((128, 16, 384), np.float32)
    jC = np.arange(384)[None, :]
    relC = np.abs(k + 128 - jC)
    for h in range(16):
        tabC[:, h, :] = np.where(relC <= 128, -slC[h] * relC.astype(np.float32), NEG)
    c["tabC"] = tabC
    selA = np.zeros((16, 8 * 128), np.float32)
    for col_ in range(8 * 128):
        selA[2 * (col_ // 128) + (col_ % 128) // 64, col_] = 1
    c["selA"] = selA
    selB = np.zeros((6, 9 * 128), np.float32)
    for col_ in range(9 * 128):
        hd = 2 * (col_ // 128) + (col_ % 128) // 64
        selB[hd % 6, col_] = 1
    c["selB"] = selB
    m1 = np.zeros((18, 6), np.float32)
    for h in range(18):
        m1[h, h % 6] = 1
    c["m1B"] = m1
    return c


def arr_k(w, nchunk):
    return np.ascontiguousarray(w.reshape(nchunk, 128, w.shape[1]).transpose(1, 0, 2))


def host_prep(inp, nlayers=DEPTH):
    shared = host_consts()
    G = np.zeros((128, NG), np.float32)
    for l in range(DEPTH):
        G[:, gcol_attn(l):gcol_attn(l) + 8] = inp["attn_norm"][l].reshape(8, 128).T
        G[:, gcol_mlp(l):gcol_mlp(l) + 8] = inp["mlp_norm"][l].reshape(8, 128).T
    G[:, GCOL_FINAL:GCOL_FINAL + 8] = inp["final_norm"].reshape(8, 128).T
    for j in range(2):
        G[:, GCOL_QG + j] = np.tile(inp["a_q_gain"][j], 2)
        G[:, GCOL_KG + j] = np.tile(inp["a_k_gain"][j], 2)
    G[:, GCOL_EPS] = EPS
    G[:, GCOL_EPS64] = 64 * EPS
    G[:, GCOL_SCL] = 1.0 / 64
    G[:, GCOL_BIA] = EPS
    G[0:16, GCOL_SCL] = 1.0
    G[0:16, GCOL_BIA] = 64 * EPS
    shared["G"] = G
    shared["sinks"] = np.ascontiguousarray(inp["c_sinks"][0:1]).astype(np.float32)
    used = [0, 0, 0]
    for l in range(nlayers):
        kind = KINDS[l]
        j = used[kind]
        used[kind] += 1
        nh, nkv = CFG[kind]
        w = [inp["a_w_qkv"], inp["b_w_qkv"], inp["c_w_qkv"]][kind][j]
        wo = [inp["a_w_o"], inp["b_w_o"], inp["c_w_o"]][kind][j]
        nq = nh * 64
        q, k, v = w[:, :nq], w[:, nq:nq + nkv * 64], w[:, nq + nkv * 64:]
        kd = np.concatenate([np.concatenate([k[:, g * 64:(g + 1) * 64]] * 2, axis=1) for g in range(nkv)], axis=1)
        shared["wqkv%d" % l] = arr_k(np.concatenate([q, kd, v], axis=1), 8)
        shared["wo%d" % l] = arr_k(wo, nh // 2)
        shared["w1_%d" % l] = arr_k(inp["mlp_w1"][l], 8)
        shared["w2_%d" % l] = arr_k(inp["mlp_w2"][l], 32)
    return shared


_NC_CACHE = {}


def kernel(**inputs):
    inp = {k: np.asarray(v) for k, v in inputs.items()}
    shared = host_prep(inp)
    x = inp["x"].astype(np.float32)
    if DEPTH not in _NC_CACHE:
        _NC_CACHE[DEPTH] = build(DEPTH)
    nc = _NC_CACHE[DEPTH]
    in_maps = []
    for b in range(NCORES):
        m = dict(shared)
        m["xT"] = np.ascontiguousarray(x[b].T)
        in_maps.append(m)
    res = run_bass_kernel_spmd(nc, in_maps, core_ids=list(range(NCORES)))
    out = np.empty((NCORES, S, D), np.float32)
    for b in range(NCORES):
        out[b] = res.results[b]["yT"].T
    return out
```
